# Optimizing a Trainium2 kernel written in Bass

```python
import math
import jax, jax.numpy as jnp
from jax import lax
import numpy as np

D_MODEL = 1024
BATCH = 2
SEQ = 16384
DEPTH = 2

N_MIXERS = 2
N_HYENA_LAYERS = (DEPTH + N_MIXERS - 1) // N_MIXERS
N_NA_LAYERS = DEPTH // N_MIXERS
GRID_W = 64
RMS_EPS = 1e-6
HY_SHORT_CONV = 3
HY_EMB_DIM = 33
HY_N_BANDS = (HY_EMB_DIM - 1) // 2
HY_FILTER_WIDTH = 64
HY_FAST_DECAY_PCT = 0.3
HY_SLOW_DECAY_PCT = 1.5
HY_DECAY_TARGET = 1e-2
NA_HEADS = 16
NA_HEAD_DIM = D_MODEL // NA_HEADS
NA_KH = 8
NA_KW = 16
FFN_HIDDEN = -(-8 * D_MODEL // (3 * 256)) * 256

kernel_name = "hybrid_hyena_natten_encoder"


def rmsnorm(x, g):
    x32 = x.astype(jnp.float32)
    y = x32 * lax.rsqrt(jnp.mean(x32 * x32, axis=-1, keepdims=True) + RMS_EPS)
    return (y * g.astype(jnp.float32)).astype(x.dtype)


def short_conv(z, w, b):
    c = z.shape[-1]
    pad = HY_SHORT_CONV // 2
    y = lax.conv_general_dilated(
        z, w[:, None, :].astype(z.dtype), window_strides=(1,), padding=((pad, pad),),
        dimension_numbers=("NWC", "WIO", "NWC"), feature_group_count=c)
    return y + b


def hyena_filter(L, w1, b1, w2, b2, w3, b3, freq, w_out, decay):
    f32 = jnp.float32
    t = jnp.linspace(0.0, 1.0, L, dtype=f32)[:, None]
    w = 2.0 * math.pi * jnp.arange(L, dtype=f32)[:, None] / L
    bands = jnp.linspace(1e-4, HY_N_BANDS - 1, HY_N_BANDS, dtype=f32)[None, :]
    feat = jnp.concatenate([t, jnp.cos(bands * w), -jnp.sin(bands * w)], axis=-1)
    fr = freq.astype(f32)
    act = lambda a: jnp.sin(fr * a)
    h = act(feat @ w1.astype(f32) + b1.astype(f32))
    h = act(h @ w2.astype(f32) + b2.astype(f32))
    h = act(h @ w3.astype(f32) + b3.astype(f32))
    h = (h @ w_out.astype(f32)).reshape(L, 2, D_MODEL)
    h = h * jnp.exp(-t[:, :, None] * jnp.abs(decay.astype(f32)))
    g = jnp.concatenate([h[:, 0], jnp.zeros((1, D_MODEL), f32), jnp.flip(h[1:, 1], axis=0)], axis=0)
    return g / jnp.sum(jnp.abs(g), axis=0, keepdims=True)


def hyena_mixer(u, w_in, b_in, conv_w, conv_b, f_w1, f_b1, f_w2, f_b2, f_w3, f_b3,
                f_freq, f_wout, decay, skip, w_out, b_out):
    L = u.shape[1]
    z = short_conv(u @ w_in + b_in, conv_w, conv_b)
    x0, x1, v = jnp.split(z, 3, axis=-1)
    s = (v * x1).astype(jnp.float32)
    g = hyena_filter(L, f_w1, f_b1, f_w2, f_b2, f_w3, f_b3, f_freq, f_wout, decay)
    n = 2 * L
    y = jnp.fft.irfft(jnp.fft.rfft(s, n=n, axis=1) * jnp.fft.rfft(g, n=n, axis=0)[None],
                      n=n, axis=1)[:, :L]
    y = (y + s * skip.astype(jnp.float32)).astype(u.dtype) * x0
    return y @ w_out + b_out


def na_mixer(u, w_qkv, b_qkv, rpb, w_o, b_o):
    B_, L, D = u.shape
    rows = L // GRID_W
    kh = min(NA_KH, rows)
    kw = NA_KW
    qkv = u @ w_qkv + b_qkv
    q, k, v = [a.reshape(B_, rows, GRID_W, NA_HEADS, NA_HEAD_DIM) for a in jnp.split(qkv, 3, axis=-1)]
    q = q * (NA_HEAD_DIM ** -0.5)
    col = jnp.arange(GRID_W)
    col_start = jnp.clip(col - kw // 2, 0, GRID_W - kw)
    col_idx = col_start[:, None] + jnp.arange(kw)[None, :]
    col_off = col_idx - col[:, None] + (NA_KW - 1)

    def row_block(r):
        r_start = jnp.clip(r - kh // 2, 0, rows - kh)
        k_rows = lax.dynamic_slice_in_dim(k, r_start, kh, axis=1)
        v_rows = lax.dynamic_slice_in_dim(v, r_start, kh, axis=1)
        k_win = k_rows[:, :, col_idx]
        v_win = v_rows[:, :, col_idx]
        q_r = lax.dynamic_index_in_dim(q, r, axis=1, keepdims=False)
        s = jnp.einsum("bwhd,biwjhd->bhwij", q_r, k_win)
        row_off = r_start + jnp.arange(kh) - r + (NA_KH - 1)
        bias = rpb[:, row_off[None, :, None], col_off[:, None, :]]
        s = (s + bias[None]).astype(jnp.float32).reshape(B_, NA_HEADS, GRID_W, kh * kw)
        p = jax.nn.softmax(s, axis=-1).reshape(B_, NA_HEADS, GRID_W, kh, kw).astype(v.dtype)
        return jnp.einsum("bhwij,biwjhd->bwhd", p, v_win)

    out = lax.map(row_block, jnp.arange(rows))
    out = jnp.transpose(out, (1, 0, 2, 3, 4)).reshape(B_, L, D)
    return out @ w_o + b_o


def swiglu(x, w_gate, w_up, w_down):
    return (jax.nn.silu(x @ w_gate) * (x @ w_up)) @ w_down


def setup_inputs(seed: int = 0) -> dict:
    key = jax.random.key(seed)
    ks = iter(jax.random.split(key, 40))
    nrm = lambda shape, scale: scale * jax.random.normal(next(ks), shape, jnp.float32)
    D, F, NH, NN = D_MODEL, FFN_HIDDEN, N_HYENA_LAYERS, N_NA_LAYERS
    base_decay = jnp.abs(jnp.linspace(math.log(HY_DECAY_TARGET) / HY_SLOW_DECAY_PCT,
                                      math.log(HY_DECAY_TARGET) / HY_FAST_DECAY_PCT, D, dtype=jnp.float32))
    return {
        "x": nrm((BATCH, SEQ, D), 1.0),
        "norm_mix": 1.0 + nrm((DEPTH, D), 0.02),
        "norm_ffn": 1.0 + nrm((DEPTH, D), 0.02),
        "norm_final": 1.0 + nrm((D,), 0.02),
        "hy_w_in": nrm((NH, D, 3 * D), D ** -0.5),
        "hy_b_in": nrm((NH, 3 * D), 0.02),
        "hy_conv_w": nrm((NH, HY_SHORT_CONV, 3 * D), HY_SHORT_CONV ** -0.5),
        "hy_conv_b": nrm((NH, 3 * D), 0.02),
        "hy_f_w1": nrm((NH, HY_EMB_DIM, HY_FILTER_WIDTH), HY_EMB_DIM ** -0.5),
        "hy_f_b1": nrm((NH, HY_FILTER_WIDTH), 0.02),
        "hy_f_w2": nrm((NH, HY_FILTER_WIDTH, HY_FILTER_WIDTH), HY_FILTER_WIDTH ** -0.5),
        "hy_f_b2": nrm((NH, HY_FILTER_WIDTH), 0.02),
        "hy_f_w3": nrm((NH, HY_FILTER_WIDTH, HY_FILTER_WIDTH), HY_FILTER_WIDTH ** -0.5),
        "hy_f_b3": nrm((NH, HY_FILTER_WIDTH), 0.02),
        "hy_f_freq": 1.0 + nrm((NH, HY_FILTER_WIDTH), 0.02),
        "hy_f_wout": nrm((NH, HY_FILTER_WIDTH, 2 * D), HY_FILTER_WIDTH ** -0.5),
        "hy_decay": base_decay * (1.0 + nrm((NH, 2, D), 0.05)),
        "hy_skip": nrm((NH, D), 0.5),
        "hy_w_out": nrm((NH, D, D), D ** -0.5),
        "hy_b_out": nrm((NH, D), 0.02),
        "na_w_qkv": nrm((NN, D, 3 * D), D ** -0.5),
        "na_b_qkv": nrm((NN, 3 * D), 0.02),
        "na_rpb": nrm((NN, NA_HEADS, 2 * NA_KH - 1, 2 * NA_KW - 1), 0.02),
        "na_w_o": nrm((NN, D, D), D ** -0.5),
        "na_b_o": nrm((NN, D), 0.02),
        "ffn_w_gate": nrm((DEPTH, D, F), D ** -0.5),
        "ffn_w_up": nrm((DEPTH, D, F), D ** -0.5),
        "ffn_w_down": nrm((DEPTH, F, D), F ** -0.5),
    }


def reference(x, norm_mix, norm_ffn, norm_final,
              hy_w_in, hy_b_in, hy_conv_w, hy_conv_b, hy_f_w1, hy_f_b1, hy_f_w2, hy_f_b2,
              hy_f_w3, hy_f_b3, hy_f_freq, hy_f_wout, hy_decay, hy_skip, hy_w_out, hy_b_out,
              na_w_qkv, na_b_qkv, na_rpb, na_w_o, na_b_o,
              ffn_w_gate, ffn_w_up, ffn_w_down):
    for i in range(DEPTH):
        h = rmsnorm(x, norm_mix[i])
        j = i // N_MIXERS
        if i % N_MIXERS == 0:
            mixed = hyena_mixer(h, hy_w_in[j], hy_b_in[j], hy_conv_w[j], hy_conv_b[j],
                                hy_f_w1[j], hy_f_b1[j], hy_f_w2[j], hy_f_b2[j], hy_f_w3[j], hy_f_b3[j],
                                hy_f_freq[j], hy_f_wout[j], hy_decay[j], hy_skip[j],
                                hy_w_out[j], hy_b_out[j])
        else:
            mixed = na_mixer(h, na_w_qkv[j], na_b_qkv[j], na_rpb[j], na_w_o[j], na_b_o[j])
        x = x + mixed
        x = x + swiglu(rmsnorm(x, norm_ffn[i]), ffn_w_gate[i], ffn_w_up[i], ffn_w_down[i])
    return rmsnorm(x, norm_final)
```

```python
import math
import numpy as np
import ml_dtypes
import concourse.bass as bass
import concourse.mybir as mybir
from concourse.bass_utils import run_bass_kernel_spmd

F32 = mybir.dt.float32
BF16 = mybir.dt.bfloat16
AF = mybir.ActivationFunctionType
ALU = mybir.AluOpType
AX = mybir.AxisListType

D = 1024
SEQ = 16384
BATCH = 2
NCORES = 8
TOK = 4096
FF = 2816
EPS = 1e-6


class Buf:
    __slots__ = ("name", "last_w", "readers", "sem", "dma_cnt")

    def __init__(self, name):
        self.name = name
        self.last_w = None
        self.readers = []
        self.sem = None
        self.dma_cnt = 0


class Ins:
    __slots__ = ("eng", "fn", "deps", "milestone", "ms", "dma_sem", "dma_val")

    def __init__(self, eng, fn):
        self.eng = eng
        self.fn = fn
        self.deps = []
        self.milestone = False
        self.ms = 0
        self.dma_sem = None
        self.dma_val = 0


class Sched:
    ENGS = ("pe", "act", "dve", "pool", "sp")

    def __init__(self, nc):
        self.nc = nc
        self.q = {e: [] for e in self.ENGS}
        self.esem = {e: nc.alloc_semaphore("prog_" + e) for e in self.ENGS}
        self.nbuf = 0
        self.out_events = []
        self.pfx = ""
        self.pending_barrier = {}
        self.all_dma = []

    def buf(self, name=None):
        self.nbuf += 1
        return Buf(self.pfx + (name or ("b%d" % self.nbuf)))

    def barrier(self):
        deps = [self.q[e][-1] for e in self.ENGS if self.q[e] and self.q[e][-1].fn is not None]
        last = {}
        for d in self.all_dma:
            last[id(d.dma_sem)] = d
        deps += list(last.values())
        self.pending_barrier = {e: list(deps) for e in self.ENGS}

    def _push(self, eng, ins):
        pb = self.pending_barrier.pop(eng, None)
        if pb:
            for d in pb:
                if d is not ins and d not in ins.deps:
                    ins.deps.append(d)
        self.q[eng].append(ins)

    def _collect(self, ins, reads, writes):
        deps = []
        for b in reads:
            if b.last_w is not None:
                deps.append(b.last_w)
        for b in writes:
            if b.last_w is not None:
                deps.append(b.last_w)
            deps.extend(b.readers)
        for d in deps:
            if d is ins:
                continue
            if d.dma_sem is None and d.eng == "pe" and ins.eng == "pe":
                continue
            ins.deps.append(d)
        for b in writes:
            b.last_w = ins
            b.readers = []
        for b in reads:
            if b not in writes:
                b.readers.append(ins)

    def op(self, eng, fn, reads=(), writes=()):
        ins = Ins(eng, fn)
        self._collect(ins, list(reads), list(writes))
        self._push(eng, ins)
        return ins

    def dma(self, eng, out_ap, in_ap, reads=(), writes=(), sem_buf=None, is_output=False, **kw):
        if sem_buf is None:
            sem_buf = (list(writes) + list(reads))[0]
        if sem_buf.sem is None:
            sem_buf.sem = self.nc.alloc_semaphore("dma_" + sem_buf.name)
        ins = Ins(eng, lambda e: e.dma_start(out=out_ap, in_=in_ap, **kw))
        self._collect(ins, list(reads), list(writes))
        sem_buf.dma_cnt += 16
        ins.dma_sem = sem_buf.sem
        ins.dma_val = sem_buf.dma_cnt
        self.all_dma.append(ins)
        self._push(eng, ins)
        if is_output:
            self.out_events.append(ins)
        return ins

    def coll(self, kind, out_ap, in_ap, reads=(), writes=(), sem_buf=None):
        if sem_buf is None:
            sem_buf = (list(writes) + list(reads))[0]
        if sem_buf.sem is None:
            sem_buf.sem = self.nc.alloc_semaphore("dma_" + sem_buf.name)
        groups = [list(range(NCORES))]
        ins = Ins("pool", lambda e: e.collective_compute(kind, ALU.bypass, replica_groups=groups, ins=[in_ap], outs=[out_ap]))
        self._collect(ins, list(reads), list(writes))
        sem_buf.dma_cnt += 16
        ins.dma_sem = sem_buf.sem
        ins.dma_val = sem_buf.dma_cnt
        self.all_dma.append(ins)
        self._push("pool", ins)
        return ins

    def finish(self):
        fin = Ins("sp", None)
        fin.deps = list(self.out_events)
        self.q["sp"].append(fin)
        for e in self.ENGS:
            for ins in self.q[e]:
                for d in ins.deps:
                    if d.dma_sem is None:
                        d.milestone = True
        for e in self.ENGS:
            c = 0
            for ins in self.q[e]:
                if ins.milestone:
                    c += 1
                    ins.ms = c
        nc = self.nc
        engobj = {"pe": "tensor", "act": "scalar", "dve": "vector", "pool": "gpsimd", "sp": "sync"}

        def emit(ename, e):
            waited = {}
            for ins in self.q[ename]:
                need = {}
                for d in ins.deps:
                    if d.dma_sem is not None:
                        s, v = d.dma_sem, d.dma_val
                    else:
                        s, v = self.esem[d.eng], d.ms
                    k = id(s)
                    if waited.get(k, 0) >= v:
                        continue
                    if k not in need or need[k][1] < v:
                        need[k] = (s, v)
                for k, (s, v) in need.items():
                    e.wait_ge(s, v)
                    waited[k] = v
                if ins.fn is None:
                    continue
                r = ins.fn(e)
                if ins.dma_sem is not None:
                    r.then_inc(ins.dma_sem, 16)
                elif ins.milestone:
                    r.then_inc(self.esem[ename], 1)

        with nc.Block() as block:
            for ename in self.ENGS:
                if not self.q[ename]:
                    continue
                getattr(block, engobj[ename])(lambda e, en=ename: emit(en, e))


ARENA_BYTES = 212800


class NCP:
    _DTB = None

    def __init__(self, nc):
        self._nc = nc
        self._arena = nc.alloc_sbuf_tensor("arena", [128, ARENA_BYTES // 4], F32)
        self._banks = [nc.alloc_psum_tensor("bank%d" % i, [128, 512], F32) for i in range(8)]
        self.reset()

    def reset(self):
        self._off = 0
        self._nbank = 0

    @staticmethod
    def _view(ap2d, shape, dt):
        if dt != F32:
            ap2d = ap2d.bitcast(dt)
        if len(shape) == 2:
            return ap2d
        names = " ".join("d%d" % i for i in range(1, len(shape)))
        kw = {"d%d" % i: shape[i] for i in range(1, len(shape))}
        return ap2d.rearrange("p (%s) -> p %s" % (names, names), **kw)

    def alloc_sbuf_tensor(self, name, shape, dt):
        esz = 2 if dt == BF16 else 4
        n = 1
        for d in shape[1:]:
            n *= d
        nbytes = (n * esz + 31) // 32 * 32
        assert self._off + nbytes <= ARENA_BYTES, "arena overflow at %s (%d + %d)" % (name, self._off, nbytes)
        o4 = self._off // 4
        self._off += nbytes
        return self._view(self._arena[0:shape[0], o4:o4 + (n * esz + 3) // 4], shape, dt)

    def alloc_psum_tensor(self, name, shape, dt):
        esz = 2 if dt == BF16 else 4
        n = 1
        for d in shape[1:]:
            n *= d
        assert n * esz <= 2048 and self._nbank < 8, "psum overflow at " + name
        bk = self._banks[self._nbank]
        self._nbank += 1
        return self._view(bk[0:shape[0], 0:(n * esz + 3) // 4], shape, dt)

    def __getattr__(self, k):
        return getattr(self._nc, k)


def bcast_rows(ap_row, nparts):
    return ap_row.broadcast(0, nparts) if hasattr(ap_row, "broadcast") else ap_row


def build_A():
    nc = bass.Bass("TRN2", target_bir_lowering=False)
    S = Sched(nc)
    NT = TOK // 128
    x_own = nc.dram_tensor("x_own", [TOK, D], F32, kind="ExternalInput").ap()
    x_halo = nc.dram_tensor("x_halo", [128, D], F32, kind="ExternalInput").ap()
    emask = nc.dram_tensor("emask", [128, 2], F32, kind="ExternalInput").ap()
    gnorm = nc.dram_tensor("gnorm", [128, D], F32, kind="ExternalInput").ap()
    ident_d = nc.dram_tensor("ident", [128, 128], F32, kind="ExternalInput").ap()
    w_in = nc.dram_tensor("w_in", [D, 3 * D], F32, kind="ExternalInput").ap()
    b_in = nc.dram_tensor("b_in", [128, 24], F32, kind="ExternalInput").ap()
    cw = nc.dram_tensor("cw", [128, 3 * 24], F32, kind="ExternalInput").ap()
    cb = nc.dram_tensor("cb", [128, 24], F32, kind="ExternalInput").ap()
    sT = nc.dram_tensor("sT", [D, TOK], F32, kind="ExternalOutput").ap()
    x0T = nc.dram_tensor("x0T", [D, TOK], F32, kind="ExternalOutput").ap()

    A = nc.alloc_sbuf_tensor
    hT = A("hT", [128, 8, TOK + 2], BF16)
    wbf = A("wbf", [128, 8, 3 * D], BF16)
    wst = [A("wst%d" % i, [128, 768], F32) for i in range(2)]
    xt = [A("xt%d" % i, [128, D], F32) for i in range(2)]
    sq = A("sq", [128, D], F32)
    hb = [A("hb%d" % i, [128, D], BF16) for i in range(2)]
    ss = [A("ss%d" % i, [128, 1], F32) for i in range(2)]
    rs = [A("rs%d" % i, [128, 1], F32) for i in range(2)]
    gt = A("gt", [128, D], F32)
    idf = A("idf", [128, 128], F32)
    idb = A("idb", [128, 128], BF16)
    em = A("em", [128, 2], F32)
    bi = A("bi", [128, 24], F32)
    cwt = A("cwt", [128, 72], F32)
    cbt = A("cbt", [128, 24], F32)
    zb = [A("zb%d" % i, [128, TOK + 2], F32) for i in range(2)]
    acc = [A("acc%d" % i, [128, TOK], F32) for i in range(2)]
    tp = [nc.alloc_psum_tensor("tp%d" % i, [128, 8 * 128], BF16) for i in range(2)]
    mp = [nc.alloc_psum_tensor("mp%d" % i, [128, 512], F32) for i in range(4)]
    hp = nc.alloc_psum_tensor("hp", [128, 2], F32)

    B = S.buf
    b_hT, b_wbf = B("hT"), B("wbf")
    b_wst = [B("wst0"), B("wst1")]
    b_xt = [B("xt0"), B("xt1")]
    b_sq = B("sq")
    b_hb = [B("hb0"), B("hb1")]
    b_ss = [B("ss0"), B("ss1")]
    b_rs = [B("rs0"), B("rs1")]
    b_c = B("consts")
    b_idb = B("idb")
    b_zb = [B("zb0"), B("zb1")]
    b_acc = [B("acc0"), B("acc1")]
    b_tp = [B("tp0"), B("tp1")]
    b_mp = [B("mp%d" % i) for i in range(4)]
    b_hp = B("hp")

    for dst, src in ((gt, gnorm), (idf, ident_d), (em, emask), (bi, b_in), (cwt, cw), (cbt, cb)):
        S.dma("sp", dst[:], src, writes=[b_c], sem_buf=b_c)
    S.op("dve", lambda e: e.tensor_copy(out=idb[:], in_=idf[:]), reads=[b_c], writes=[b_idb])

    for k2 in range(32):
        k, hf = divmod(k2, 4)
        S.dma("pool", wst[k2 % 2][:], w_in[k * 128:(k + 1) * 128, hf * 768:(hf + 1) * 768], writes=[b_wst[k2 % 2]])
        S.op("pool", lambda e, k=k, hf=hf, k2=k2: e.tensor_copy(out=wbf[:, k, hf * 768:(hf + 1) * 768], in_=wst[k2 % 2][:]),
             reads=[b_wst[k2 % 2]], writes=[b_wbf])

    for i in range(NT + 1):
        j = i % 2
        src = x_own[i * 128:(i + 1) * 128, :] if i < NT else x_halo
        S.dma("sp", xt[j][:], src, writes=[b_xt[j]])
        S.op("act", lambda e, j=j: e.activation(out=sq[:], in_=xt[j][:], func=AF.Square, accum_out=ss[j][:]),
             reads=[b_xt[j]], writes=[b_sq, b_ss[j]])
        S.op("act", lambda e, j=j: e.activation(out=rs[j][:], in_=ss[j][:], func=AF.Sqrt, scale=1.0 / D, bias=EPS),
             reads=[b_ss[j]], writes=[b_rs[j]])
        S.op("dve", lambda e, j=j: e.reciprocal(out=rs[j][:], in_=rs[j][:]), reads=[b_rs[j]], writes=[b_rs[j]])
        S.op("dve", lambda e, j=j: e.scalar_tensor_tensor(out=hb[j][:], in0=xt[j][:], scalar=rs[j][:, 0:1], in1=gt[:],
                                                          op0=ALU.mult, op1=ALU.mult),
             reads=[b_xt[j], b_rs[j], b_c], writes=[b_hb[j]])
        for k in range(8):
            S.op("pe", lambda e, j=j, k=k: e.transpose(out=tp[j][:, k * 128:(k + 1) * 128],
                                                        in_=hb[j][:, k * 128:(k + 1) * 128], identity=idb[:]),
                 reads=[b_hb[j], b_idb], writes=[b_tp[j]])
        if i < NT:
            S.op("act", lambda e, j=j, i=i: e.copy(out=hT[:, :, 1 + i * 128:1 + (i + 1) * 128],
                                                   in_=tp[j][:].rearrange("p (k t) -> p k t", k=8)),
                 reads=[b_tp[j]], writes=[b_hT])
        else:
            S.op("act", lambda e, j=j: e.copy(out=hT[:, :, 0:TOK + 2:TOK + 1],
                                              in_=tp[j][:].rearrange("p (k t) -> p k t", k=8)[:, :, 0:2]),
                 reads=[b_tp[j]], writes=[b_hT])

    def proj_conv(cc, zi, ai):
        for jg in range(8):
            m = (cc * 8 + jg) % 4
            for k in range(8):
                S.op("pe", lambda e, m=m, k=k, jg=jg: e.matmul(mp[m][:], lhsT=wbf[:, k, cc * 128:(cc + 1) * 128],
                                                               rhs=hT[:, k, 1 + jg * 512:1 + (jg + 1) * 512],
                                                               start=(k == 0), stop=(k == 7)),
                     reads=[b_wbf, b_hT], writes=[b_mp[m]])
            S.op("act", lambda e, m=m, jg=jg: e.activation(out=zb[zi][:, 1 + jg * 512:1 + (jg + 1) * 512], in_=mp[m][:],
                                                           func=AF.Identity, bias=bi[:, cc:cc + 1], scale=1.0),
                 reads=[b_mp[m], b_c], writes=[b_zb[zi]])
        for k in range(8):
            S.op("pe", lambda e, k=k: e.matmul(hp[:], lhsT=wbf[:, k, cc * 128:(cc + 1) * 128],
                                               rhs=hT[:, k, 0:TOK + 2:TOK + 1], start=(k == 0), stop=(k == 7)),
                 reads=[b_wbf, b_hT], writes=[b_hp])
        S.op("act", lambda e: e.activation(out=zb[zi][:, 0:TOK + 2:TOK + 1], in_=hp[:], func=AF.Identity,
                                           bias=bi[:, cc:cc + 1], scale=1.0),
             reads=[b_hp, b_c], writes=[b_zb[zi]])
        S.op("dve", lambda e: e.tensor_tensor(out=zb[zi][:, 0:TOK + 2:TOK + 1], in0=zb[zi][:, 0:TOK + 2:TOK + 1],
                                              in1=em[:], op=ALU.mult),
             reads=[b_zb[zi], b_c], writes=[b_zb[zi]])
        eng = "dve"
        S.op(eng, lambda e: e.tensor_scalar(out=acc[ai][:], in0=zb[zi][:, 0:TOK], scalar1=cwt[:, cc:cc + 1],
                                            scalar2=cbt[:, cc:cc + 1], op0=ALU.mult, op1=ALU.add),
             reads=[b_zb[zi], b_c], writes=[b_acc[ai]])
        for t in (1, 2):
            S.op(eng, lambda e, t=t: e.scalar_tensor_tensor(out=acc[ai][:], in0=zb[zi][:, t:t + TOK],
                                                           scalar=cwt[:, t * 24 + cc:t * 24 + cc + 1], in1=acc[ai][:],
                                                           op0=ALU.mult, op1=ALU.add),
                 reads=[b_zb[zi], b_c, b_acc[ai]], writes=[b_acc[ai]])

    for c in range(8):
        proj_conv(c, 0, 0)
        S.dma("sp", x0T[c * 128:(c + 1) * 128, :], acc[0][:], reads=[b_acc[0]], is_output=True)
        proj_conv(8 + c, 1, 1)
        proj_conv(16 + c, 0, 0)
        S.op("pool", lambda e: e.tensor_tensor(out=acc[0][:], in0=acc[0][:], in1=acc[1][:], op=ALU.mult),
             reads=[b_acc[1], b_acc[0]], writes=[b_acc[0]])
        S.dma("sp", sT[c * 128:(c + 1) * 128, :], acc[0][:], reads=[b_acc[0]], is_output=True)
    S.finish()
    return nc


def run_A(inp):
    x = np.ascontiguousarray(inp["x"], dtype=np.float32)
    nc = build_A()
    g = np.ascontiguousarray(np.broadcast_to(inp["norm_mix"][0][None, :], (128, D)), dtype=np.float32)
    ident = np.eye(128, dtype=np.float32)
    w_in = np.ascontiguousarray(inp["hy_w_in"][0], dtype=np.float32)
    b_in = np.ascontiguousarray(inp["hy_b_in"][0].reshape(24, 128).T)
    cwv = np.ascontiguousarray(inp["hy_conv_w"][0].reshape(3, 24, 128).transpose(2, 0, 1).reshape(128, 72))
    cbv = np.ascontiguousarray(inp["hy_conv_b"][0].reshape(24, 128).T)
    in_maps = []
    for c in range(NCORES):
        b, q = divmod(c, 4)
        t0 = q * TOK
        halo = np.zeros((128, D), np.float32)
        em = np.zeros((128, 2), np.float32)
        if q > 0:
            halo[0] = x[b, t0 - 1]
            em[:, 0] = 1.0
        if q < 3:
            halo[1] = x[b, t0 + TOK]
            em[:, 1] = 1.0
        in_maps.append({"x_own": np.ascontiguousarray(x[b, t0:t0 + TOK]), "x_halo": halo, "emask": em, "gnorm": g,
                        "ident": ident, "w_in": w_in, "b_in": b_in, "cw": cwv, "cb": cbv})
    res = run_bass_kernel_spmd(nc, in_maps, core_ids=list(range(NCORES)))
    return res.results


NFFT = 2 * SEQ
CG = 8
NG = 128 // CG


def fft_consts():
    i128 = np.arange(128, dtype=np.float64)
    i256 = np.arange(256, dtype=np.float64)
    c = {}
    a = 2 * np.pi * np.outer(i128, i256) / 256.0
    c["FA"] = np.concatenate([np.cos(a), -np.sin(a)], 1)
    c["FB"] = np.concatenate([np.sin(a), np.cos(a)], 1)
    t = 2 * np.pi * np.outer(i128, i256) / NFFT
    c["TW"] = np.stack([np.cos(t), -np.sin(t)], 1)
    f = 2 * np.pi * np.outer(i128, i128) / 128.0
    c["F128"] = np.stack([np.cos(f), -np.sin(f), np.sin(f)], 1)
    c["GA"] = np.concatenate([np.cos(f), np.sin(f)], 1)
    c["GB"] = np.concatenate([-np.sin(f), np.cos(f)], 1)
    k1 = (128 * np.arange(2)[None, :, None] + i128[:, None, None])
    it = 2 * np.pi * k1 * i128[None, None, :] / NFFT
    c["ITW"] = np.stack([np.cos(it), np.sin(it)], 2)
    h = 2 * np.pi * k1 * i128[None, None, :] / 256.0
    c["H"] = np.stack([np.cos(h), np.sin(h), -np.sin(h)], 2)
    pos = 128 * i128[:, None] + i128[None, :]
    tl = np.linspace(0.0, 1.0, SEQ, dtype=np.float32)
    c["NEGT"] = -tl[pos.astype(np.int64)]
    return {k: np.ascontiguousarray(v, dtype=np.float32) for k, v in c.items()}


def filter_feat():
    f32 = np.float32
    L = SEQ
    t = np.linspace(0.0, 1.0, L, dtype=f32)[:, None]
    w = (f32(2.0 * math.pi) * np.arange(L, dtype=f32)[:, None] / f32(L)).astype(f32)
    bands = np.linspace(1e-4, 15, 16, dtype=f32)[None, :]
    bw = (bands * w).astype(f32)
    feat = np.concatenate([t, np.cos(bw), -np.sin(bw)], axis=-1).astype(f32)
    return np.ascontiguousarray(feat.T)


def build_B():
    nc = bass.Bass("TRN2", target_bir_lowering=False)
    S = Sched(nc)
    DT = nc.dram_tensor
    s_in = DT("s_in", [2, 128, SEQ], F32, kind="ExternalInput").ap()
    featT = DT("featT", [33, SEQ], F32, kind="ExternalInput").ap()
    w1d = DT("f_w1", [33, 64], F32, kind="ExternalInput").ap()
    w2d = DT("f_w2", [64, 64], F32, kind="ExternalInput").ap()
    w3d = DT("f_w3", [64, 64], F32, kind="ExternalInput").ap()
    fbd = DT("f_bf", [64, 4], F32, kind="ExternalInput").ap()
    woutd = DT("f_wout", [64, NG * 2 * CG], F32, kind="ExternalInput").ap()
    decd = DT("decay", [1, NG * 2 * CG], F32, kind="ExternalInput").ap()
    cd = {}
    shapes = {"FA": [128, 512], "FB": [128, 512], "TW": [128, 2, 256], "F128": [128, 3, 128], "GA": [128, 256],
              "GB": [128, 256], "ITW": [128, 2, 2, 128], "H": [128, 2, 3, 128], "NEGT": [128, 128]}
    for k, sh in shapes.items():
        cd[k] = DT("c_" + k, sh, F32, kind="ExternalInput").ap()
    y_out = DT("y_out", [2, 128, SEQ], F32, kind="ExternalOutput").ap()

    A = nc.alloc_sbuf_tensor
    B = S.buf
    cf = {k: A("cf_" + k, sh, F32) for k, sh in shapes.items()}
    cb = {k: A("cb_" + k, shapes[k], BF16) for k in ("FA", "FB", "F128", "GA", "GB", "H")}
    b_const = B("const")
    for k in shapes:
        S.dma("sp", cf[k][:], cd[k], writes=[b_const], sem_buf=b_const)
    b_cb = B("constbf")
    for k in cb:
        S.op("pool", lambda e, k=k: e.tensor_copy(out=cb[k][:], in_=cf[k][:]), reads=[b_const], writes=[b_cb])
    w1s, w2s, w3s = A("w1s", [33, 64], F32), A("w2s", [64, 64], F32), A("w3s", [64, 64], F32)
    fb = A("fb", [64, 4], F32)
    fbb = A("fbb", [64, 3], F32)
    wout = A("wout", [64, NG * 2 * CG], F32)
    absdec = A("absdec", [128, NG * 2 * CG], F32)
    ones = A("ones", [128, 128], F32)
    b_fw = B("fw")
    for dst, src in ((w1s, w1d), (w2s, w2d), (w3s, w3d), (fb, fbd), (wout, woutd)):
        S.dma("sp", dst[:], src, writes=[b_fw], sem_buf=b_fw)
    S.dma("sp", absdec[:], decd.broadcast_to([128, NG * 2 * CG]), writes=[b_fw], sem_buf=b_fw)
    b_fw2 = B("fw2")
    S.op("act", lambda e: e.activation(out=absdec[:], in_=absdec[:], func=AF.Abs),
         reads=[b_fw], writes=[b_fw])
    S.op("dve", lambda e: e.tensor_tensor(out=fbb[:], in0=fb[:, 0:3], in1=fb[:, 3:4].broadcast_to([64, 3]), op=ALU.mult),
         reads=[b_fw], writes=[b_fw2])
    S.op("pool", lambda e: e.memset(ones[:], 1.0), writes=[b_fw2])

    h3 = A("h3", [64, SEQ], F32)
    b_h3 = B("h3")
    ft = [A("ft%d" % i, [33, 512], F32) for i in range(2)]
    b_ft = [B("ft0"), B("ft1")]
    arg = [A("arg%d" % i, [64, 512], F32) for i in range(2)]
    b_arg = [B("arg0"), B("arg1")]
    hh = [A("hh%d" % i, [64, 512], F32) for i in range(2)]
    b_hh = [B("hh0"), B("hh1")]
    NPS = 8
    ps = [nc.alloc_psum_tensor("ps%d" % i, [128, 512], F32) for i in range(NPS)]
    b_ps = [B("ps%d" % i) for i in range(NPS)]
    TWO_PI = 2.0 * math.pi
    OFF = math.pi + 4 * TWO_PI
    cnt = [0]

    fs = A("fs", [64, 1], F32)
    fu = A("fu", [64, 3], F32)
    S.op("dve", lambda e: e.tensor_scalar(out=fs[:], in0=fb[:, 3:4], scalar1=1.0 / TWO_PI, scalar2=None, op0=ALU.mult),
         reads=[b_fw], writes=[b_fw2])
    S.op("dve", lambda e: e.tensor_scalar(out=fu[:], in0=fbb[:], scalar1=1.0 / TWO_PI, scalar2=4.5, op0=ALU.mult, op1=ALU.add),
         reads=[b_fw2], writes=[b_fw2])
    ki = [A("ki%d" % i, [64, 512], mybir.dt.int32) for i in range(2)]
    kf = [A("kf%d" % i, [64, 512], F32) for i in range(2)]

    def mlp_layer(pi, lhsT, rhs_ap, rhs_buf, li, out_ap, out_buf):
        a = cnt[0] % 2
        cnt[0] += 1
        S.op("pe", lambda e: e.matmul(ps[pi][0:64, :], lhsT=lhsT, rhs=rhs_ap, start=True, stop=True),
             reads=[b_fw, rhs_buf], writes=[b_ps[pi]])
        S.op("dve", lambda e: e.tensor_scalar(out=arg[a][:], in0=ps[pi][0:64, :], scalar1=fs[:, 0:1], scalar2=fu[:, li:li + 1],
                                              op0=ALU.mult, op1=ALU.add),
             reads=[b_ps[pi], b_fw, b_fw2], writes=[b_arg[a]])
        S.op("dve", lambda e: e.tensor_copy(out=ki[a][:], in_=arg[a][:]), reads=[b_arg[a]], writes=[b_arg[a]])
        S.op("dve", lambda e: e.tensor_copy(out=kf[a][:], in_=ki[a][:]), reads=[b_arg[a]], writes=[b_arg[a]])
        S.op("dve", lambda e: e.tensor_tensor(out=arg[a][:], in0=arg[a][:], in1=kf[a][:], op=ALU.subtract),
             reads=[b_arg[a]], writes=[b_arg[a]])
        S.op("dve", lambda e: e.scalar_tensor_tensor(out=arg[a][:], in0=arg[a][:], scalar=0.0, in1=arg[a][:],
                                                     op0=ALU.is_lt, op1=ALU.add),
             reads=[b_arg[a]], writes=[b_arg[a]])
        S.op("act", lambda e: e.activation(out=out_ap, in_=arg[a][:], func=AF.Sin, bias=negpi[:, 0:1], scale=6.283185),
             reads=[b_arg[a], b_fw2], writes=[out_buf])

    negpi = A("negpi", [64, 1], F32)
    S.op("pool", lambda e: e.memset(negpi[:], -3.1415925), writes=[b_fw2])
    for pg in range(SEQ // 512):
        j = pg % 2
        S.dma("sp", ft[j][:], featT[:, pg * 512:(pg + 1) * 512], writes=[b_ft[j]])
        mlp_layer(0, w1s[:], ft[j][:], b_ft[j], 0, hh[0][:], b_hh[0])
        mlp_layer(1, w2s[:], hh[0][:], b_hh[0], 1, hh[1][:], b_hh[1])
        mlp_layer(2, w3s[:], hh[1][:], b_hh[1], 2, h3[:, pg * 512:(pg + 1) * 512], b_h3)

    G1f = A("G1f", [128, 2 * CG, 128], F32)
    win = A("win", [128, 128, 2 * CG], F32)
    pm = A("pm", [128, 2, CG, 128], BF16)
    Gh = A("Gh", [128, CG, 2, 256], F32)
    part = A("part", [128, 2 * CG], F32)
    tot = A("tot", [128, 2 * CG], F32)
    rn = A("rn", [128, CG], F32)
    D1f = A("D1f", [128, CG, 2, 128], F32)
    D1b = A("D1b", [128, CG, 2, 128], BF16)
    Bb = [A("Bb%d" % i, [128, 2, 256], BF16) for i in range(2)]
    P1 = [A("P1_%d" % i, [128, 512], F32) for i in range(2)]
    P2 = [A("P2_%d" % i, [128, 512], F32) for i in range(2)]
    Pb = [A("Pb%d" % i, [128, 2, 256], BF16) for i in range(2)]
    Cb = [A("Cb%d" % i, [128, 2, 2, 4, 128], BF16) for i in range(2)]
    yb = [A("yb%d" % i, [128, 4, 2, 128], F32) for i in range(2)]
    b_G1f, b_win, b_pm, b_Gh, b_part, b_rn = B("G1f"), B("win"), B("pm"), B("Gh"), B("part"), B("rn")
    b_D1f, b_D1b = B("D1f"), B("D1b")
    b_Bb = [B("Bb0"), B("Bb1")]
    b_P = [B("P0"), B("P1")]
    b_Pb = [B("Pb0"), B("Pb1")]
    b_Cb = [B("Cb0"), B("Cb1")]
    b_yb = [B("yb0"), B("yb1")]
    pctr = [0]
    cctr = [0]

    def next_ps():
        pctr[0] += 1
        return pctr[0] % NPS

    def cmul(psv, t1, t2, o_re, o_im, shp):
        k = cctr[0] % 2
        cctr[0] += 1
        p1 = P1[k][:].rearrange(shp[0], **shp[1])
        p2 = P2[k][:].rearrange(shp[0], **shp[1])
        return k, p1, p2

    def fwd_stage(lhs_re, lhs_im, lhs_buf, bsel):
        pi = next_ps()
        S.op("pe", lambda e: e.matmul(ps[pi][:], lhsT=lhs_re, rhs=cb["FA"][:], start=True, stop=(lhs_im is None)),
             reads=[lhs_buf, b_cb], writes=[b_ps[pi]])
        if lhs_im is not None:
            S.op("pe", lambda e: e.matmul(ps[pi][:], lhsT=lhs_im, rhs=cb["FB"][:], start=False, stop=True),
                 reads=[lhs_buf, b_cb], writes=[b_ps[pi]])
        k = cctr[0] % 2
        cctr[0] += 1
        pv = ps[pi][:].rearrange("p (r k) -> p r k", r=2)
        p1 = P1[k][:].rearrange("p (r k) -> p r k", r=2)
        p2 = P2[k][:].rearrange("p (r k) -> p r k", r=2)
        tre = cf["TW"][:, 0:1, :].broadcast_to([128, 2, 256])
        tim = cf["TW"][:, 1:2, :].broadcast_to([128, 2, 256])
        S.op("dve", lambda e: e.tensor_tensor(out=p1, in0=pv, in1=tre, op=ALU.mult),
             reads=[b_ps[pi], b_const], writes=[b_P[k]])
        S.op("dve", lambda e: e.tensor_tensor(out=p2, in0=pv, in1=tim, op=ALU.mult),
             reads=[b_ps[pi], b_const], writes=[b_P[k]])
        S.op("pool", lambda e: e.tensor_tensor(out=Bb[bsel][:, 0, :], in0=P1[k][:, 0:256], in1=P2[k][:, 256:512], op=ALU.subtract),
             reads=[b_P[k]], writes=[b_Bb[bsel]])
        S.op("pool", lambda e: e.tensor_tensor(out=Bb[bsel][:, 1, :], in0=P2[k][:, 0:256], in1=P1[k][:, 256:512], op=ALU.add),
             reads=[b_P[k]], writes=[b_Bb[bsel]])

    for g in range(NG):
        c0 = g * CG
        gs = slice(g * 2 * CG, (g + 1) * 2 * CG)
        S.op("dve", lambda e, gs=gs: e.tensor_tensor(out=win[:], in0=cf["NEGT"][:].unsqueeze(2).broadcast_to([128, 128, 2 * CG]),
                                                     in1=absdec[:, gs].unsqueeze(1).broadcast_to([128, 128, 2 * CG]), op=ALU.mult),
             reads=[b_const, b_fw], writes=[b_win])
        S.op("act", lambda e: e.activation(out=win[:], in_=win[:], func=AF.Exp), reads=[b_win], writes=[b_win])
        for a16 in range(8):
            pi = next_ps()
            fo = ps[pi][:, 0:16 * 2 * CG].rearrange("p (i s) -> p i s", i=16)
            for i in range(16):
                n2 = a16 * 16 + i
                S.op("pe", lambda e, n2=n2, i=i, pi=pi, gs=gs: e.matmul(ps[pi][:, i * 2 * CG:(i + 1) * 2 * CG], lhsT=h3[:, n2:SEQ:128],
                                                                       rhs=wout[:, gs], start=True, stop=True),
                     reads=[b_h3, b_fw], writes=[b_ps[pi]])
            S.op("dve", lambda e, a16=a16, fo=fo: e.tensor_tensor(
                out=G1f[:, :, a16 * 16:(a16 + 1) * 16].rearrange("p s i -> p i s"), in0=fo,
                in1=win[:, a16 * 16:(a16 + 1) * 16, :], op=ALU.mult),
                 reads=[b_ps[pi], b_win], writes=[b_G1f])
        S.op("pool", lambda e: e.memset(G1f[0:1, CG:2 * CG, 0:1], 0.0), writes=[b_G1f])
        wv = win[:].rearrange("p a s -> p (a s)").rearrange("p (s n) -> p s n", n=128)
        S.op("act", lambda e, wv=wv: e.activation(out=wv, in_=G1f[:], func=AF.Abs),
             reads=[b_G1f], writes=[b_win])
        S.op("dve", lambda e, wv=wv: e.tensor_reduce(out=part[:], in_=wv, axis=AX.X, op=ALU.add),
             reads=[b_win], writes=[b_part])
        pi = next_ps()
        S.op("pe", lambda e, pi=pi: e.matmul(ps[pi][:, 0:2 * CG], lhsT=ones[:], rhs=part[:], start=True, stop=True),
             reads=[b_part, b_fw2], writes=[b_ps[pi]])
        S.op("act", lambda e, pi=pi: e.copy(out=tot[:], in_=ps[pi][:, 0:2 * CG]), reads=[b_ps[pi]], writes=[b_part])
        S.op("dve", lambda e: e.tensor_tensor(out=rn[:], in0=tot[:, 0:CG], in1=tot[:, CG:2 * CG], op=ALU.add),
             reads=[b_part], writes=[b_rn])
        S.op("dve", lambda e: e.tensor_scalar(out=rn[:], in0=rn[:], scalar1=float(NFFT), scalar2=None, op0=ALU.mult),
             reads=[b_rn], writes=[b_rn])
        S.op("dve", lambda e: e.reciprocal(out=rn[:], in_=rn[:]), reads=[b_rn], writes=[b_rn])
        S.op("pool", lambda e: e.tensor_tensor(out=pm[:, 0], in0=G1f[:, 0:CG, :], in1=G1f[:, CG:2 * CG, :], op=ALU.add),
             reads=[b_G1f], writes=[b_pm])
        S.op("pool", lambda e: e.tensor_tensor(out=pm[:, 1], in0=G1f[:, 0:CG, :], in1=G1f[:, CG:2 * CG, :], op=ALU.subtract),
             reads=[b_G1f], writes=[b_pm])
        for b in range(2):
            S.dma("sp", D1f[:, :, b, :], s_in[b, c0:c0 + CG, :].rearrange("c (n1 n2) -> n1 c n2", n2=128),
                  writes=[b_D1f])
        S.op("act", lambda e: e.copy(out=D1b[:], in_=D1f[:]), reads=[b_D1f], writes=[b_D1b])
        for cl in range(CG):
            fwd_stage(pm[:, 0, cl, :], None, b_pm, 0)
            fwd_stage(pm[:, 1, cl, :], None, b_pm, 1)
            pi = next_ps()
            F = cb["F128"]
            S.op("pe", lambda e, pi=pi: e.matmul(ps[pi][:, 0:256], lhsT=F[:, 0, :], rhs=Bb[0][:, 0, :], start=True, stop=False),
                 reads=[b_Bb[0], b_cb], writes=[b_ps[pi]])
            S.op("pe", lambda e, pi=pi: e.matmul(ps[pi][:, 0:256], lhsT=F[:, 2, :], rhs=Bb[0][:, 1, :], start=False, stop=True),
                 reads=[b_Bb[0], b_cb], writes=[b_ps[pi]])
            S.op("pe", lambda e, pi=pi: e.matmul(ps[pi][:, 256:512], lhsT=F[:, 1, :], rhs=Bb[1][:, 0, :], start=True, stop=False),
                 reads=[b_Bb[1], b_cb], writes=[b_ps[pi]])
            S.op("pe", lambda e, pi=pi: e.matmul(ps[pi][:, 256:512], lhsT=F[:, 0, :], rhs=Bb[1][:, 1, :], start=False, stop=True),
                 reads=[b_Bb[1], b_cb], writes=[b_ps[pi]])
            S.op("act", lambda e, pi=pi, cl=cl: e.activation(out=Gh[:, cl].rearrange("p r k -> p (r k)"), in_=ps[pi][:],
                                                            func=AF.Copy, scale=rn[:, cl:cl + 1]),
                 reads=[b_ps[pi], b_rn], writes=[b_Gh])
        for q4 in range(CG // 4):
            cbi = (g * (CG // 4) + q4) % 2
            for ci in range(4):
                cl = q4 * 4 + ci
                fwd_stage(D1b[:, cl, 0, :], D1b[:, cl, 1, :], b_D1b, 0)
                pi = next_ps()
                F = cb["F128"]
                S.op("pe", lambda e, pi=pi: e.matmul(ps[pi][:, 0:256], lhsT=F[:, 0, :], rhs=Bb[0][:, 0, :], start=True, stop=False),
                     reads=[b_Bb[0], b_cb], writes=[b_ps[pi]])
                S.op("pe", lambda e, pi=pi: e.matmul(ps[pi][:, 0:256], lhsT=F[:, 2, :], rhs=Bb[0][:, 1, :], start=False, stop=True),
                     reads=[b_Bb[0], b_cb], writes=[b_ps[pi]])
                S.op("pe", lambda e, pi=pi: e.matmul(ps[pi][:, 256:512], lhsT=F[:, 1, :], rhs=Bb[0][:, 0, :], start=True, stop=False),
                     reads=[b_Bb[0], b_cb], writes=[b_ps[pi]])
                S.op("pe", lambda e, pi=pi: e.matmul(ps[pi][:, 256:512], lhsT=F[:, 0, :], rhs=Bb[0][:, 1, :], start=False, stop=True),
                     reads=[b_Bb[0], b_cb], writes=[b_ps[pi]])
                k = cctr[0] % 2
                cctr[0] += 1
                pk = pctr[0] % 2
                pv = ps[pi][:].rearrange("p (r k) -> p r k", r=2)
                p1 = P1[k][:].rearrange("p (r k) -> p r k", r=2)
                p2 = P2[k][:].rearrange("p (r k) -> p r k", r=2)
                S.op("dve", lambda e, pv=pv, p1=p1, cl=cl: e.tensor_tensor(out=p1, in0=pv, in1=Gh[:, cl, 0:1, :].broadcast_to([128, 2, 256]), op=ALU.mult),
                     reads=[b_ps[pi], b_Gh], writes=[b_P[k]])
                S.op("dve", lambda e, pv=pv, p2=p2, cl=cl: e.tensor_tensor(out=p2, in0=pv, in1=Gh[:, cl, 1:2, :].broadcast_to([128, 2, 256]), op=ALU.mult),
                     reads=[b_ps[pi], b_Gh], writes=[b_P[k]])
                S.op("pool", lambda e, k=k, pk=pk: e.tensor_tensor(out=Pb[pk][:, 0, :], in0=P1[k][:, 0:256], in1=P2[k][:, 256:512], op=ALU.subtract),
                     reads=[b_P[k]], writes=[b_Pb[pk]])
                S.op("pool", lambda e, k=k, pk=pk: e.tensor_tensor(out=Pb[pk][:, 1, :], in0=P2[k][:, 0:256], in1=P1[k][:, 256:512], op=ALU.add),
                     reads=[b_P[k]], writes=[b_Pb[pk]])
                pi2 = next_ps()
                for j in range(2):
                    S.op("pe", lambda e, pi2=pi2, j=j, pk=pk: e.matmul(ps[pi2][:, j * 256:(j + 1) * 256], lhsT=Pb[pk][:, 0, j * 128:(j + 1) * 128],
                                                                     rhs=cb["GA"][:], start=True, stop=False),
                         reads=[b_Pb[pk], b_cb], writes=[b_ps[pi2]])
                    S.op("pe", lambda e, pi2=pi2, j=j, pk=pk: e.matmul(ps[pi2][:, j * 256:(j + 1) * 256], lhsT=Pb[pk][:, 1, j * 128:(j + 1) * 128],
                                                                     rhs=cb["GB"][:], start=False, stop=True),
                         reads=[b_Pb[pk], b_cb], writes=[b_ps[pi2]])
                k = cctr[0] % 2
                cctr[0] += 1
                cv = ps[pi2][:].rearrange("p (j r n) -> p j r n", j=2, r=2)
                p1 = P1[k][:].rearrange("p (j r n) -> p j r n", j=2, r=2)
                p2 = P2[k][:].rearrange("p (j r n) -> p j r n", j=2, r=2)
                S.op("dve", lambda e, cv=cv, p1=p1: e.tensor_tensor(out=p1, in0=cv, in1=cf["ITW"][:, :, 0:1, :].broadcast_to([128, 2, 2, 128]), op=ALU.mult),
                     reads=[b_ps[pi2], b_const], writes=[b_P[k]])
                S.op("dve", lambda e, cv=cv, p2=p2: e.tensor_tensor(out=p2, in0=cv, in1=cf["ITW"][:, :, 1:2, :].broadcast_to([128, 2, 2, 128]), op=ALU.mult),
                     reads=[b_ps[pi2], b_const], writes=[b_P[k]])
                S.op("pool", lambda e, p1=p1, p2=p2, ci=ci, cbi=cbi: e.tensor_tensor(out=Cb[cbi][:, :, 0, ci, :], in0=p1[:, :, 0, :], in1=p2[:, :, 1, :], op=ALU.subtract),
                     reads=[b_P[k]], writes=[b_Cb[cbi]])
                S.op("pool", lambda e, p1=p1, p2=p2, ci=ci, cbi=cbi: e.tensor_tensor(out=Cb[cbi][:, :, 1, ci, :], in0=p2[:, :, 0, :], in1=p1[:, :, 1, :], op=ALU.add),
                     reads=[b_P[k]], writes=[b_Cb[cbi]])
            pr, pim = next_ps(), next_ps()
            H = cb["H"]
            seq = [(pr, 0, 0, True), (pr, 2, 1, False), (pim, 1, 0, True), (pim, 0, 1, False)]
            for (pp, hsel, ri, first) in seq:
                for j in range(2):
                    S.op("pe", lambda e, pp=pp, hsel=hsel, ri=ri, j=j, first=first, cbi=cbi: e.matmul(
                        ps[pp][:], lhsT=H[:, j, hsel, :], rhs=Cb[cbi][:, j, ri, :, :].rearrange("p c n -> p (c n)"),
                        start=(first and j == 0), stop=((not first) and j == 1)),
                         reads=[b_Cb[cbi], b_cb], writes=[b_ps[pp]])
            S.op("act", lambda e, pr=pr, cbi=cbi: e.copy(out=yb[cbi][:, :, 0, :], in_=ps[pr][:].rearrange("p (c n) -> p c n", c=4)),
                 reads=[b_ps[pr]], writes=[b_yb[cbi]])
            S.op("act", lambda e, pim=pim, cbi=cbi: e.copy(out=yb[cbi][:, :, 1, :], in_=ps[pim][:].rearrange("p (c n) -> p c n", c=4)),
                 reads=[b_ps[pim]], writes=[b_yb[cbi]])
            for b in range(2):
                cc = c0 + q4 * 4
                S.dma("sp", y_out[b, cc:cc + 4, :].rearrange("c (n1 n2) -> n1 c n2", n2=128), yb[cbi][:, :, b, :],
                      reads=[b_yb[cbi]], is_output=True)
    S.finish()
    return nc


def run_B(inp, s_cs):
    nc = build_B()
    consts = fft_consts()
    featT = filter_feat()
    fbf = np.stack([inp["hy_f_b1"][0], inp["hy_f_b2"][0], inp["hy_f_b3"][0], inp["hy_f_freq"][0]], 1).astype(np.float32)
    in_maps = []
    for c in range(NCORES):
        wo = inp["hy_f_wout"][0].reshape(64, 2, 8, NG, CG)[:, :, c]
        wo = np.ascontiguousarray(wo.transpose(0, 2, 1, 3).reshape(64, NG * 2 * CG))
        de = inp["hy_decay"][0].reshape(2, 8, NG, CG)[:, c]
        de = np.ascontiguousarray(de.transpose(1, 0, 2).reshape(1, NG * 2 * CG))
        m = {"s_in": np.ascontiguousarray(s_cs[c]), "featT": featT,
             "f_w1": np.ascontiguousarray(inp["hy_f_w1"][0]), "f_w2": np.ascontiguousarray(inp["hy_f_w2"][0]),
             "f_w3": np.ascontiguousarray(inp["hy_f_w3"][0]), "f_bf": np.ascontiguousarray(fbf),
             "f_wout": wo, "decay": de}
        for k, v in consts.items():
            m["c_" + k] = v
        in_maps.append(m)
    res = run_bass_kernel_spmd(nc, in_maps, core_ids=list(range(NCORES)))
    return res.results


def load_weight_bf16(S, nc, w_dram, rows, cols, name, bufname, qeng="pool", ceng="pool", stage=None):
    nk = rows // 128
    wt = nc.alloc_sbuf_tensor(name + "_sb", [128, nk, cols], BF16)
    b_w = S.buf(bufname)
    if stage is None:
        st = [nc.alloc_sbuf_tensor(name + "_st%d" % i, [128, 1024], F32) for i in range(2)]
        b_st = [S.buf(name + "_st0"), S.buf(name + "_st1")]
        stage = (st, b_st, [0])
    st, b_st, ctr = stage
    for k in range(nk):
        for c0 in range(0, cols, 1024):
            cw = min(1024, cols - c0)
            i = ctr[0] % 2
            ctr[0] += 1
            S.dma(qeng, st[i][:, 0:cw], w_dram[k * 128:(k + 1) * 128, c0:c0 + cw], writes=[b_st[i]])
            ce = ceng if isinstance(ceng, str) else ceng[ctr[0] % len(ceng)]
            if ce == "act":
                S.op("act", lambda e, i=i, k=k, c0=c0, cw=cw: e.copy(out=wt[:, k, c0:c0 + cw], in_=st[i][:, 0:cw]),
                     reads=[b_st[i]], writes=[b_w])
            else:
                S.op(ce, lambda e, i=i, k=k, c0=c0, cw=cw: e.tensor_copy(out=wt[:, k, c0:c0 + cw], in_=st[i][:, 0:cw]),
                     reads=[b_st[i]], writes=[b_w])
    return wt, b_w, stage


class NormT:
    def __init__(self, S, nc, gt, b_gt, idb, b_idb, tag, ntp=2):
        A = nc.alloc_sbuf_tensor
        self.S, self.nc = S, nc
        self.ntp = ntp
        self.gt, self.b_gt, self.idb, self.b_idb = gt, b_gt, idb, b_idb
        self.sq = A(tag + "sq", [128, D], F32)
        self.b_sq = S.buf(tag + "sq")
        self.hb = [A(tag + "hb%d" % i, [128, D], BF16) for i in range(2)]
        self.b_hb = [S.buf(tag + "hb0"), S.buf(tag + "hb1")]
        self.ss = [A(tag + "ss%d" % i, [128, 1], F32) for i in range(2)]
        self.rs = [A(tag + "rs%d" % i, [128, 1], F32) for i in range(2)]
        self.b_s = [S.buf(tag + "s0"), S.buf(tag + "s1")]
        self.tp = [nc.alloc_psum_tensor(tag + "tp%d" % i, [128, 8 * 128], BF16) for i in range(ntp)]
        self.b_tp = [S.buf(tag + "tp%d" % i) for i in range(ntp)]
        self.n = 0

    def rstd(self, x_ap, b_x, j):
        S = self.S
        ss, rs, sq = self.ss[j], self.rs[j], self.sq
        S.op("act", lambda e: e.activation(out=sq[:], in_=x_ap, func=AF.Square, accum_out=ss[:]),
             reads=[b_x], writes=[self.b_sq, self.b_s[j]])
        S.op("act", lambda e: e.activation(out=rs[:], in_=ss[:], func=AF.Sqrt, scale=1.0 / D, bias=EPS),
             reads=[self.b_s[j]], writes=[self.b_s[j]])
        S.op("dve", lambda e: e.reciprocal(out=rs[:], in_=rs[:]), reads=[self.b_s[j]], writes=[self.b_s[j]])
        return rs

    def __call__(self, x_ap, b_x, hT_dst, b_hT):
        S = self.S
        j = self.n % 2
        self.n += 1
        rs = self.rstd(x_ap, b_x, j)
        hb, tp = self.hb[j], self.tp[j % self.ntp]
        b_tp = self.b_tp[j % self.ntp]
        gt, idb = self.gt, self.idb
        S.op("dve", lambda e: e.scalar_tensor_tensor(out=hb[:], in0=x_ap, scalar=rs[:, 0:1], in1=gt[:], op0=ALU.mult, op1=ALU.mult),
             reads=[b_x, self.b_s[j], self.b_gt], writes=[self.b_hb[j]])
        for k in range(8):
            S.op("pe", lambda e, k=k: e.transpose(out=tp[:, k * 128:(k + 1) * 128], in_=hb[:, k * 128:(k + 1) * 128], identity=idb[:]),
                 reads=[self.b_hb[j], self.b_idb], writes=[b_tp])
        S.op("act", lambda e: e.copy(out=hT_dst, in_=tp[:].rearrange("p (k t) -> p k t", k=8)),
             reads=[b_tp], writes=[b_hT])


def load_consts_common(S, nc, gnorm_d, ident_d):
    A = nc.alloc_sbuf_tensor
    gt = A("gt", [128, D], F32)
    idf = A("idf", [128, 128], F32)
    idb = A("idb", [128, 128], BF16)
    b_gt, b_idf, b_idb = S.buf("gt"), S.buf("idf"), S.buf("idb")
    S.dma("sp", gt[:], gnorm_d, writes=[b_gt])
    S.dma("sp", idf[:], ident_d, writes=[b_idf])
    S.op("dve", lambda e: e.tensor_copy(out=idb[:], in_=idf[:]), reads=[b_idf], writes=[b_idb])
    return gt, b_gt, idb, b_idb


FBLK = 256


def phase_F(nc, S, io, ntok, final_norm):
    x_in, gnorm, ident_d, wg_d, wu_d, wd_d, x_out = (io[k] for k in ("x_in", "gnorm", "ident", "wg", "wu", "wd", "x_out"))
    gfin_d = io.get("gfin")
    A = nc.alloc_sbuf_tensor
    gt, b_gt, idb, b_idb = load_consts_common(S, nc, gnorm, ident_d)
    if final_norm:
        gf = A("gf", [128, D], F32)
        b_gf = S.buf("gf")
        S.dma("sp", gf[:], gfin_d, writes=[b_gf])
    wg, b_wg, stg = load_weight_bf16(S, nc, wg_d, D, FF, "wg", "wg", ceng=("pool", "act"))
    wu, b_wu, stg = load_weight_bf16(S, nc, wu_d, D, FF, "wu", "wu", ceng=("pool", "act"), stage=stg)
    wd, b_wd, stg = load_weight_bf16(S, nc, wd_d, FF, D, "wd", "wd", ceng=("pool", "act"), stage=stg)
    NF = FF // 128
    nt = FBLK // 128
    xt = [A("xt%d" % i, [128, D], F32) for i in range(2 * nt)]
    b_xt = [S.buf("xt%d" % i) for i in range(2 * nt)]
    hT = A("hT", [128, 8, FBLK], BF16)
    b_hT = S.buf("hT")
    aT = A("aT", [128, NF, FBLK], BF16)
    b_aT = S.buf("aT")
    sg = [A("sg%d" % i, [128, FBLK], F32) for i in range(2)]
    b_sg = [S.buf("sg0"), S.buf("sg1")]
    ot = [A("ot%d" % i, [128, D], F32) for i in range(2)]
    b_ot = [S.buf("ot0"), S.buf("ot1")]
    norm = NormT(S, nc, gt, b_gt, idb, b_idb, "n")
    gp = [nc.alloc_psum_tensor("gp%d" % i, [128, 512], F32) for i in range(4)]
    b_gp = [S.buf("gp%d" % i) for i in range(4)]
    dp = [nc.alloc_psum_tensor("dp%d" % i, [128, 512], F32) for i in range(2)]
    b_dp = [S.buf("dp0"), S.buf("dp1")]
    oc = 0
    for blk in range(ntok // FBLK):
        par = (blk % 2) * nt
        for t in range(nt):
            tok0 = blk * FBLK + t * 128
            S.dma("sp", xt[par + t][:], x_in[tok0:tok0 + 128, :], writes=[b_xt[par + t]])
            norm(xt[par + t][:], b_xt[par + t], hT[:, :, t * 128:(t + 1) * 128], b_hT)
        for f in range(NF):
            g_i, u_i = (2 * f) % 4, (2 * f + 1) % 4
            for k in range(8):
                S.op("pe", lambda e, f=f, k=k, g_i=g_i: e.matmul(gp[g_i][:, 0:FBLK], lhsT=wg[:, k, f * 128:(f + 1) * 128], rhs=hT[:, k, :],
                                                                start=(k == 0), stop=(k == 7)),
                     reads=[b_wg, b_hT], writes=[b_gp[g_i]])
            for k in range(8):
                S.op("pe", lambda e, f=f, k=k, u_i=u_i: e.matmul(gp[u_i][:, 0:FBLK], lhsT=wu[:, k, f * 128:(f + 1) * 128], rhs=hT[:, k, :],
                                                                start=(k == 0), stop=(k == 7)),
                     reads=[b_wu, b_hT], writes=[b_gp[u_i]])
            si = f % 2
            S.op("act", lambda e, g_i=g_i, si=si: e.activation(out=sg[si][:], in_=gp[g_i][:, 0:FBLK], func=AF.Silu),
                 reads=[b_gp[g_i]], writes=[b_sg[si]])
            S.op("dve", lambda e, f=f, u_i=u_i, si=si: e.tensor_tensor(out=aT[:, f, :], in0=sg[si][:], in1=gp[u_i][:, 0:FBLK], op=ALU.mult),
                 reads=[b_sg[si], b_gp[u_i]], writes=[b_aT])
        for t in range(nt):
            tok0 = blk * FBLK + t * 128
            for hf in range(2):
                for f in range(NF):
                    S.op("pe", lambda e, f=f, hf=hf, t=t: e.matmul(dp[hf][:], lhsT=aT[:, f, t * 128:(t + 1) * 128], rhs=wd[:, f, hf * 512:(hf + 1) * 512],
                                                                  start=(f == 0), stop=(f == NF - 1)),
                         reads=[b_aT, b_wd], writes=[b_dp[hf]])
            o = oc % 2
            oc += 1
            for hf in range(2):
                S.op("dve", lambda e, hf=hf, o=o, t=t, par=par: e.tensor_tensor(out=ot[o][:, hf * 512:(hf + 1) * 512], in0=dp[hf][:],
                                                                               in1=xt[par + t][:, hf * 512:(hf + 1) * 512], op=ALU.add),
                     reads=[b_dp[hf], b_xt[par + t]], writes=[b_ot[o]])
            if final_norm:
                rs = norm.rstd(ot[o][:], b_ot[o], o)
                S.op("dve", lambda e, o=o, rs=rs: e.scalar_tensor_tensor(out=ot[o][:], in0=ot[o][:], scalar=rs[:, 0:1], in1=gf[:],
                                                                        op0=ALU.mult, op1=ALU.mult),
                     reads=[b_ot[o], norm.b_s[o], b_gf], writes=[b_ot[o]])
            S.dma("sp", x_out[tok0:tok0 + 128, :], ot[o][:], reads=[b_ot[o]], is_output=True)


def build_F(final_norm):
    nc = bass.Bass("TRN2", target_bir_lowering=False)
    S = Sched(nc)
    DT = nc.dram_tensor
    io = {"x_in": DT("x_in", [TOK, D], F32, kind="ExternalInput").ap(),
          "gnorm": DT("gnorm", [128, D], F32, kind="ExternalInput").ap(),
          "ident": DT("ident", [128, 128], F32, kind="ExternalInput").ap(),
          "wg": DT("wg", [D, FF], F32, kind="ExternalInput").ap(),
          "wu": DT("wu", [D, FF], F32, kind="ExternalInput").ap(),
          "wd": DT("wd", [FF, D], F32, kind="ExternalInput").ap()}
    if final_norm:
        io["gfin"] = DT("gfin", [128, D], F32, kind="ExternalInput").ap()
    io["x_out"] = DT("x_out", [TOK, D], F32, kind="ExternalOutput").ap()
    phase_F(nc, S, io, TOK, final_norm)
    S.finish()
    return nc


def rep_rows(v):
    return np.ascontiguousarray(np.broadcast_to(np.asarray(v, np.float32)[None, :], (128, v.shape[0])))


def run_F(inp, layer, x_tok, final_norm):
    nc = build_F(final_norm)
    ident = np.eye(128, dtype=np.float32)
    base = {"gnorm": rep_rows(inp["norm_ffn"][layer]), "ident": ident,
            "wg": np.ascontiguousarray(inp["ffn_w_gate"][layer]), "wu": np.ascontiguousarray(inp["ffn_w_up"][layer]),
            "wd": np.ascontiguousarray(inp["ffn_w_down"][layer])}
    if final_norm:
        base["gfin"] = rep_rows(inp["norm_final"])
    in_maps = [dict(base, x_in=np.ascontiguousarray(x_tok[c])) for c in range(NCORES)]
    res = run_bass_kernel_spmd(nc, in_maps, core_ids=list(range(NCORES)))
    return [r["x_out"] for r in res.results]


def phase_C(nc, S, io, ntok):
    x_in, yT, sT, x0T, skip_d, wo_d, bo_d, x_out = (io[k] for k in ("x_in", "yT", "sT", "x0T", "skip", "w_out", "b_out", "x_out"))
    A = nc.alloc_sbuf_tensor
    wo, b_wo, _ = load_weight_bf16(S, nc, wo_d, D, D, "wo", "wo")
    skip = A("skip_sb", [128, 8], F32)
    bof = A("bof", [1, D], F32)
    bob = A("bob", [1, D], BF16)
    onesb = A("onesb", [1, 128], BF16)
    b_c = S.buf("c")
    S.dma("sp", skip[:], skip_d, writes=[b_c], sem_buf=b_c)
    S.dma("sp", bof[:], bo_d, writes=[b_c], sem_buf=b_c)
    b_c2 = S.buf("c2")
    S.op("dve", lambda e: e.tensor_copy(out=bob[:], in_=bof[:]), reads=[b_c], writes=[b_c2])
    S.op("pool", lambda e: e.memset(onesb[:], 1.0), writes=[b_c2])
    BL = 512
    yt = [A("yt%d" % i, [128, BL], F32) for i in range(2)]
    st_ = [A("st%d" % i, [128, BL], F32) for i in range(2)]
    x0t = [A("x0t%d" % i, [128, BL], F32) for i in range(2)]
    b_in = [S.buf("in0"), S.buf("in1")]
    tmp = [A("tmp%d" % i, [128, BL], F32) for i in range(2)]
    b_tmp = [S.buf("tmp0"), S.buf("tmp1")]
    uT = [A("uT%d" % i, [128, 8, BL], BF16) for i in range(2)]
    b_uT = [S.buf("uT0"), S.buf("uT1")]
    xt = [A("xt%d" % i, [128, D], F32) for i in range(2)]
    b_xt = [S.buf("xt0"), S.buf("xt1")]
    ot = [A("ot%d" % i, [128, D], F32) for i in range(2)]
    b_ot = [S.buf("ot0"), S.buf("ot1")]
    mp = [nc.alloc_psum_tensor("mp%d" % i, [128, 512], F32) for i in range(4)]
    b_mp = [S.buf("mp%d" % i) for i in range(4)]
    n = 0
    tc_ = 0
    for blk in range(ntok // BL):
        ub = blk % 2
        cs = slice(blk * BL, (blk + 1) * BL)
        for k in range(8):
            i = n % 2
            n += 1
            rs_ = slice(k * 128, (k + 1) * 128)
            S.dma("sp", yt[i][:], yT[rs_, cs], writes=[b_in[i]], sem_buf=b_in[i])
            S.dma("sp", st_[i][:], sT[rs_, cs], writes=[b_in[i]], sem_buf=b_in[i])
            S.dma("sp", x0t[i][:], x0T[rs_, cs], writes=[b_in[i]], sem_buf=b_in[i])
            S.op("dve", lambda e, i=i, k=k: e.scalar_tensor_tensor(out=tmp[i][:], in0=st_[i][:], scalar=skip[:, k:k + 1], in1=yt[i][:],
                                                                  op0=ALU.mult, op1=ALU.add),
                 reads=[b_in[i], b_c], writes=[b_tmp[i]])
            S.op("pool", lambda e, i=i, k=k, ub=ub: e.tensor_tensor(out=uT[ub][:, k, :], in0=tmp[i][:], in1=x0t[i][:], op=ALU.mult),
                 reads=[b_tmp[i], b_in[i]], writes=[b_uT[ub]])
        for t in range(BL // 128):
            tok0 = blk * BL + t * 128
            j = tc_ % 2
            tc_ += 1
            S.dma("sp", xt[j][:], x_in[tok0:tok0 + 128, :], writes=[b_xt[j]])
            for hf in range(2):
                m = 2 * j + hf
                for k in range(8):
                    S.op("pe", lambda e, m=m, k=k, hf=hf, t=t, ub=ub: e.matmul(mp[m][:], lhsT=uT[ub][:, k, t * 128:(t + 1) * 128],
                                                                              rhs=wo[:, k, hf * 512:(hf + 1) * 512], start=(k == 0), stop=False),
                         reads=[b_uT[ub], b_wo], writes=[b_mp[m]])
                S.op("pe", lambda e, m=m, hf=hf: e.matmul(mp[m][:], lhsT=onesb[:], rhs=bob[:, hf * 512:(hf + 1) * 512], start=False, stop=True),
                     reads=[b_c2], writes=[b_mp[m]])
                S.op("dve", lambda e, m=m, hf=hf, j=j: e.tensor_tensor(out=ot[j][:, hf * 512:(hf + 1) * 512], in0=mp[m][:],
                                                                      in1=xt[j][:, hf * 512:(hf + 1) * 512], op=ALU.add),
                     reads=[b_mp[m], b_xt[j]], writes=[b_ot[j]])
            S.dma("sp", x_out[tok0:tok0 + 128, :], ot[j][:], reads=[b_ot[j]], is_output=True)


def build_C():
    nc = bass.Bass("TRN2", target_bir_lowering=False)
    S = Sched(nc)
    DT = nc.dram_tensor
    io = {"x_in": DT("x_in", [TOK, D], F32, kind="ExternalInput").ap(),
          "yT": DT("yT", [D, TOK], F32, kind="ExternalInput").ap(),
          "sT": DT("sT", [D, TOK], F32, kind="ExternalInput").ap(),
          "x0T": DT("x0T", [D, TOK], F32, kind="ExternalInput").ap(),
          "skip": DT("skip", [128, 8], F32, kind="ExternalInput").ap(),
          "w_out": DT("w_out", [D, D], F32, kind="ExternalInput").ap(),
          "b_out": DT("b_out", [1, D], F32, kind="ExternalInput").ap(),
          "x_out": DT("x_out", [TOK, D], F32, kind="ExternalOutput").ap()}
    phase_C(nc, S, io, TOK)
    S.finish()
    return nc


def run_C(inp, x_tok, yT, sT, x0T):
    nc = build_C()
    base = {"skip": np.ascontiguousarray(inp["hy_skip"][0].reshape(8, 128).T), "w_out": np.ascontiguousarray(inp["hy_w_out"][0]),
            "b_out": np.ascontiguousarray(inp["hy_b_out"][0][None, :])}
    in_maps = [dict(base, x_in=np.ascontiguousarray(x_tok[c]), yT=np.ascontiguousarray(yT[c]), sT=np.ascontiguousarray(sT[c]),
                    x0T=np.ascontiguousarray(x0T[c])) for c in range(NCORES)]
    res = run_bass_kernel_spmd(nc, in_maps, core_ids=list(range(NCORES)))
    return [r["x_out"] for r in res.results]


NROWS_LOC = 72
NEG = -30000.0


def na_tables(rpb):
    H = 16
    j = np.arange(64)
    w = np.arange(64)
    cs = np.clip(w - 8, 0, 48)
    colok = (j[:, None] >= cs[None, :]) & (j[:, None] < cs[None, :] + 16)
    coff = np.clip(j[:, None] - w[None, :] + 15, 0, 30)
    out = np.full((2, 64, H, 7, 2, 64), NEG, np.float32)
    for idx in range(7):
        d0 = -6 + 2 * idx
        for i2 in range(2):
            for q2 in range(2):
                dl = d0 + i2 - q2
                if abs(dl) > 7:
                    continue
                g = rpb[:, dl + 7, :][:, coff]
                g = np.where(colok[None], g, np.float32(NEG))
                out[i2, :, :, idx, q2, :] = g.transpose(1, 0, 2)
    return np.ascontiguousarray(out.reshape(128, H, 7, 128))


def na_mlist(p):
    if p == 0:
        return list(range(0, 6))
    if p == 31:
        return list(range(-1, 5))
    return list(range(0, 5))


def na_rowmask(q):
    R0 = 64 * q
    m = np.zeros((128, 32, 6, 2), np.float32)
    for p in range(32):
        for mi, mm in enumerate(na_mlist(p)):
            for i2 in range(2):
                for q2 in range(2):
                    gr = R0 + 2 * p + q2
                    rs = min(max(gr - 4, 0), 248)
                    kr = R0 - 4 + 2 * p + 2 * mm + i2
                    if rs <= kr < rs + 8:
                        m[i2 * 64:(i2 + 1) * 64, p, mi, q2] = 1.0
    return m


def phase_D(nc, S, io):
    xe, gnorm, ident_d, wqkv_d, bqk_d, bv_d, wo_d, bo_d, bt_d, rm_d, x_out = (io[k] for k in (
        "xe", "gnorm", "ident", "w_qkv", "b_qk", "b_v", "w_o", "b_o", "bt", "rowmask", "x_out"))
    A = nc.alloc_sbuf_tensor
    B = S.buf
    gt, b_gt, idb, b_idb = load_consts_common(S, nc, gnorm, ident_d)
    st = [A("wst%d" % i, [128, 1024], F32) for i in range(2)]
    stage = (st, [B("wst0"), B("wst1")], [0])
    wqkv, b_wqkv, stage = load_weight_bf16(S, nc, wqkv_d, D, 3 * D, "wqkv", "wqkv", ceng=("pool", "act"), stage=stage)
    wo, b_wo, stage = load_weight_bf16(S, nc, wo_d, D, D, "wo", "wo", ceng=("pool", "act"), stage=stage)
    BTb = A("BTb", [128, 16, 7, 128], BF16)
    b_BT = B("BT")
    st_, b_st, ctr = stage
    for h in range(16):
        i = ctr[0] % 2
        ctr[0] += 1
        S.dma("pool", st_[i][:, 0:896], bt_d[:, h].rearrange("p a b -> p (a b)"), writes=[b_st[i]])
        S.op("pool", lambda e, i=i, h=h: e.tensor_copy(out=BTb[:, h].rearrange("p a b -> p (a b)"), in_=st_[i][:, 0:896]),
             reads=[b_st[i]], writes=[b_BT])
    rmask = A("rmask", [128, 32, 6, 2], F32)
    bqk = A("bqk", [128, 16], F32)
    bq8 = A("bq8", [128, 8], F32)
    bvb = A("bvb", [1, D], BF16)
    bob = A("bob", [1, D], BF16)
    onesr = A("onesr", [1, 128], BF16)
    onesc = A("onesc", [128, 64], BF16)
    b_c, b_c2 = B("c"), B("c2")
    for dst, src in ((rmask, rm_d), (bqk, bqk_d)):
        S.dma("sp", dst[:], src, writes=[b_c], sem_buf=b_c)
    S.op("dve", lambda e: e.tensor_scalar(out=bq8[:], in0=bqk[:, 0:8], scalar1=0.125, scalar2=None, op0=ALU.mult), reads=[b_c], writes=[b_c2])
    for dstb, src in ((bvb, bv_d), (bob, bo_d)):
        i = ctr[0] % 2
        ctr[0] += 1
        S.dma("pool", st_[i][0:1, :], src, writes=[b_st[i]])
        S.op("pool", lambda e, i=i, dstb=dstb: e.tensor_copy(out=dstb[:], in_=st_[i][0:1, :]), reads=[b_st[i]], writes=[b_c2])
    S.op("pool", lambda e: e.memset(onesr[:], 1.0), writes=[b_c2])
    S.op("pool", lambda e: e.memset(onesc[:], 1.0), writes=[b_c2])

    KT = [A("KT%d" % i, [128, 8, 512], BF16) for i in range(2)]
    VV = [A("VV%d" % i, [128, 4, D], BF16) for i in range(2)]
    QQ = [A("QQ%d" % i, [128, 8, 512], BF16) for i in range(2)]
    b_KT, b_VV, b_QQ = [B("KT0"), B("KT1")], [B("VV0"), B("VV1")], [B("QQ0"), B("QQ1")]
    hT = A("hT", [128, 8, 512], BF16)
    b_hT = B("hT")
    aT = A("aT", [128, 8, 512], BF16)
    b_aT = B("aT")
    xt = [A("xt%d" % i, [128, D], F32) for i in range(2)]
    b_xt = [B("xt0"), B("xt1")]
    ot = [A("ot0", [128, D], F32)]
    b_ot = [B("ot0")]
    Sb = [A("Sb%d" % i, [128, 512], F32) for i in range(2)]
    Eb = [A("Eb%d" % i, [128, 512], BF16) for i in range(2)]
    b_Sb, b_Eb = [B("Sb0"), B("Sb1")], [B("Eb0"), B("Eb1")]
    rz = [A("rz%d" % i, [128, 128], F32) for i in range(2)]
    b_rz = [B("rz0"), B("rz1")]
    norm = NormT(S, nc, gt, b_gt, idb, b_idb, "n", ntp=1)
    pp = [nc.alloc_psum_tensor("pp%d" % i, [128, 512], F32) for i in range(2)]
    b_pp = [B("pp0"), B("pp1")]
    sp_ = [nc.alloc_psum_tensor("sps%d" % i, [128, 512], F32) for i in range(3)]
    b_sp = [B("sps%d" % i) for i in range(3)]
    obank = nc.alloc_psum_tensor("obank", [128, 512], F32)
    zbank = nc.alloc_psum_tensor("zbank", [128, 512], F32)
    b_oz = [B("oz0"), B("oz1")]
    cnt = {"pp": 0, "sp": 0, "e": 0, "oz": 0, "x": 0, "o": 0}

    def nxt(k, n):
        v = cnt[k] % n
        cnt[k] += 1
        return v

    def project(kb):
        rb = kb % 2
        for t in range(4):
            j = nxt("x", 2)
            tok0 = kb * 512 + t * 128
            S.dma("sp", xt[j][:], xe[tok0:tok0 + 128, :], writes=[b_xt[j]])
            norm(xt[j][:], b_xt[j], hT[:, :, t * 128:(t + 1) * 128], b_hT)
        for c in range(8):
            pi = nxt("pp", 2)
            for k in range(8):
                S.op("pe", lambda e, pi=pi, k=k, c=c: e.matmul(pp[pi][:], lhsT=wqkv[:, k, D + c * 128:D + (c + 1) * 128], rhs=hT[:, k, :],
                                                              start=(k == 0), stop=(k == 7)),
                     reads=[b_wqkv, b_hT], writes=[b_pp[pi]])
            S.op("act", lambda e, pi=pi, c=c, rb=rb: e.activation(out=KT[rb][:, c, :], in_=pp[pi][:], func=AF.Identity,
                                                                 bias=bqk[:, 8 + c:9 + c], scale=1.0),
                 reads=[b_pp[pi], b_c], writes=[b_KT[rb]])
            pi = nxt("pp", 2)
            for k in range(8):
                S.op("pe", lambda e, pi=pi, k=k, c=c: e.matmul(pp[pi][:], lhsT=wqkv[:, k, c * 128:(c + 1) * 128], rhs=hT[:, k, :],
                                                              start=(k == 0), stop=(k == 7)),
                     reads=[b_wqkv, b_hT], writes=[b_pp[pi]])
            if kb >= 1:
                S.op("act", lambda e, pi=pi, c=c, kb=kb: e.activation(out=QQ[(kb - 1) % 2][:, c, 256:512], in_=pp[pi][:, 0:256], func=AF.Identity,
                                                                     bias=bq8[:, c:c + 1], scale=0.125),
                     reads=[b_pp[pi], b_c2], writes=[b_QQ[(kb - 1) % 2]])
            if kb <= 7:
                S.op("act", lambda e, pi=pi, c=c, kb=kb: e.activation(out=QQ[kb % 2][:, c, 0:256], in_=pp[pi][:, 256:512], func=AF.Identity,
                                                                     bias=bq8[:, c:c + 1], scale=0.125),
                     reads=[b_pp[pi], b_c2], writes=[b_QQ[kb % 2]])
        for t in range(4):
            for hf in range(2):
                pi = nxt("pp", 2)
                for k in range(8):
                    S.op("pe", lambda e, pi=pi, k=k, t=t, hf=hf: e.matmul(pp[pi][:], lhsT=hT[:, k, t * 128:(t + 1) * 128],
                                                                         rhs=wqkv[:, k, 2 * D + hf * 512:2 * D + (hf + 1) * 512],
                                                                         start=(k == 0), stop=False),
                         reads=[b_wqkv, b_hT], writes=[b_pp[pi]])
                S.op("pe", lambda e, pi=pi, hf=hf: e.matmul(pp[pi][:], lhsT=onesr[:], rhs=bvb[:, hf * 512:(hf + 1) * 512], start=False, stop=True),
                     reads=[b_c2], writes=[b_pp[pi]])
                S.op("act", lambda e, pi=pi, t=t, hf=hf, rb=rb: e.copy(out=VV[rb][:, t, hf * 512:(hf + 1) * 512], in_=pp[pi][:]),
                     reads=[b_pp[pi]], writes=[b_VV[rb]])

    def attend(bq):
        qb = bq % 2
        for pl in range(4):
            p = 4 * bq + pl
            ml = na_mlist(p)
            groups = [ml[0:4], ml[4:]]
            for c in range(8):
                o_i = nxt("oz", 2)
                first = {0: True, 1: True}
                nmm = len(ml)
                for hp in range(2):
                    h = 2 * c + hp
                    hs = slice(hp * 64, (hp + 1) * 64)
                    done = 0
                    for gi, grp in enumerate(groups):
                        si = nxt("sp", 3)
                        ncol = len(grp) * 128
                        for mi, mm in enumerate(grp):
                            blk, tl = bq + (pl + mm) // 4, (pl + mm) % 4
                            S.op("pe", lambda e, si=si, mi=mi, blk=blk, tl=tl, hs=hs, c=c, pl=pl: e.matmul(
                                sp_[si][:, mi * 128:(mi + 1) * 128], lhsT=KT[blk % 2][hs, c, tl * 128:(tl + 1) * 128],
                                rhs=QQ[qb][hs, c, pl * 128:(pl + 1) * 128], start=True, stop=True),
                                 reads=[b_KT[blk % 2], b_QQ[qb]], writes=[b_sp[si]])
                        ei = nxt("e", 2)
                        i0 = grp[0] + 1
                        mi0 = 4 * gi
                        S.op("dve", lambda e, si=si, ei=ei, ncol=ncol, h=h, i0=i0, n=len(grp): e.tensor_tensor(
                            out=Sb[ei][:, 0:ncol], in0=sp_[si][:, 0:ncol], in1=BTb[:, h, i0:i0 + n, :].rearrange("p a b -> p (a b)"), op=ALU.add),
                             reads=[b_sp[si], b_BT], writes=[b_Sb[ei]])
                        S.op("act", lambda e, ei=ei, ncol=ncol: e.activation(out=Eb[ei][:, 0:ncol], in_=Sb[ei][:, 0:ncol], func=AF.Exp),
                             reads=[b_Sb[ei]], writes=[b_Eb[ei]])
                        S.op("dve", lambda e, ei=ei, ncol=ncol, p=p, mi0=mi0, n=len(grp): e.tensor_tensor(
                            out=Eb[ei][:, 0:ncol].rearrange("p (a q w) -> p a q w", q=2, w=64),
                            in0=Eb[ei][:, 0:ncol].rearrange("p (a q w) -> p a q w", q=2, w=64),
                            in1=rmask[:, p, mi0:mi0 + n, :].unsqueeze(3).broadcast_to([128, n, 2, 64]), op=ALU.mult),
                             reads=[b_Eb[ei], b_c], writes=[b_Eb[ei]])
                        for mi, mm in enumerate(grp):
                            blk, tl = bq + (pl + mm) // 4, (pl + mm) % 4
                            done += 1
                            S.op("pe", lambda e, o_i=o_i, ei=ei, mi=mi, blk=blk, tl=tl, hs=hs, h=h, st=(done == 1), sp=(done == nmm): e.matmul(
                                obank[hs, o_i * 128:(o_i + 1) * 128], lhsT=VV[blk % 2][:, tl, h * 64:(h + 1) * 64], rhs=Eb[ei][:, mi * 128:(mi + 1) * 128],
                                start=st, stop=sp),
                                 reads=[b_VV[blk % 2], b_Eb[ei]], writes=[b_oz[o_i]])
                            S.op("pe", lambda e, o_i=o_i, ei=ei, mi=mi, hs=hs, st=(done == 1), sp=(done == nmm): e.matmul(
                                zbank[hs, o_i * 128:(o_i + 1) * 128], lhsT=onesc[:], rhs=Eb[ei][:, mi * 128:(mi + 1) * 128], start=st, stop=sp),
                                 reads=[b_c2, b_Eb[ei]], writes=[b_oz[o_i]])
                S.op("dve", lambda e, o_i=o_i: e.reciprocal(out=rz[o_i][:], in_=zbank[:, o_i * 128:(o_i + 1) * 128]), reads=[b_oz[o_i]], writes=[b_rz[o_i]])
                S.op("dve", lambda e, o_i=o_i, c=c, pl=pl: e.tensor_tensor(out=aT[:, c, pl * 128:(pl + 1) * 128], in0=obank[:, o_i * 128:(o_i + 1) * 128],
                                                                          in1=rz[o_i][:], op=ALU.mult),
                     reads=[b_oz[o_i], b_rz[o_i]], writes=[b_aT])
        for t in range(4):
            j = nxt("x", 2)
            tok_e = (4 + 8 * bq) * 64 + t * 128
            tok_o = bq * 512 + t * 128
            S.dma("sp", xt[j][:], xe[tok_e:tok_e + 128, :], writes=[b_xt[j]])
            o = 0
            for hf in range(2):
                pi = nxt("pp", 2)
                for k in range(8):
                    S.op("pe", lambda e, pi=pi, k=k, t=t, hf=hf: e.matmul(pp[pi][:], lhsT=aT[:, k, t * 128:(t + 1) * 128],
                                                                         rhs=wo[:, k, hf * 512:(hf + 1) * 512], start=(k == 0), stop=False),
                         reads=[b_aT, b_wo], writes=[b_pp[pi]])
                S.op("pe", lambda e, pi=pi, hf=hf: e.matmul(pp[pi][:], lhsT=onesr[:], rhs=bob[:, hf * 512:(hf + 1) * 512], start=False, stop=True),
                     reads=[b_c2], writes=[b_pp[pi]])
                S.op("dve", lambda e, pi=pi, hf=hf, j=j, o=o: e.tensor_tensor(out=ot[o][:, hf * 512:(hf + 1) * 512], in0=pp[pi][:],
                                                                             in1=xt[j][:, hf * 512:(hf + 1) * 512], op=ALU.add),
                     reads=[b_pp[pi], b_xt[j]], writes=[b_ot[o]])
            S.dma("sp", x_out[tok_o:tok_o + 128, :], ot[o][:], reads=[b_ot[o]], is_output=True)

    for kb in range(9):
        project(kb)
        if kb >= 1:
            attend(kb - 1)


def d_io(nc, ident=None):
    DT = nc.dram_tensor
    return {"gnorm": DT("d_gnorm", [128, D], F32, kind="ExternalInput").ap(),
            "ident": ident if ident is not None else DT("ident", [128, 128], F32, kind="ExternalInput").ap(),
            "w_qkv": DT("w_qkv", [D, 3 * D], F32, kind="ExternalInput").ap(),
            "b_qk": DT("b_qk", [128, 16], F32, kind="ExternalInput").ap(),
            "b_v": DT("b_v", [1, D], F32, kind="ExternalInput").ap(),
            "w_o": DT("w_o", [D, D], F32, kind="ExternalInput").ap(),
            "b_o": DT("b_o", [1, D], F32, kind="ExternalInput").ap(),
            "bt": DT("bt", [128, 16, 7, 128], F32, kind="ExternalInput").ap(),
            "rowmask": DT("rowmask", [128, 32, 6, 2], F32, kind="ExternalInput").ap()}


def build_D():
    nc = bass.Bass("TRN2", target_bir_lowering=False)
    S = Sched(nc)
    io = d_io(nc)
    io["xe"] = nc.dram_tensor("xe", [NROWS_LOC * 64, D], F32, kind="ExternalInput").ap()
    io["x_out"] = nc.dram_tensor("x_out", [TOK, D], F32, kind="ExternalOutput").ap()
    phase_D(nc, S, io)
    S.finish()
    return nc


def run_D(inp, xb_full):
    nc = build_D()
    ident = np.eye(128, dtype=np.float32)
    bq = inp["na_b_qkv"][0]
    bqk = np.concatenate([bq[0:D].reshape(8, 128).T, bq[D:2 * D].reshape(8, 128).T], 1).astype(np.float32)
    base = {"d_gnorm": rep_rows(inp["norm_mix"][1]), "ident": ident, "w_qkv": np.ascontiguousarray(inp["na_w_qkv"][0]),
            "b_qk": np.ascontiguousarray(bqk), "b_v": np.ascontiguousarray(bq[2 * D:][None, :]),
            "w_o": np.ascontiguousarray(inp["na_w_o"][0]), "b_o": np.ascontiguousarray(inp["na_b_o"][0][None, :]),
            "bt": na_tables(np.asarray(inp["na_rpb"][0], np.float32))}
    in_maps = []
    for c in range(NCORES):
        b, q = divmod(c, 4)
        xe = np.zeros((NROWS_LOC * 64, D), np.float32)
        g0 = (64 * q - 4) * 64
        lo, hi = max(g0, 0), min(g0 + NROWS_LOC * 64, SEQ)
        xe[lo - g0:hi - g0] = xb_full[b, lo:hi]
        in_maps.append(dict(base, xe=xe, rowmask=na_rowmask(q)))
    res = run_bass_kernel_spmd(nc, in_maps, core_ids=list(range(NCORES)))
    return [r["x_out"] for r in res.results]


def _tok_shards(a):
    return [np.ascontiguousarray(a[c // 4, (c % 4) * TOK:(c % 4 + 1) * TOK]) for c in range(NCORES)]


def _assemble(shards):
    out = np.empty((BATCH, SEQ, D), np.float32)
    for c in range(NCORES):
        out[c // 4, (c % 4) * TOK:(c % 4 + 1) * TOK] = shards[c]
    return out


NEXT = NROWS_LOC * 64


def build_L2():
    nc0 = bass.Bass("TRN2", target_bir_lowering=False)
    S = Sched(nc0)
    nc = NCP(nc0)
    DT = nc0.dram_tensor
    ext = lambda name, shape: DT(name, shape, F32, kind="ExternalInput").ap()
    ident = ext("ident", [128, 128])
    xa_s = DT("xa_scr", [NEXT, D], F32).ap()
    xb_s = DT("xb_scr", [NEXT, D], F32).ap()
    xc_s = DT("xc_scr", [TOK, D], F32).ap()
    out = DT("out", [TOK, D], F32, kind="ExternalOutput").ap()
    ioC = {"x_in": ext("x_ext", [NEXT, D]), "yT": ext("yT", [D, NEXT]), "sT": ext("sT", [D, NEXT]), "x0T": ext("x0T", [D, NEXT]),
           "skip": ext("skip", [128, 8]), "w_out": ext("w_out", [D, D]), "b_out": ext("b_out", [1, D]), "x_out": xa_s}
    ioF0 = {"x_in": xa_s, "gnorm": ext("gn_f0", [128, D]), "ident": ident, "wg": ext("wg0", [D, FF]), "wu": ext("wu0", [D, FF]),
            "wd": ext("wd0", [FF, D]), "x_out": xb_s}
    ioD = d_io(nc0, ident)
    ioD["xe"] = xb_s
    ioD["x_out"] = xc_s
    ioF1 = {"x_in": xc_s, "gnorm": ext("gn_f1", [128, D]), "ident": ident, "wg": ext("wg1", [D, FF]), "wu": ext("wu1", [D, FF]),
            "wd": ext("wd1", [FF, D]), "gfin": ext("gfin", [128, D]), "x_out": out}
    S.pfx = "c_"
    phase_C(nc, S, ioC, NEXT)
    S.barrier()
    nc.reset()
    S.pfx = "f0_"
    phase_F(nc, S, ioF0, NEXT, False)
    S.barrier()
    nc.reset()
    S.pfx = "d_"
    phase_D(nc, S, ioD)
    S.barrier()
    nc.reset()
    S.pfx = "f1_"
    phase_F(nc, S, ioF1, TOK, True)
    S.finish()
    return nc0


def _ext_tok(a, c):
    b, q = divmod(c, 4)
    g0 = q * TOK - 256
    out = np.zeros((NEXT,) + a.shape[2:], np.float32)
    lo, hi = max(g0, 0), min(g0 + NEXT, SEQ)
    out[lo - g0:hi - g0] = a[b, lo:hi]
    return out


def run_L2(inp, x, y_full, s_full, x0_full):
    nc = build_L2()
    bq = inp["na_b_qkv"][0]
    bqk = np.concatenate([bq[0:D].reshape(8, 128).T, bq[D:2 * D].reshape(8, 128).T], 1).astype(np.float32)
    base = {"ident": np.eye(128, dtype=np.float32),
            "skip": np.ascontiguousarray(inp["hy_skip"][0].reshape(8, 128).T), "w_out": np.ascontiguousarray(inp["hy_w_out"][0]),
            "b_out": np.ascontiguousarray(inp["hy_b_out"][0][None, :]),
            "gn_f0": rep_rows(inp["norm_ffn"][0]), "wg0": np.ascontiguousarray(inp["ffn_w_gate"][0]),
            "wu0": np.ascontiguousarray(inp["ffn_w_up"][0]), "wd0": np.ascontiguousarray(inp["ffn_w_down"][0]),
            "gn_f1": rep_rows(inp["norm_ffn"][1]), "wg1": np.ascontiguousarray(inp["ffn_w_gate"][1]),
            "wu1": np.ascontiguousarray(inp["ffn_w_up"][1]), "wd1": np.ascontiguousarray(inp["ffn_w_down"][1]),
            "gfin": rep_rows(inp["norm_final"]),
            "d_gnorm": rep_rows(inp["norm_mix"][1]), "w_qkv": np.ascontiguousarray(inp["na_w_qkv"][0]),
            "b_qk": np.ascontiguousarray(bqk), "b_v": np.ascontiguousarray(bq[2 * D:][None, :]),
            "w_o": np.ascontiguousarray(inp["na_w_o"][0]), "b_o": np.ascontiguousarray(inp["na_b_o"][0][None, :]),
            "bt": na_tables(np.asarray(inp["na_rpb"][0], np.float32))}
    in_maps = []
    for c in range(NCORES):
        m = dict(base)
        m["x_ext"] = _ext_tok(x, c)
        m["yT"] = np.ascontiguousarray(_ext_tok(y_full, c).T)
        m["sT"] = np.ascontiguousarray(_ext_tok(s_full, c).T)
        m["x0T"] = np.ascontiguousarray(_ext_tok(x0_full, c).T)
        m["rowmask"] = na_rowmask(c % 4)
        in_maps.append(m)
    res = run_bass_kernel_spmd(nc, in_maps, core_ids=list(range(NCORES)))
    return [r["out"] for r in res.results]


def kernel(**inp):
    inp = {k: np.asarray(v, dtype=np.float32) for k, v in inp.items()}
    x = inp["x"]
    resA = run_A(inp)
    sT = [r["sT"] for r in resA]
    x0T = [r["x0T"] for r in resA]
    s_cs = [np.empty((2, 128, SEQ), np.float32) for _ in range(NCORES)]
    for c in range(NCORES):
        b, q = divmod(c, 4)
        for cg in range(NCORES):
            s_cs[cg][b, :, q * TOK:(q + 1) * TOK] = sT[c][cg * 128:(cg + 1) * 128]
    resB = run_B(inp, s_cs)
    y_full = np.empty((BATCH, SEQ, D), np.float32)
    for cg in range(NCORES):
        y_full[:, :, cg * 128:(cg + 1) * 128] = resB[cg]["y_out"].transpose(0, 2, 1)
    s_full = _assemble([t.T for t in sT])
    x0_full = _assemble([t.T for t in x0T])
    out = run_L2(inp, x, y_full, s_full, x0_full)
    return _assemble(out)
```

```python
import math
import numpy as np
import ml_dtypes
import concourse.bass as bass
import concourse.mybir as mybir
from concourse.bass_utils import run_bass_kernel_spmd
import os

F32 = mybir.dt.float32
BF16 = mybir.dt.bfloat16
AF = mybir.ActivationFunctionType
ALU = mybir.AluOpType
AX = mybir.AxisListType

D = 1024
SEQ = 16384
BATCH = 2
NCORES = 8
TOK = 4096
FF = 2816
EPS = 1e-6


class Buf:
    __slots__ = ("name", "last_w", "readers", "sem", "dma_cnt")

    def __init__(self, name):
        self.name = name
        self.last_w = None
        self.readers = []
        self.sem = None
        self.dma_cnt = 0


class Ins:
    __slots__ = ("eng", "fn", "deps", "milestone", "ms", "dma_sem", "dma_val")

    def __init__(self, eng, fn):
        self.eng = eng
        self.fn = fn
        self.deps = []
        self.milestone = False
        self.ms = 0
        self.dma_sem = None
        self.dma_val = 0


class Sched:
    ENGS = ("pe", "act", "dve", "pool", "sp")

    def __init__(self, nc):
        self.nc = nc
        self.q = {e: [] for e in self.ENGS}
        self.esem = {e: nc.alloc_semaphore("prog_" + e) for e in self.ENGS}
        self.nbuf = 0
        self.out_events = []
        self.pfx = ""
        self.pending_barrier = {}
        self.all_dma = []

    def buf(self, name=None):
        self.nbuf += 1
        return Buf(self.pfx + (name or ("b%d" % self.nbuf)))

    def barrier(self):
        deps = [self.q[e][-1] for e in self.ENGS if self.q[e] and self.q[e][-1].fn is not None]
        last = {}
        for d in self.all_dma:
            last[id(d.dma_sem)] = d
        deps += list(last.values())
        self.pending_barrier = {e: list(deps) for e in self.ENGS}

    def _push(self, eng, ins):
        pb = self.pending_barrier.pop(eng, None)
        if pb:
            for d in pb:
                if d is not ins and d not in ins.deps:
                    ins.deps.append(d)
        self.q[eng].append(ins)

    def _collect(self, ins, reads, writes):
        deps = []
        for b in reads:
            if b.last_w is not None:
                deps.append(b.last_w)
        for b in writes:
            if b.last_w is not None:
                deps.append(b.last_w)
            deps.extend(b.readers)
        for d in deps:
            if d is ins:
                continue
            if d.dma_sem is None and d.eng == "pe" and ins.eng == "pe":
                continue
            ins.deps.append(d)
        for b in writes:
            b.last_w = ins
            b.readers = []
        for b in reads:
            if b not in writes:
                b.readers.append(ins)

    def op(self, eng, fn, reads=(), writes=()):
        ins = Ins(eng, fn)
        self._collect(ins, list(reads), list(writes))
        self._push(eng, ins)
        return ins

    def dma(self, eng, out_ap, in_ap, reads=(), writes=(), sem_buf=None, is_output=False, **kw):
        if sem_buf is None:
            sem_buf = (list(writes) + list(reads))[0]
        if sem_buf.sem is None:
            sem_buf.sem = self.nc.alloc_semaphore("dma_" + sem_buf.name)
        ins = Ins(eng, lambda e: e.dma_start(out=out_ap, in_=in_ap, **kw))
        self._collect(ins, list(reads), list(writes))
        sem_buf.dma_cnt += 16
        ins.dma_sem = sem_buf.sem
        ins.dma_val = sem_buf.dma_cnt
        self.all_dma.append(ins)
        self._push(eng, ins)
        if is_output:
            self.out_events.append(ins)
        return ins

    def coll(self, kind, out_ap, in_ap, reads=(), writes=(), sem_buf=None):
        if sem_buf is None:
            sem_buf = (list(writes) + list(reads))[0]
        if sem_buf.sem is None:
            sem_buf.sem = self.nc.alloc_semaphore("dma_" + sem_buf.name)
        groups = [list(range(NCORES))]
        ins = Ins("pool", lambda e: e.collective_compute(kind, ALU.bypass, replica_groups=groups, ins=[in_ap], outs=[out_ap]))
        self._collect(ins, list(reads), list(writes))
        sem_buf.dma_cnt += 16
        ins.dma_sem = sem_buf.sem
        ins.dma_val = sem_buf.dma_cnt
        self.all_dma.append(ins)
        self._push("pool", ins)
        return ins

    def finish(self):
        fin = Ins("sp", None)
        fin.deps = list(self.out_events)
        self.q["sp"].append(fin)
        for e in self.ENGS:
            for ins in self.q[e]:
                for d in ins.deps:
                    if d.dma_sem is None:
                        d.milestone = True
        for e in self.ENGS:
            c = 0
            for ins in self.q[e]:
                if ins.milestone:
                    c += 1
                    ins.ms = c
        nc = self.nc
        engobj = {"pe": "tensor", "act": "scalar", "dve": "vector", "pool": "gpsimd", "sp": "sync"}

        def emit(ename, e):
            waited = {}
            for ins in self.q[ename]:
                need = {}
                for d in ins.deps:
                    if d.dma_sem is not None:
                        s, v = d.dma_sem, d.dma_val
                    else:
                        s, v = self.esem[d.eng], d.ms
                    k = id(s)
                    if waited.get(k, 0) >= v:
                        continue
                    if k not in need or need[k][1] < v:
                        need[k] = (s, v)
                for k, (s, v) in need.items():
                    e.wait_ge(s, v)
                    waited[k] = v
                if ins.fn is None:
                    continue
                r = ins.fn(e)
                if ins.dma_sem is not None:
                    r.then_inc(ins.dma_sem, 16)
                elif ins.milestone:
                    r.then_inc(self.esem[ename], 1)

        with nc.Block() as block:
            for ename in self.ENGS:
                if not self.q[ename]:
                    continue
                getattr(block, engobj[ename])(lambda e, en=ename: emit(en, e))


ARENA_BYTES = 212800


class NCP:
    _DTB = None

    def __init__(self, nc):
        self._nc = nc
        self._arena = nc.alloc_sbuf_tensor("arena", [128, ARENA_BYTES // 4], F32)
        self._banks = [nc.alloc_psum_tensor("bank%d" % i, [128, 512], F32) for i in range(8)]
        self.reset()

    def reset(self):
        self._off = 0
        self._nbank = 0

    @staticmethod
    def _view(ap2d, shape, dt):
        if dt != F32:
            ap2d = ap2d.bitcast(dt)
        if len(shape) == 2:
            return ap2d
        names = " ".join("d%d" % i for i in range(1, len(shape)))
        kw = {"d%d" % i: shape[i] for i in range(1, len(shape))}
        return ap2d.rearrange("p (%s) -> p %s" % (names, names), **kw)

    def alloc_sbuf_tensor(self, name, shape, dt):
        esz = 2 if dt == BF16 else 4
        n = 1
        for d in shape[1:]:
            n *= d
        nbytes = (n * esz + 31) // 32 * 32
        assert self._off + nbytes <= ARENA_BYTES, "arena overflow at %s (%d + %d)" % (name, self._off, nbytes)
        o4 = self._off // 4
        self._off += nbytes
        return self._view(self._arena[0:shape[0], o4:o4 + (n * esz + 3) // 4], shape, dt)

    def alloc_psum_tensor(self, name, shape, dt):
        esz = 2 if dt == BF16 else 4
        n = 1
        for d in shape[1:]:
            n *= d
        assert n * esz <= 2048 and self._nbank < 8, "psum overflow at " + name
        bk = self._banks[self._nbank]
        self._nbank += 1
        return self._view(bk[0:shape[0], 0:(n * esz + 3) // 4], shape, dt)

    def __getattr__(self, k):
        return getattr(self._nc, k)


def _run(nc, in_maps, tag=""):
    tr = bool(os.environ.get("K_TRACE"))
    res = run_bass_kernel_spmd(nc, in_maps, core_ids=list(range(NCORES)), **({"trace": True} if tr else {}))
    if tr:
        print("K_TRACE", tag, "exec_time_ns", res.exec_time_ns, flush=True)
    return res


def run_pipeline(items, stages):
    n, m = len(items), len(stages)
    for t in range(n + m - 1):
        for k in range(m):
            i = t - k
            if 0 <= i < n:
                stages[k](items[i])


def bcast_rows(ap_row, nparts):
    return ap_row.broadcast(0, nparts) if hasattr(ap_row, "broadcast") else ap_row


def build_A():
    nc = bass.Bass("TRN2", target_bir_lowering=False)
    S = Sched(nc)
    NT = TOK // 128
    x_own = nc.dram_tensor("x_own", [TOK, D], F32, kind="ExternalInput").ap()
    x_halo = nc.dram_tensor("x_halo", [128, D], F32, kind="ExternalInput").ap()
    emask = nc.dram_tensor("emask", [128, 2], F32, kind="ExternalInput").ap()
    gnorm = nc.dram_tensor("gnorm", [128, D], F32, kind="ExternalInput").ap()
    ident_d = nc.dram_tensor("ident", [128, 128], F32, kind="ExternalInput").ap()
    w_in = nc.dram_tensor("w_in", [D, 3 * D], F32, kind="ExternalInput").ap()
    b_in = nc.dram_tensor("b_in", [128, 24], F32, kind="ExternalInput").ap()
    cw = nc.dram_tensor("cw", [128, 3 * 24], F32, kind="ExternalInput").ap()
    cb = nc.dram_tensor("cb", [128, 24], F32, kind="ExternalInput").ap()
    sT = nc.dram_tensor("sT", [D, TOK], F32, kind="ExternalOutput").ap()
    x0T = nc.dram_tensor("x0T", [D, TOK], F32, kind="ExternalOutput").ap()

    A = nc.alloc_sbuf_tensor
    hT = A("hT", [128, 8, TOK + 2], BF16)
    wbf = A("wbf", [128, 8, 3 * D], BF16)
    wst = [A("wst%d" % i, [128, 768], F32) for i in range(2)]
    xt = [A("xt%d" % i, [128, D], F32) for i in range(2)]
    sq = A("sq", [128, D], F32)
    hb = [A("hb%d" % i, [128, D], BF16) for i in range(2)]
    ss = [A("ss%d" % i, [128, 1], F32) for i in range(2)]
    rs = [A("rs%d" % i, [128, 1], F32) for i in range(2)]
    gt = A("gt", [128, D], F32)
    idf = A("idf", [128, 128], F32)
    idb = A("idb", [128, 128], BF16)
    em = A("em", [128, 2], F32)
    bi = A("bi", [128, 24], F32)
    cwt = A("cwt", [128, 72], F32)
    cbt = A("cbt", [128, 24], F32)
    zb = [A("zb%d" % i, [128, TOK + 2], F32) for i in range(2)]
    acc = [A("acc%d" % i, [128, TOK], F32) for i in range(2)]
    tp = [nc.alloc_psum_tensor("tp%d" % i, [128, 8 * 128], BF16) for i in range(2)]
    mp = [nc.alloc_psum_tensor("mp%d" % i, [128, 512], F32) for i in range(4)]
    hp = nc.alloc_psum_tensor("hp", [128, 2], F32)

    B = S.buf
    b_hT, b_wbf = B("hT"), B("wbf")
    b_wst = [B("wst0"), B("wst1")]
    b_xt = [B("xt0"), B("xt1")]
    b_sq = B("sq")
    b_hb = [B("hb0"), B("hb1")]
    b_ss = [B("ss0"), B("ss1")]
    b_rs = [B("rs0"), B("rs1")]
    b_c = B("consts")
    b_idb = B("idb")
    b_zb = [B("zb0"), B("zb1")]
    b_acc = [B("acc0"), B("acc1")]
    b_tp = [B("tp0"), B("tp1")]
    b_mp = [B("mp%d" % i) for i in range(4)]
    b_hp = B("hp")

    for dst, src in ((gt, gnorm), (idf, ident_d), (em, emask), (bi, b_in), (cwt, cw), (cbt, cb)):
        S.dma("sp", dst[:], src, writes=[b_c], sem_buf=b_c)
    S.op("dve", lambda e: e.tensor_copy(out=idb[:], in_=idf[:]), reads=[b_c], writes=[b_idb])

    for k2 in range(32):
        k, hf = divmod(k2, 4)
        S.dma("pool", wst[k2 % 2][:], w_in[k * 128:(k + 1) * 128, hf * 768:(hf + 1) * 768], writes=[b_wst[k2 % 2]])
        S.op("pool", lambda e, k=k, hf=hf, k2=k2: e.tensor_copy(out=wbf[:, k, hf * 768:(hf + 1) * 768], in_=wst[k2 % 2][:]),
             reads=[b_wst[k2 % 2]], writes=[b_wbf])

    for i in range(NT + 1):
        j = i % 2
        src = x_own[i * 128:(i + 1) * 128, :] if i < NT else x_halo
        S.dma("sp", xt[j][:], src, writes=[b_xt[j]])
        S.op("act", lambda e, j=j: e.activation(out=sq[:], in_=xt[j][:], func=AF.Square, accum_out=ss[j][:]),
             reads=[b_xt[j]], writes=[b_sq, b_ss[j]])
        S.op("act", lambda e, j=j: e.activation(out=rs[j][:], in_=ss[j][:], func=AF.Sqrt, scale=1.0 / D, bias=EPS),
             reads=[b_ss[j]], writes=[b_rs[j]])
        S.op("dve", lambda e, j=j: e.reciprocal(out=rs[j][:], in_=rs[j][:]), reads=[b_rs[j]], writes=[b_rs[j]])
        S.op("dve", lambda e, j=j: e.scalar_tensor_tensor(out=hb[j][:], in0=xt[j][:], scalar=rs[j][:, 0:1], in1=gt[:],
                                                          op0=ALU.mult, op1=ALU.mult),
             reads=[b_xt[j], b_rs[j], b_c], writes=[b_hb[j]])
        for k in range(8):
            S.op("pe", lambda e, j=j, k=k: e.transpose(out=tp[j][:, k * 128:(k + 1) * 128],
                                                        in_=hb[j][:, k * 128:(k + 1) * 128], identity=idb[:]),
                 reads=[b_hb[j], b_idb], writes=[b_tp[j]])
        if i < NT:
            S.op("act", lambda e, j=j, i=i: e.copy(out=hT[:, :, 1 + i * 128:1 + (i + 1) * 128],
                                                   in_=tp[j][:].rearrange("p (k t) -> p k t", k=8)),
                 reads=[b_tp[j]], writes=[b_hT])
        else:
            S.op("act", lambda e, j=j: e.copy(out=hT[:, :, 0:TOK + 2:TOK + 1],
                                              in_=tp[j][:].rearrange("p (k t) -> p k t", k=8)[:, :, 0:2]),
                 reads=[b_tp[j]], writes=[b_hT])

    def proj_conv(cc, zi, ai):
        for jg in range(8):
            m = (cc * 8 + jg) % 4
            for k in range(8):
                S.op("pe", lambda e, m=m, k=k, jg=jg: e.matmul(mp[m][:], lhsT=wbf[:, k, cc * 128:(cc + 1) * 128],
                                                               rhs=hT[:, k, 1 + jg * 512:1 + (jg + 1) * 512],
                                                               start=(k == 0), stop=(k == 7)),
                     reads=[b_wbf, b_hT], writes=[b_mp[m]])
            S.op("act", lambda e, m=m, jg=jg: e.activation(out=zb[zi][:, 1 + jg * 512:1 + (jg + 1) * 512], in_=mp[m][:],
                                                           func=AF.Identity, bias=bi[:, cc:cc + 1], scale=1.0),
                 reads=[b_mp[m], b_c], writes=[b_zb[zi]])
        for k in range(8):
            S.op("pe", lambda e, k=k: e.matmul(hp[:], lhsT=wbf[:, k, cc * 128:(cc + 1) * 128],
                                               rhs=hT[:, k, 0:TOK + 2:TOK + 1], start=(k == 0), stop=(k == 7)),
                 reads=[b_wbf, b_hT], writes=[b_hp])
        S.op("act", lambda e: e.activation(out=zb[zi][:, 0:TOK + 2:TOK + 1], in_=hp[:], func=AF.Identity,
                                           bias=bi[:, cc:cc + 1], scale=1.0),
             reads=[b_hp, b_c], writes=[b_zb[zi]])
        S.op("dve", lambda e: e.tensor_tensor(out=zb[zi][:, 0:TOK + 2:TOK + 1], in0=zb[zi][:, 0:TOK + 2:TOK + 1],
                                              in1=em[:], op=ALU.mult),
             reads=[b_zb[zi], b_c], writes=[b_zb[zi]])
        eng = "dve"
        S.op(eng, lambda e: e.tensor_scalar(out=acc[ai][:], in0=zb[zi][:, 0:TOK], scalar1=cwt[:, cc:cc + 1],
                                            scalar2=cbt[:, cc:cc + 1], op0=ALU.mult, op1=ALU.add),
             reads=[b_zb[zi], b_c], writes=[b_acc[ai]])
        for t in (1, 2):
            S.op(eng, lambda e, t=t: e.scalar_tensor_tensor(out=acc[ai][:], in0=zb[zi][:, t:t + TOK],
                                                           scalar=cwt[:, t * 24 + cc:t * 24 + cc + 1], in1=acc[ai][:],
                                                           op0=ALU.mult, op1=ALU.add),
                 reads=[b_zb[zi], b_c, b_acc[ai]], writes=[b_acc[ai]])

    for c in range(8):
        proj_conv(c, 0, 0)
        S.dma("sp", x0T[c * 128:(c + 1) * 128, :], acc[0][:], reads=[b_acc[0]], is_output=True)
        proj_conv(8 + c, 1, 1)
        proj_conv(16 + c, 0, 0)
        S.op("pool", lambda e: e.tensor_tensor(out=acc[0][:], in0=acc[0][:], in1=acc[1][:], op=ALU.mult),
             reads=[b_acc[1], b_acc[0]], writes=[b_acc[0]])
        S.dma("sp", sT[c * 128:(c + 1) * 128, :], acc[0][:], reads=[b_acc[0]], is_output=True)
    S.finish()
    return nc


def run_A(inp):
    x = np.ascontiguousarray(inp["x"], dtype=np.float32)
    nc = build_A()
    g = np.ascontiguousarray(np.broadcast_to(inp["norm_mix"][0][None, :], (128, D)), dtype=np.float32)
    ident = np.eye(128, dtype=np.float32)
    w_in = np.ascontiguousarray(inp["hy_w_in"][0], dtype=np.float32)
    b_in = np.ascontiguousarray(inp["hy_b_in"][0].reshape(24, 128).T)
    cwv = np.ascontiguousarray(inp["hy_conv_w"][0].reshape(3, 24, 128).transpose(2, 0, 1).reshape(128, 72))
    cbv = np.ascontiguousarray(inp["hy_conv_b"][0].reshape(24, 128).T)
    in_maps = []
    for c in range(NCORES):
        b, q = divmod(c, 4)
        t0 = q * TOK
        halo = np.zeros((128, D), np.float32)
        em = np.zeros((128, 2), np.float32)
        if q > 0:
            halo[0] = x[b, t0 - 1]
            em[:, 0] = 1.0
        if q < 3:
            halo[1] = x[b, t0 + TOK]
            em[:, 1] = 1.0
        in_maps.append({"x_own": np.ascontiguousarray(x[b, t0:t0 + TOK]), "x_halo": halo, "emask": em, "gnorm": g,
                        "ident": ident, "w_in": w_in, "b_in": b_in, "cw": cwv, "cb": cbv})
    res = _run(nc, in_maps, "A")
    return res.results


NFFT = 2 * SEQ
CG = 8
NG = 128 // CG


def fft_consts():
    i128 = np.arange(128, dtype=np.float64)
    i256 = np.arange(256, dtype=np.float64)
    c = {}
    a = 2 * np.pi * np.outer(i128, i256) / 256.0
    c["FA"] = np.concatenate([np.cos(a), -np.sin(a)], 1)
    c["FB"] = np.concatenate([np.sin(a), np.cos(a)], 1)
    t = 2 * np.pi * np.outer(i128, i256) / NFFT
    c["TW"] = np.stack([np.cos(t), -np.sin(t)], 1)
    f = 2 * np.pi * np.outer(i128, i128) / 128.0
    c["F128"] = np.stack([np.cos(f), -np.sin(f), np.sin(f)], 1)
    c["GA"] = np.concatenate([np.cos(f), np.sin(f)], 1)
    c["GB"] = np.concatenate([-np.sin(f), np.cos(f)], 1)
    k1 = (128 * np.arange(2)[None, :, None] + i128[:, None, None])
    it = 2 * np.pi * k1 * i128[None, None, :] / NFFT
    c["ITW"] = np.stack([np.cos(it), np.sin(it)], 2)
    h = 2 * np.pi * k1 * i128[None, None, :] / 256.0
    c["H"] = np.stack([np.cos(h), np.sin(h), -np.sin(h)], 2)
    pos = 128 * i128[:, None] + i128[None, :]
    tl = np.linspace(0.0, 1.0, SEQ, dtype=np.float32)
    c["NEGT"] = -tl[pos.astype(np.int64)]
    return {k: np.ascontiguousarray(v, dtype=np.float32) for k, v in c.items()}


def filter_feat():
    f32 = np.float32
    L = SEQ
    t = np.linspace(0.0, 1.0, L, dtype=f32)[:, None]
    w = (f32(2.0 * math.pi) * np.arange(L, dtype=f32)[:, None] / f32(L)).astype(f32)
    bands = np.linspace(1e-4, 15, 16, dtype=f32)[None, :]
    bw = (bands * w).astype(f32)
    feat = np.concatenate([t, np.cos(bw), -np.sin(bw)], axis=-1).astype(f32)
    return np.ascontiguousarray(feat.T)


def build_B():
    nc = bass.Bass("TRN2", target_bir_lowering=False)
    S = Sched(nc)
    DT = nc.dram_tensor
    s_in = DT("s_in", [2, 128, SEQ], F32, kind="ExternalInput").ap()
    featT = DT("featT", [33, SEQ], F32, kind="ExternalInput").ap()
    w1d = DT("f_w1", [33, 64], F32, kind="ExternalInput").ap()
    w2d = DT("f_w2", [64, 64], F32, kind="ExternalInput").ap()
    w3d = DT("f_w3", [64, 64], F32, kind="ExternalInput").ap()
    fbd = DT("f_bf", [64, 4], F32, kind="ExternalInput").ap()
    woutd = DT("f_wout", [64, NG * 2 * CG], F32, kind="ExternalInput").ap()
    decd = DT("decay", [1, NG * 2 * CG], F32, kind="ExternalInput").ap()
    cd = {}
    shapes = {"FA": [128, 512], "FB": [128, 512], "TW": [128, 2, 256], "F128": [128, 3, 128], "GA": [128, 256],
              "GB": [128, 256], "ITW": [128, 2, 2, 128], "H": [128, 2, 3, 128], "NEGT": [128, 128]}
    for k, sh in shapes.items():
        cd[k] = DT("c_" + k, sh, F32, kind="ExternalInput").ap()
    y_out = DT("y_out", [2, 128, SEQ], F32, kind="ExternalOutput").ap()

    A = nc.alloc_sbuf_tensor
    B = S.buf
    cf = {k: A("cf_" + k, sh, F32) for k, sh in shapes.items()}
    cb = {k: A("cb_" + k, shapes[k], BF16) for k in ("FA", "FB", "F128", "GA", "GB", "H")}
    b_const = B("const")
    for k in shapes:
        S.dma("sp", cf[k][:], cd[k], writes=[b_const], sem_buf=b_const)
    b_cb = B("constbf")
    for k in cb:
        S.op("pool", lambda e, k=k: e.tensor_copy(out=cb[k][:], in_=cf[k][:]), reads=[b_const], writes=[b_cb])
    w1s, w2s, w3s = A("w1s", [33, 64], F32), A("w2s", [64, 64], F32), A("w3s", [64, 64], F32)
    fb = A("fb", [64, 4], F32)
    fbb = A("fbb", [64, 3], F32)
    wout = A("wout", [64, NG * 2 * CG], F32)
    absdec = A("absdec", [128, NG * 2 * CG], F32)
    ones = A("ones", [128, 128], F32)
    b_fw = B("fw")
    for dst, src in ((w1s, w1d), (w2s, w2d), (w3s, w3d), (fb, fbd), (wout, woutd)):
        S.dma("sp", dst[:], src, writes=[b_fw], sem_buf=b_fw)
    S.dma("sp", absdec[:], decd.broadcast_to([128, NG * 2 * CG]), writes=[b_fw], sem_buf=b_fw)
    b_fw2 = B("fw2")
    S.op("act", lambda e: e.activation(out=absdec[:], in_=absdec[:], func=AF.Abs),
         reads=[b_fw], writes=[b_fw])
    S.op("dve", lambda e: e.tensor_tensor(out=fbb[:], in0=fb[:, 0:3], in1=fb[:, 3:4].broadcast_to([64, 3]), op=ALU.mult),
         reads=[b_fw], writes=[b_fw2])
    S.op("pool", lambda e: e.memset(ones[:], 1.0), writes=[b_fw2])

    h3 = A("h3", [64, SEQ], F32)
    b_h3 = B("h3")
    ft = [A("ft%d" % i, [33, 512], F32) for i in range(2)]
    b_ft = [B("ft0"), B("ft1")]
    arg = [A("arg%d" % i, [64, 512], F32) for i in range(2)]
    b_arg = [B("arg0"), B("arg1")]
    hh = [A("hh%d" % i, [64, 512], F32) for i in range(2)]
    b_hh = [B("hh0"), B("hh1")]
    NPS = 8
    ps = [nc.alloc_psum_tensor("ps%d" % i, [128, 512], F32) for i in range(NPS)]
    b_ps = [B("ps%d" % i) for i in range(NPS)]
    TWO_PI = 2.0 * math.pi
    OFF = math.pi + 4 * TWO_PI
    cnt = [0]

    fs = A("fs", [64, 1], F32)
    fu = A("fu", [64, 3], F32)
    S.op("dve", lambda e: e.tensor_scalar(out=fs[:], in0=fb[:, 3:4], scalar1=1.0 / TWO_PI, scalar2=None, op0=ALU.mult),
         reads=[b_fw], writes=[b_fw2])
    S.op("dve", lambda e: e.tensor_scalar(out=fu[:], in0=fbb[:], scalar1=1.0 / TWO_PI, scalar2=4.5, op0=ALU.mult, op1=ALU.add),
         reads=[b_fw2], writes=[b_fw2])
    ki = [A("ki%d" % i, [64, 512], mybir.dt.int32) for i in range(2)]
    kf = [A("kf%d" % i, [64, 512], F32) for i in range(2)]

    def mlp_layer(pi, lhsT, rhs_ap, rhs_buf, li, out_ap, out_buf):
        a = cnt[0] % 2
        cnt[0] += 1
        S.op("pe", lambda e: e.matmul(ps[pi][0:64, :], lhsT=lhsT, rhs=rhs_ap, start=True, stop=True),
             reads=[b_fw, rhs_buf], writes=[b_ps[pi]])
        S.op("dve", lambda e: e.tensor_scalar(out=arg[a][:], in0=ps[pi][0:64, :], scalar1=fs[:, 0:1], scalar2=fu[:, li:li + 1],
                                              op0=ALU.mult, op1=ALU.add),
             reads=[b_ps[pi], b_fw, b_fw2], writes=[b_arg[a]])
        S.op("dve", lambda e: e.tensor_copy(out=ki[a][:], in_=arg[a][:]), reads=[b_arg[a]], writes=[b_arg[a]])
        S.op("dve", lambda e: e.tensor_copy(out=kf[a][:], in_=ki[a][:]), reads=[b_arg[a]], writes=[b_arg[a]])
        S.op("dve", lambda e: e.tensor_tensor(out=arg[a][:], in0=arg[a][:], in1=kf[a][:], op=ALU.subtract),
             reads=[b_arg[a]], writes=[b_arg[a]])
        S.op("dve", lambda e: e.scalar_tensor_tensor(out=arg[a][:], in0=arg[a][:], scalar=0.0, in1=arg[a][:],
                                                     op0=ALU.is_lt, op1=ALU.add),
             reads=[b_arg[a]], writes=[b_arg[a]])
        S.op("act", lambda e: e.activation(out=out_ap, in_=arg[a][:], func=AF.Sin, bias=negpi[:, 0:1], scale=6.283185),
             reads=[b_arg[a], b_fw2], writes=[out_buf])

    negpi = A("negpi", [64, 1], F32)
    S.op("pool", lambda e: e.memset(negpi[:], -3.1415925), writes=[b_fw2])
    for pg in range(SEQ // 512):
        j = pg % 2
        S.dma("sp", ft[j][:], featT[:, pg * 512:(pg + 1) * 512], writes=[b_ft[j]])
        mlp_layer(0, w1s[:], ft[j][:], b_ft[j], 0, hh[0][:], b_hh[0])
        mlp_layer(1, w2s[:], hh[0][:], b_hh[0], 1, hh[1][:], b_hh[1])
        mlp_layer(2, w3s[:], hh[1][:], b_hh[1], 2, h3[:, pg * 512:(pg + 1) * 512], b_h3)

    G1f = A("G1f", [128, 2 * CG, 128], F32)
    win = A("win", [128, 128, 2 * CG], F32)
    pm = A("pm", [128, 2, CG, 128], BF16)
    Gh = A("Gh", [128, CG, 2, 256], F32)
    part = A("part", [128, 2 * CG], F32)
    tot = A("tot", [128, 2 * CG], F32)
    rn = A("rn", [128, CG], F32)
    D1f = A("D1f", [128, CG, 2, 128], F32)
    D1b = A("D1b", [128, CG, 2, 128], BF16)
    NBB, NPB, NPBB = 6, 4, 3
    Bb = [A("Bb%d" % i, [128, 2, 256], BF16) for i in range(NBB)]
    P1 = [A("P1_%d" % i, [128, 512], F32) for i in range(NPB)]
    P2 = [A("P2_%d" % i, [128, 512], F32) for i in range(NPB)]
    Pb = [A("Pb%d" % i, [128, 2, 256], BF16) for i in range(NPBB)]
    ring = {}

    def nxt(key, n):
        v = ring.get(key, 0)
        ring[key] = v + 1
        return v % n

    Cb = [A("Cb%d" % i, [128, 2, 2, 4, 128], BF16) for i in range(2)]
    yb = [A("yb%d" % i, [128, 4, 2, 128], F32) for i in range(2)]
    b_G1f, b_win, b_pm, b_Gh, b_part, b_rn = B("G1f"), B("win"), B("pm"), B("Gh"), B("part"), B("rn")
    b_D1f, b_D1b = B("D1f"), B("D1b")
    b_Bb = [B("Bb%d" % i) for i in range(NBB)]
    b_P1 = [B("P1_%d" % i) for i in range(NPB)]
    b_P2 = [B("P2_%d" % i) for i in range(NPB)]
    b_Pb = [B("Pb%d" % i) for i in range(NPBB)]
    b_Cb = [B("Cb0"), B("Cb1")]
    b_yb = [B("yb0"), B("yb1")]
    pctr = [0]
    cctr = [0]

    def next_ps():
        pctr[0] += 1
        return pctr[0] % NPS

    def cmul(psv, t1, t2, o_re, o_im, shp):
        k = cctr[0] % 2
        cctr[0] += 1
        p1 = P1[k][:].rearrange(shp[0], **shp[1])
        p2 = P2[k][:].rearrange(shp[0], **shp[1])
        return k, p1, p2

    def fwd_stage(lhs_re, lhs_im, lhs_buf, bsel):
        pi = next_ps()
        S.op("pe", lambda e: e.matmul(ps[pi][:], lhsT=lhs_re, rhs=cb["FA"][:], start=True, stop=(lhs_im is None)),
             reads=[lhs_buf, b_cb], writes=[b_ps[pi]])
        if lhs_im is not None:
            S.op("pe", lambda e: e.matmul(ps[pi][:], lhsT=lhs_im, rhs=cb["FB"][:], start=False, stop=True),
                 reads=[lhs_buf, b_cb], writes=[b_ps[pi]])
        k = cctr[0] % 2
        cctr[0] += 1
        pv = ps[pi][:].rearrange("p (r k) -> p r k", r=2)
        p1 = P1[k][:].rearrange("p (r k) -> p r k", r=2)
        p2 = P2[k][:].rearrange("p (r k) -> p r k", r=2)
        tre = cf["TW"][:, 0:1, :].broadcast_to([128, 2, 256])
        tim = cf["TW"][:, 1:2, :].broadcast_to([128, 2, 256])
        S.op("dve", lambda e: e.tensor_tensor(out=p1, in0=pv, in1=tre, op=ALU.mult),
             reads=[b_ps[pi], b_const], writes=[b_P[k]])
        S.op("dve", lambda e: e.tensor_tensor(out=p2, in0=pv, in1=tim, op=ALU.mult),
             reads=[b_ps[pi], b_const], writes=[b_P[k]])
        S.op("pool", lambda e: e.tensor_tensor(out=Bb[bsel][:, 0, :], in0=P1[k][:, 0:256], in1=P2[k][:, 256:512], op=ALU.subtract),
             reads=[b_P[k]], writes=[b_Bb[bsel]])
        S.op("pool", lambda e: e.tensor_tensor(out=Bb[bsel][:, 1, :], in0=P2[k][:, 0:256], in1=P1[k][:, 256:512], op=ALU.add),
             reads=[b_P[k]], writes=[b_Bb[bsel]])

    for g in range(NG):
        c0 = g * CG
        gs = slice(g * 2 * CG, (g + 1) * 2 * CG)
        S.op("dve", lambda e, gs=gs: e.tensor_tensor(out=win[:], in0=cf["NEGT"][:].unsqueeze(2).broadcast_to([128, 128, 2 * CG]),
                                                     in1=absdec[:, gs].unsqueeze(1).broadcast_to([128, 128, 2 * CG]), op=ALU.mult),
             reads=[b_const, b_fw], writes=[b_win])
        S.op("act", lambda e: e.activation(out=win[:], in_=win[:], func=AF.Exp), reads=[b_win], writes=[b_win])
        for a16 in range(8):
            pi = next_ps()
            fo = ps[pi][:, 0:16 * 2 * CG].rearrange("p (i s) -> p i s", i=16)
            for i in range(16):
                n2 = a16 * 16 + i
                S.op("pe", lambda e, n2=n2, i=i, pi=pi, gs=gs: e.matmul(ps[pi][:, i * 2 * CG:(i + 1) * 2 * CG], lhsT=h3[:, n2:SEQ:128],
                                                                       rhs=wout[:, gs], start=True, stop=True),
                     reads=[b_h3, b_fw], writes=[b_ps[pi]])
            S.op("dve", lambda e, a16=a16, fo=fo: e.tensor_tensor(
                out=G1f[:, :, a16 * 16:(a16 + 1) * 16].rearrange("p s i -> p i s"), in0=fo,
                in1=win[:, a16 * 16:(a16 + 1) * 16, :], op=ALU.mult),
                 reads=[b_ps[pi], b_win], writes=[b_G1f])
        S.op("pool", lambda e: e.memset(G1f[0:1, CG:2 * CG, 0:1], 0.0), writes=[b_G1f])
        wv = win[:].rearrange("p a s -> p (a s)").rearrange("p (s n) -> p s n", n=128)
        S.op("act", lambda e, wv=wv: e.activation(out=wv, in_=G1f[:], func=AF.Abs),
             reads=[b_G1f], writes=[b_win])
        S.op("dve", lambda e, wv=wv: e.tensor_reduce(out=part[:], in_=wv, axis=AX.X, op=ALU.add),
             reads=[b_win], writes=[b_part])
        pi = next_ps()
        S.op("pe", lambda e, pi=pi: e.matmul(ps[pi][:, 0:2 * CG], lhsT=ones[:], rhs=part[:], start=True, stop=True),
             reads=[b_part, b_fw2], writes=[b_ps[pi]])
        S.op("act", lambda e, pi=pi: e.copy(out=tot[:], in_=ps[pi][:, 0:2 * CG]), reads=[b_ps[pi]], writes=[b_part])
        S.op("dve", lambda e: e.tensor_tensor(out=rn[:], in0=tot[:, 0:CG], in1=tot[:, CG:2 * CG], op=ALU.add),
             reads=[b_part], writes=[b_rn])
        S.op("dve", lambda e: e.tensor_scalar(out=rn[:], in0=rn[:], scalar1=float(NFFT), scalar2=None, op0=ALU.mult),
             reads=[b_rn], writes=[b_rn])
        S.op("dve", lambda e: e.reciprocal(out=rn[:], in_=rn[:]), reads=[b_rn], writes=[b_rn])
        S.op("pool", lambda e: e.tensor_tensor(out=pm[:, 0], in0=G1f[:, 0:CG, :], in1=G1f[:, CG:2 * CG, :], op=ALU.add),
             reads=[b_G1f], writes=[b_pm])
        S.op("pool", lambda e: e.tensor_tensor(out=pm[:, 1], in0=G1f[:, 0:CG, :], in1=G1f[:, CG:2 * CG, :], op=ALU.subtract),
             reads=[b_G1f], writes=[b_pm])
        for b in range(2):
            S.dma("sp", D1f[:, :, b, :], s_in[b, c0:c0 + CG, :].rearrange("c (n1 n2) -> n1 c n2", n2=128),
                  writes=[b_D1f])
        S.op("act", lambda e: e.copy(out=D1b[:], in_=D1f[:]), reads=[b_D1f], writes=[b_D1b])
        F = cb["F128"]
        H = cb["H"]
        tre = cf["TW"][:, 0:1, :].broadcast_to([128, 2, 256])
        tim = cf["TW"][:, 1:2, :].broadcast_to([128, 2, 256])

        def cmul_tw(pi, bsel):
            k = nxt("P", NPB)
            pv = ps[pi][:].rearrange("p (r k) -> p r k", r=2)
            p1 = P1[k][:].rearrange("p (r k) -> p r k", r=2)
            p2 = P2[k][:].rearrange("p (r k) -> p r k", r=2)
            S.op("dve", lambda e: e.tensor_tensor(out=p1, in0=pv, in1=tre, op=ALU.mult), reads=[b_ps[pi], b_const], writes=[b_P1[k]])
            S.op("dve", lambda e: e.tensor_tensor(out=p2, in0=pv, in1=tim, op=ALU.mult), reads=[b_ps[pi], b_const], writes=[b_P2[k]])
            S.op("pool", lambda e: e.tensor_tensor(out=Bb[bsel][:, 0, :], in0=P1[k][:, 0:256], in1=P2[k][:, 256:512], op=ALU.subtract),
                 reads=[b_P1[k], b_P2[k]], writes=[b_Bb[bsel]])
            S.op("pool", lambda e: e.tensor_tensor(out=Bb[bsel][:, 1, :], in0=P2[k][:, 0:256], in1=P1[k][:, 256:512], op=ALU.add),
                 reads=[b_P1[k], b_P2[k]], writes=[b_Bb[bsel]])

        def st1(it):
            if it["kind"] == "f":
                cl = it["cl"]
                it["pa"] = [next_ps(), next_ps()]
                it["bb"] = [nxt("Bb", NBB), nxt("Bb", NBB)]
                for z in range(2):
                    pi = it["pa"][z]
                    S.op("pe", lambda e, pi=pi, z=z, cl=cl: e.matmul(ps[pi][:], lhsT=pm[:, z, cl, :], rhs=cb["FA"][:], start=True, stop=True),
                         reads=[b_pm, b_cb], writes=[b_ps[pi]])
                for z in range(2):
                    cmul_tw(it["pa"][z], it["bb"][z])
            else:
                cl = it["cl"]
                pi = next_ps()
                it["bb"] = [nxt("Bb", NBB)]
                S.op("pe", lambda e, pi=pi, cl=cl: e.matmul(ps[pi][:], lhsT=D1b[:, cl, 0, :], rhs=cb["FA"][:], start=True, stop=False),
                     reads=[b_D1b, b_cb], writes=[b_ps[pi]])
                S.op("pe", lambda e, pi=pi, cl=cl: e.matmul(ps[pi][:], lhsT=D1b[:, cl, 1, :], rhs=cb["FB"][:], start=False, stop=True),
                     reads=[b_D1b, b_cb], writes=[b_ps[pi]])
                cmul_tw(pi, it["bb"][0])

        def st2(it):
            cl = it["cl"]
            pi = next_ps()
            b0 = it["bb"][0]
            b1 = it["bb"][-1]
            S.op("pe", lambda e, pi=pi, b0=b0: e.matmul(ps[pi][:, 0:256], lhsT=F[:, 0, :], rhs=Bb[b0][:, 0, :], start=True, stop=False),
                 reads=[b_Bb[b0], b_cb], writes=[b_ps[pi]])
            S.op("pe", lambda e, pi=pi, b0=b0: e.matmul(ps[pi][:, 0:256], lhsT=F[:, 2, :], rhs=Bb[b0][:, 1, :], start=False, stop=True),
                 reads=[b_Bb[b0], b_cb], writes=[b_ps[pi]])
            S.op("pe", lambda e, pi=pi, b1=b1: e.matmul(ps[pi][:, 256:512], lhsT=F[:, 1, :], rhs=Bb[b1][:, 0, :], start=True, stop=False),
                 reads=[b_Bb[b1], b_cb], writes=[b_ps[pi]])
            S.op("pe", lambda e, pi=pi, b1=b1: e.matmul(ps[pi][:, 256:512], lhsT=F[:, 0, :], rhs=Bb[b1][:, 1, :], start=False, stop=True),
                 reads=[b_Bb[b1], b_cb], writes=[b_ps[pi]])
            if it["kind"] == "f":
                S.op("act", lambda e, pi=pi, cl=cl: e.activation(out=Gh[:, cl].rearrange("p r k -> p (r k)"), in_=ps[pi][:],
                                                                func=AF.Copy, scale=rn[:, cl:cl + 1]),
                     reads=[b_ps[pi], b_rn], writes=[b_Gh])
                return
            k = nxt("P", NPB)
            pk = nxt("Pb", NPBB)
            it["pk"] = pk
            pv = ps[pi][:].rearrange("p (r k) -> p r k", r=2)
            p1 = P1[k][:].rearrange("p (r k) -> p r k", r=2)
            p2 = P2[k][:].rearrange("p (r k) -> p r k", r=2)
            S.op("dve", lambda e, pv=pv, p1=p1, cl=cl: e.tensor_tensor(out=p1, in0=pv, in1=Gh[:, cl, 0:1, :].broadcast_to([128, 2, 256]), op=ALU.mult),
                 reads=[b_ps[pi], b_Gh], writes=[b_P1[k]])
            S.op("dve", lambda e, pv=pv, p2=p2, cl=cl: e.tensor_tensor(out=p2, in0=pv, in1=Gh[:, cl, 1:2, :].broadcast_to([128, 2, 256]), op=ALU.mult),
                 reads=[b_ps[pi], b_Gh], writes=[b_P2[k]])
            S.op("pool", lambda e, k=k, pk=pk: e.tensor_tensor(out=Pb[pk][:, 0, :], in0=P1[k][:, 0:256], in1=P2[k][:, 256:512], op=ALU.subtract),
                 reads=[b_P1[k], b_P2[k]], writes=[b_Pb[pk]])
            S.op("pool", lambda e, k=k, pk=pk: e.tensor_tensor(out=Pb[pk][:, 1, :], in0=P2[k][:, 0:256], in1=P1[k][:, 256:512], op=ALU.add),
                 reads=[b_P1[k], b_P2[k]], writes=[b_Pb[pk]])

        def st3(it):
            if it["kind"] == "f":
                return
            cl, pk = it["cl"], it["pk"]
            q4, ci = divmod(cl, 4)
            cbi = (g * (CG // 4) + q4) % 2
            pi2 = next_ps()
            for j in range(2):
                S.op("pe", lambda e, pi2=pi2, j=j, pk=pk: e.matmul(ps[pi2][:, j * 256:(j + 1) * 256], lhsT=Pb[pk][:, 0, j * 128:(j + 1) * 128],
                                                                 rhs=cb["GA"][:], start=True, stop=False),
                     reads=[b_Pb[pk], b_cb], writes=[b_ps[pi2]])
                S.op("pe", lambda e, pi2=pi2, j=j, pk=pk: e.matmul(ps[pi2][:, j * 256:(j + 1) * 256], lhsT=Pb[pk][:, 1, j * 128:(j + 1) * 128],
                                                                 rhs=cb["GB"][:], start=False, stop=True),
                     reads=[b_Pb[pk], b_cb], writes=[b_ps[pi2]])
            k = nxt("P", NPB)
            cv = ps[pi2][:].rearrange("p (j r n) -> p j r n", j=2, r=2)
            p1 = P1[k][:].rearrange("p (j r n) -> p j r n", j=2, r=2)
            p2 = P2[k][:].rearrange("p (j r n) -> p j r n", j=2, r=2)
            S.op("dve", lambda e, cv=cv, p1=p1: e.tensor_tensor(out=p1, in0=cv, in1=cf["ITW"][:, :, 0:1, :].broadcast_to([128, 2, 2, 128]), op=ALU.mult),
                 reads=[b_ps[pi2], b_const], writes=[b_P1[k]])
            S.op("dve", lambda e, cv=cv, p2=p2: e.tensor_tensor(out=p2, in0=cv, in1=cf["ITW"][:, :, 1:2, :].broadcast_to([128, 2, 2, 128]), op=ALU.mult),
                 reads=[b_ps[pi2], b_const], writes=[b_P2[k]])
            S.op("pool", lambda e, p1=p1, p2=p2, ci=ci, cbi=cbi: e.tensor_tensor(out=Cb[cbi][:, :, 0, ci, :], in0=p1[:, :, 0, :], in1=p2[:, :, 1, :], op=ALU.subtract),
                 reads=[b_P1[k], b_P2[k]], writes=[b_Cb[cbi]])
            S.op("pool", lambda e, p1=p1, p2=p2, ci=ci, cbi=cbi: e.tensor_tensor(out=Cb[cbi][:, :, 1, ci, :], in0=p2[:, :, 0, :], in1=p1[:, :, 1, :], op=ALU.add),
                 reads=[b_P1[k], b_P2[k]], writes=[b_Cb[cbi]])
            if ci != 3:
                return
            pr, pim = next_ps(), next_ps()
            seq = [(pr, 0, 0, True), (pr, 2, 1, False), (pim, 1, 0, True), (pim, 0, 1, False)]
            for (pp, hsel, ri, first) in seq:
                for j in range(2):
                    S.op("pe", lambda e, pp=pp, hsel=hsel, ri=ri, j=j, first=first, cbi=cbi: e.matmul(
                        ps[pp][:], lhsT=H[:, j, hsel, :], rhs=Cb[cbi][:, j, ri, :, :].rearrange("p c n -> p (c n)"),
                        start=(first and j == 0), stop=((not first) and j == 1)),
                         reads=[b_Cb[cbi], b_cb], writes=[b_ps[pp]])
            S.op("act", lambda e, pr=pr, cbi=cbi: e.copy(out=yb[cbi][:, :, 0, :], in_=ps[pr][:].rearrange("p (c n) -> p c n", c=4)),
                 reads=[b_ps[pr]], writes=[b_yb[cbi]])
            S.op("act", lambda e, pim=pim, cbi=cbi: e.copy(out=yb[cbi][:, :, 1, :], in_=ps[pim][:].rearrange("p (c n) -> p c n", c=4)),
                 reads=[b_ps[pim]], writes=[b_yb[cbi]])
            for b in range(2):
                cc = c0 + q4 * 4
                S.dma("sp", y_out[b, cc:cc + 4, :].rearrange("c (n1 n2) -> n1 c n2", n2=128), yb[cbi][:, :, b, :],
                      reads=[b_yb[cbi]], is_output=True)

        items = [{"kind": "f", "cl": cl} for cl in range(CG)] + [{"kind": "d", "cl": cl} for cl in range(CG)]
        run_pipeline(items, [st1, st2, st3])
    S.finish()
    return nc


def run_B(inp, s_cs):
    nc = build_B()
    consts = fft_consts()
    featT = filter_feat()
    fbf = np.stack([inp["hy_f_b1"][0], inp["hy_f_b2"][0], inp["hy_f_b3"][0], inp["hy_f_freq"][0]], 1).astype(np.float32)
    in_maps = []
    for c in range(NCORES):
        wo = inp["hy_f_wout"][0].reshape(64, 2, 8, NG, CG)[:, :, c]
        wo = np.ascontiguousarray(wo.transpose(0, 2, 1, 3).reshape(64, NG * 2 * CG))
        de = inp["hy_decay"][0].reshape(2, 8, NG, CG)[:, c]
        de = np.ascontiguousarray(de.transpose(1, 0, 2).reshape(1, NG * 2 * CG))
        m = {"s_in": np.ascontiguousarray(s_cs[c]), "featT": featT,
             "f_w1": np.ascontiguousarray(inp["hy_f_w1"][0]), "f_w2": np.ascontiguousarray(inp["hy_f_w2"][0]),
             "f_w3": np.ascontiguousarray(inp["hy_f_w3"][0]), "f_bf": np.ascontiguousarray(fbf),
             "f_wout": wo, "decay": de}
        for k, v in consts.items():
            m["c_" + k] = v
        in_maps.append(m)
    res = _run(nc, in_maps, "B")
    return res.results


def load_weight_bf16(S, nc, w_dram, rows, cols, name, bufname, qeng="pool", ceng="pool", stage=None):
    nk = rows // 128
    wt = nc.alloc_sbuf_tensor(name + "_sb", [128, nk, cols], BF16)
    b_w = S.buf(bufname)
    if stage is None:
        st = [nc.alloc_sbuf_tensor(name + "_st%d" % i, [128, 1024], F32) for i in range(2)]
        b_st = [S.buf(name + "_st0"), S.buf(name + "_st1")]
        stage = (st, b_st, [0])
    st, b_st, ctr = stage
    for k in range(nk):
        for c0 in range(0, cols, 1024):
            cw = min(1024, cols - c0)
            i = ctr[0] % 2
            ctr[0] += 1
            S.dma(qeng, st[i][:, 0:cw], w_dram[k * 128:(k + 1) * 128, c0:c0 + cw], writes=[b_st[i]])
            ce = ceng if isinstance(ceng, str) else ceng[ctr[0] % len(ceng)]
            if ce == "act":
                S.op("act", lambda e, i=i, k=k, c0=c0, cw=cw: e.copy(out=wt[:, k, c0:c0 + cw], in_=st[i][:, 0:cw]),
                     reads=[b_st[i]], writes=[b_w])
            else:
                S.op(ce, lambda e, i=i, k=k, c0=c0, cw=cw: e.tensor_copy(out=wt[:, k, c0:c0 + cw], in_=st[i][:, 0:cw]),
                     reads=[b_st[i]], writes=[b_w])
    return wt, b_w, stage


class NormT:
    def __init__(self, S, nc, gt, b_gt, idb, b_idb, tag, ntp=2):
        A = nc.alloc_sbuf_tensor
        self.S, self.nc = S, nc
        self.ntp = ntp
        self.gt, self.b_gt, self.idb, self.b_idb = gt, b_gt, idb, b_idb
        self.sq = A(tag + "sq", [128, D], F32)
        self.b_sq = S.buf(tag + "sq")
        self.hb = [A(tag + "hb%d" % i, [128, D], BF16) for i in range(2)]
        self.b_hb = [S.buf(tag + "hb0"), S.buf(tag + "hb1")]
        self.ss = [A(tag + "ss%d" % i, [128, 1], F32) for i in range(2)]
        self.rs = [A(tag + "rs%d" % i, [128, 1], F32) for i in range(2)]
        self.b_s = [S.buf(tag + "s0"), S.buf(tag + "s1")]
        self.tp = [nc.alloc_psum_tensor(tag + "tp%d" % i, [128, 8 * 128], BF16) for i in range(ntp)]
        self.b_tp = [S.buf(tag + "tp%d" % i) for i in range(ntp)]
        self.n = 0

    def rstd(self, x_ap, b_x, j):
        S = self.S
        ss, rs, sq = self.ss[j], self.rs[j], self.sq
        S.op("act", lambda e: e.activation(out=sq[:], in_=x_ap, func=AF.Square, accum_out=ss[:]),
             reads=[b_x], writes=[self.b_sq, self.b_s[j]])
        S.op("act", lambda e: e.activation(out=rs[:], in_=ss[:], func=AF.Sqrt, scale=1.0 / D, bias=EPS),
             reads=[self.b_s[j]], writes=[self.b_s[j]])
        S.op("dve", lambda e: e.reciprocal(out=rs[:], in_=rs[:]), reads=[self.b_s[j]], writes=[self.b_s[j]])
        return rs

    def __call__(self, x_ap, b_x, hT_dst, b_hT):
        S = self.S
        j = self.n % 2
        self.n += 1
        rs = self.rstd(x_ap, b_x, j)
        hb, tp = self.hb[j], self.tp[j % self.ntp]
        b_tp = self.b_tp[j % self.ntp]
        gt, idb = self.gt, self.idb
        S.op("dve", lambda e: e.scalar_tensor_tensor(out=hb[:], in0=x_ap, scalar=rs[:, 0:1], in1=gt[:], op0=ALU.mult, op1=ALU.mult),
             reads=[b_x, self.b_s[j], self.b_gt], writes=[self.b_hb[j]])
        for k in range(8):
            S.op("pe", lambda e, k=k: e.transpose(out=tp[:, k * 128:(k + 1) * 128], in_=hb[:, k * 128:(k + 1) * 128], identity=idb[:]),
                 reads=[self.b_hb[j], self.b_idb], writes=[b_tp])
        S.op("act", lambda e: e.copy(out=hT_dst, in_=tp[:].rearrange("p (k t) -> p k t", k=8)),
             reads=[b_tp], writes=[b_hT])


def load_consts_common(S, nc, gnorm_d, ident_d):
    A = nc.alloc_sbuf_tensor
    gt = A("gt", [128, D], F32)
    idf = A("idf", [128, 128], F32)
    idb = A("idb", [128, 128], BF16)
    b_gt, b_idf, b_idb = S.buf("gt"), S.buf("idf"), S.buf("idb")
    S.dma("sp", gt[:], gnorm_d, writes=[b_gt])
    S.dma("sp", idf[:], ident_d, writes=[b_idf])
    S.op("dve", lambda e: e.tensor_copy(out=idb[:], in_=idf[:]), reads=[b_idf], writes=[b_idb])
    return gt, b_gt, idb, b_idb


FBLK = 256


def phase_F(nc, S, io, ntok, final_norm):
    x_in, gnorm, ident_d, wg_d, wu_d, wd_d, x_out = (io[k] for k in ("x_in", "gnorm", "ident", "wg", "wu", "wd", "x_out"))
    gfin_d = io.get("gfin")
    A = nc.alloc_sbuf_tensor
    gt, b_gt, idb, b_idb = load_consts_common(S, nc, gnorm, ident_d)
    if final_norm:
        gf = A("gf", [128, D], F32)
        b_gf = S.buf("gf")
        S.dma("sp", gf[:], gfin_d, writes=[b_gf])
    wg, b_wg, stg = load_weight_bf16(S, nc, wg_d, D, FF, "wg", "wg", ceng=("pool", "act"))
    wu, b_wu, stg = load_weight_bf16(S, nc, wu_d, D, FF, "wu", "wu", ceng=("pool", "act"), stage=stg)
    wd, b_wd, stg = load_weight_bf16(S, nc, wd_d, FF, D, "wd", "wd", ceng=("pool", "act"), stage=stg)
    NF = FF // 128
    nt = FBLK // 128
    xt = [A("xt%d" % i, [128, D], F32) for i in range(2 * nt)]
    b_xt = [S.buf("xt%d" % i) for i in range(2 * nt)]
    hT = A("hT", [128, 8, FBLK], BF16)
    b_hT = S.buf("hT")
    aT = A("aT", [128, NF, FBLK], BF16)
    b_aT = S.buf("aT")
    sg = [A("sg%d" % i, [128, FBLK], F32) for i in range(2)]
    b_sg = [S.buf("sg0"), S.buf("sg1")]
    ot = [A("ot%d" % i, [128, D], F32) for i in range(2)]
    b_ot = [S.buf("ot0"), S.buf("ot1")]
    norm = NormT(S, nc, gt, b_gt, idb, b_idb, "n")
    gp = [nc.alloc_psum_tensor("gp%d" % i, [128, 512], F32) for i in range(4)]
    b_gp = [S.buf("gp%d" % i) for i in range(4)]
    dp = [nc.alloc_psum_tensor("dp%d" % i, [128, 512], F32) for i in range(2)]
    b_dp = [S.buf("dp0"), S.buf("dp1")]
    oc = 0
    for blk in range(ntok // FBLK):
        par = (blk % 2) * nt
        for t in range(nt):
            tok0 = blk * FBLK + t * 128
            S.dma("sp", xt[par + t][:], x_in[tok0:tok0 + 128, :], writes=[b_xt[par + t]])
            norm(xt[par + t][:], b_xt[par + t], hT[:, :, t * 128:(t + 1) * 128], b_hT)
        for f in range(NF):
            g_i, u_i = (2 * f) % 4, (2 * f + 1) % 4
            for k in range(8):
                S.op("pe", lambda e, f=f, k=k, g_i=g_i: e.matmul(gp[g_i][:, 0:FBLK], lhsT=wg[:, k, f * 128:(f + 1) * 128], rhs=hT[:, k, :],
                                                                start=(k == 0), stop=(k == 7)),
                     reads=[b_wg, b_hT], writes=[b_gp[g_i]])
            for k in range(8):
                S.op("pe", lambda e, f=f, k=k, u_i=u_i: e.matmul(gp[u_i][:, 0:FBLK], lhsT=wu[:, k, f * 128:(f + 1) * 128], rhs=hT[:, k, :],
                                                                start=(k == 0), stop=(k == 7)),
                     reads=[b_wu, b_hT], writes=[b_gp[u_i]])
            si = f % 2
            S.op("act", lambda e, g_i=g_i, si=si: e.activation(out=sg[si][:], in_=gp[g_i][:, 0:FBLK], func=AF.Silu),
                 reads=[b_gp[g_i]], writes=[b_sg[si]])
            S.op("dve", lambda e, f=f, u_i=u_i, si=si: e.tensor_tensor(out=aT[:, f, :], in0=sg[si][:], in1=gp[u_i][:, 0:FBLK], op=ALU.mult),
                 reads=[b_sg[si], b_gp[u_i]], writes=[b_aT])
        for t in range(nt):
            tok0 = blk * FBLK + t * 128
            for hf in range(2):
                for f in range(NF):
                    S.op("pe", lambda e, f=f, hf=hf, t=t: e.matmul(dp[hf][:], lhsT=aT[:, f, t * 128:(t + 1) * 128], rhs=wd[:, f, hf * 512:(hf + 1) * 512],
                                                                  start=(f == 0), stop=(f == NF - 1)),
                         reads=[b_aT, b_wd], writes=[b_dp[hf]])
            o = oc % 2
            oc += 1
            for hf in range(2):
                S.op("dve", lambda e, hf=hf, o=o, t=t, par=par: e.tensor_tensor(out=ot[o][:, hf * 512:(hf + 1) * 512], in0=dp[hf][:],
                                                                               in1=xt[par + t][:, hf * 512:(hf + 1) * 512], op=ALU.add),
                     reads=[b_dp[hf], b_xt[par + t]], writes=[b_ot[o]])
            if final_norm:
                rs = norm.rstd(ot[o][:], b_ot[o], o)
                S.op("dve", lambda e, o=o, rs=rs: e.scalar_tensor_tensor(out=ot[o][:], in0=ot[o][:], scalar=rs[:, 0:1], in1=gf[:],
                                                                        op0=ALU.mult, op1=ALU.mult),
                     reads=[b_ot[o], norm.b_s[o], b_gf], writes=[b_ot[o]])
            S.dma("sp", x_out[tok0:tok0 + 128, :], ot[o][:], reads=[b_ot[o]], is_output=True)


def build_F(final_norm):
    nc = bass.Bass("TRN2", target_bir_lowering=False)
    S = Sched(nc)
    DT = nc.dram_tensor
    io = {"x_in": DT("x_in", [TOK, D], F32, kind="ExternalInput").ap(),
          "gnorm": DT("gnorm", [128, D], F32, kind="ExternalInput").ap(),
          "ident": DT("ident", [128, 128], F32, kind="ExternalInput").ap(),
          "wg": DT("wg", [D, FF], F32, kind="ExternalInput").ap(),
          "wu": DT("wu", [D, FF], F32, kind="ExternalInput").ap(),
          "wd": DT("wd", [FF, D], F32, kind="ExternalInput").ap()}
    if final_norm:
        io["gfin"] = DT("gfin", [128, D], F32, kind="ExternalInput").ap()
    io["x_out"] = DT("x_out", [TOK, D], F32, kind="ExternalOutput").ap()
    phase_F(nc, S, io, TOK, final_norm)
    S.finish()
    return nc


def rep_rows(v):
    return np.ascontiguousarray(np.broadcast_to(np.asarray(v, np.float32)[None, :], (128, v.shape[0])))


def run_F(inp, layer, x_tok, final_norm):
    nc = build_F(final_norm)
    ident = np.eye(128, dtype=np.float32)
    base = {"gnorm": rep_rows(inp["norm_ffn"][layer]), "ident": ident,
            "wg": np.ascontiguousarray(inp["ffn_w_gate"][layer]), "wu": np.ascontiguousarray(inp["ffn_w_up"][layer]),
            "wd": np.ascontiguousarray(inp["ffn_w_down"][layer])}
    if final_norm:
        base["gfin"] = rep_rows(inp["norm_final"])
    in_maps = [dict(base, x_in=np.ascontiguousarray(x_tok[c])) for c in range(NCORES)]
    res = _run(nc, in_maps, "F")
    return [r["x_out"] for r in res.results]


def phase_C(nc, S, io, ntok):
    x_in, yT, sT, x0T, skip_d, wo_d, bo_d, x_out = (io[k] for k in ("x_in", "yT", "sT", "x0T", "skip", "w_out", "b_out", "x_out"))
    A = nc.alloc_sbuf_tensor
    wo, b_wo, _ = load_weight_bf16(S, nc, wo_d, D, D, "wo", "wo")
    skip = A("skip_sb", [128, 8], F32)
    bof = A("bof", [1, D], F32)
    bob = A("bob", [1, D], BF16)
    onesb = A("onesb", [1, 128], BF16)
    b_c = S.buf("c")
    S.dma("sp", skip[:], skip_d, writes=[b_c], sem_buf=b_c)
    S.dma("sp", bof[:], bo_d, writes=[b_c], sem_buf=b_c)
    b_c2 = S.buf("c2")
    S.op("dve", lambda e: e.tensor_copy(out=bob[:], in_=bof[:]), reads=[b_c], writes=[b_c2])
    S.op("pool", lambda e: e.memset(onesb[:], 1.0), writes=[b_c2])
    BL = 512
    yt = [A("yt%d" % i, [128, BL], F32) for i in range(2)]
    st_ = [A("st%d" % i, [128, BL], F32) for i in range(2)]
    x0t = [A("x0t%d" % i, [128, BL], F32) for i in range(2)]
    b_in = [S.buf("in0"), S.buf("in1")]
    tmp = [A("tmp%d" % i, [128, BL], F32) for i in range(2)]
    b_tmp = [S.buf("tmp0"), S.buf("tmp1")]
    uT = [A("uT%d" % i, [128, 8, BL], BF16) for i in range(2)]
    b_uT = [S.buf("uT0"), S.buf("uT1")]
    xt = [A("xt%d" % i, [128, D], F32) for i in range(2)]
    b_xt = [S.buf("xt0"), S.buf("xt1")]
    ot = [A("ot%d" % i, [128, D], F32) for i in range(2)]
    b_ot = [S.buf("ot0"), S.buf("ot1")]
    mp = [nc.alloc_psum_tensor("mp%d" % i, [128, 512], F32) for i in range(4)]
    b_mp = [S.buf("mp%d" % i) for i in range(4)]
    n = 0
    tc_ = 0
    for blk in range(ntok // BL):
        ub = blk % 2
        cs = slice(blk * BL, (blk + 1) * BL)
        for k in range(8):
            i = n % 2
            n += 1
            rs_ = slice(k * 128, (k + 1) * 128)
            S.dma("sp", yt[i][:], yT[rs_, cs], writes=[b_in[i]], sem_buf=b_in[i])
            S.dma("sp", st_[i][:], sT[rs_, cs], writes=[b_in[i]], sem_buf=b_in[i])
            S.dma("sp", x0t[i][:], x0T[rs_, cs], writes=[b_in[i]], sem_buf=b_in[i])
            S.op("dve", lambda e, i=i, k=k: e.scalar_tensor_tensor(out=tmp[i][:], in0=st_[i][:], scalar=skip[:, k:k + 1], in1=yt[i][:],
                                                                  op0=ALU.mult, op1=ALU.add),
                 reads=[b_in[i], b_c], writes=[b_tmp[i]])
            S.op("pool", lambda e, i=i, k=k, ub=ub: e.tensor_tensor(out=uT[ub][:, k, :], in0=tmp[i][:], in1=x0t[i][:], op=ALU.mult),
                 reads=[b_tmp[i], b_in[i]], writes=[b_uT[ub]])
        for t in range(BL // 128):
            tok0 = blk * BL + t * 128
            j = tc_ % 2
            tc_ += 1
            S.dma("sp", xt[j][:], x_in[tok0:tok0 + 128, :], writes=[b_xt[j]])
            for hf in range(2):
                m = 2 * j + hf
                for k in range(8):
                    S.op("pe", lambda e, m=m, k=k, hf=hf, t=t, ub=ub: e.matmul(mp[m][:], lhsT=uT[ub][:, k, t * 128:(t + 1) * 128],
                                                                              rhs=wo[:, k, hf * 512:(hf + 1) * 512], start=(k == 0), stop=False),
                         reads=[b_uT[ub], b_wo], writes=[b_mp[m]])
                S.op("pe", lambda e, m=m, hf=hf: e.matmul(mp[m][:], lhsT=onesb[:], rhs=bob[:, hf * 512:(hf + 1) * 512], start=False, stop=True),
                     reads=[b_c2], writes=[b_mp[m]])
                S.op("dve", lambda e, m=m, hf=hf, j=j: e.tensor_tensor(out=ot[j][:, hf * 512:(hf + 1) * 512], in0=mp[m][:],
                                                                      in1=xt[j][:, hf * 512:(hf + 1) * 512], op=ALU.add),
                     reads=[b_mp[m], b_xt[j]], writes=[b_ot[j]])
            S.dma("sp", x_out[tok0:tok0 + 128, :], ot[j][:], reads=[b_ot[j]], is_output=True)


def build_C():
    nc = bass.Bass("TRN2", target_bir_lowering=False)
    S = Sched(nc)
    DT = nc.dram_tensor
    io = {"x_in": DT("x_in", [TOK, D], F32, kind="ExternalInput").ap(),
          "yT": DT("yT", [D, TOK], F32, kind="ExternalInput").ap(),
          "sT": DT("sT", [D, TOK], F32, kind="ExternalInput").ap(),
          "x0T": DT("x0T", [D, TOK], F32, kind="ExternalInput").ap(),
          "skip": DT("skip", [128, 8], F32, kind="ExternalInput").ap(),
          "w_out": DT("w_out", [D, D], F32, kind="ExternalInput").ap(),
          "b_out": DT("b_out", [1, D], F32, kind="ExternalInput").ap(),
          "x_out": DT("x_out", [TOK, D], F32, kind="ExternalOutput").ap()}
    phase_C(nc, S, io, TOK)
    S.finish()
    return nc


def run_C(inp, x_tok, yT, sT, x0T):
    nc = build_C()
    base = {"skip": np.ascontiguousarray(inp["hy_skip"][0].reshape(8, 128).T), "w_out": np.ascontiguousarray(inp["hy_w_out"][0]),
            "b_out": np.ascontiguousarray(inp["hy_b_out"][0][None, :])}
    in_maps = [dict(base, x_in=np.ascontiguousarray(x_tok[c]), yT=np.ascontiguousarray(yT[c]), sT=np.ascontiguousarray(sT[c]),
                    x0T=np.ascontiguousarray(x0T[c])) for c in range(NCORES)]
    res = _run(nc, in_maps, "C")
    return [r["x_out"] for r in res.results]


NROWS_LOC = 72
NEG = -30000.0


def na_tables(rpb):
    H = 16
    j = np.arange(64)
    w = np.arange(64)
    cs = np.clip(w - 8, 0, 48)
    colok = (j[:, None] >= cs[None, :]) & (j[:, None] < cs[None, :] + 16)
    coff = np.clip(j[:, None] - w[None, :] + 15, 0, 30)
    out = np.full((2, 64, H, 7, 2, 64), NEG, np.float32)
    for idx in range(7):
        d0 = -6 + 2 * idx
        for i2 in range(2):
            for q2 in range(2):
                dl = d0 + i2 - q2
                if abs(dl) > 7:
                    continue
                g = rpb[:, dl + 7, :][:, coff]
                g = np.where(colok[None], g, np.float32(NEG))
                out[i2, :, :, idx, q2, :] = g.transpose(1, 0, 2)
    return np.ascontiguousarray(out.reshape(128, H, 7, 128))


def na_mlist(p):
    if p == 0:
        return list(range(0, 6))
    if p == 31:
        return list(range(-1, 5))
    return list(range(0, 5))


def na_rowmask(q):
    R0 = 64 * q
    m = np.zeros((128, 32, 6, 2), np.float32)
    for p in range(32):
        for mi, mm in enumerate(na_mlist(p)):
            for i2 in range(2):
                for q2 in range(2):
                    gr = R0 + 2 * p + q2
                    rs = min(max(gr - 4, 0), 248)
                    kr = R0 - 4 + 2 * p + 2 * mm + i2
                    if rs <= kr < rs + 8:
                        m[i2 * 64:(i2 + 1) * 64, p, mi, q2] = 1.0
    return m


def phase_D(nc, S, io):
    xe, gnorm, ident_d, wqkv_d, bqk_d, bv_d, wo_d, bo_d, bt_d, rm_d, x_out = (io[k] for k in (
        "xe", "gnorm", "ident", "w_qkv", "b_qk", "b_v", "w_o", "b_o", "bt", "rowmask", "x_out"))
    A = nc.alloc_sbuf_tensor
    B = S.buf
    gt, b_gt, idb, b_idb = load_consts_common(S, nc, gnorm, ident_d)
    st = [A("wst%d" % i, [128, 1024], F32) for i in range(2)]
    stage = (st, [B("wst0"), B("wst1")], [0])
    wqkv, b_wqkv, stage = load_weight_bf16(S, nc, wqkv_d, D, 3 * D, "wqkv", "wqkv", ceng=("pool", "act"), stage=stage)
    wo, b_wo, stage = load_weight_bf16(S, nc, wo_d, D, D, "wo", "wo", ceng=("pool", "act"), stage=stage)
    BTb = A("BTb", [128, 16, 7, 128], BF16)
    b_BT = B("BT")
    st_, b_st, ctr = stage
    for h in range(16):
        i = ctr[0] % 2
        ctr[0] += 1
        S.dma("pool", st_[i][:, 0:896], bt_d[:, h].rearrange("p a b -> p (a b)"), writes=[b_st[i]])
        S.op("pool", lambda e, i=i, h=h: e.tensor_copy(out=BTb[:, h].rearrange("p a b -> p (a b)"), in_=st_[i][:, 0:896]),
             reads=[b_st[i]], writes=[b_BT])
    rmask = A("rmask", [128, 32, 6, 2], F32)
    bqk = A("bqk", [128, 16], F32)
    bq8 = A("bq8", [128, 8], F32)
    bvb = A("bvb", [1, D], BF16)
    bob = A("bob", [1, D], BF16)
    onesr = A("onesr", [1, 128], BF16)
    onesc = A("onesc", [128, 64], BF16)
    b_c, b_c2 = B("c"), B("c2")
    for dst, src in ((rmask, rm_d), (bqk, bqk_d)):
        S.dma("sp", dst[:], src, writes=[b_c], sem_buf=b_c)
    S.op("dve", lambda e: e.tensor_scalar(out=bq8[:], in0=bqk[:, 0:8], scalar1=0.125, scalar2=None, op0=ALU.mult), reads=[b_c], writes=[b_c2])
    for dstb, src in ((bvb, bv_d), (bob, bo_d)):
        i = ctr[0] % 2
        ctr[0] += 1
        S.dma("pool", st_[i][0:1, :], src, writes=[b_st[i]])
        S.op("pool", lambda e, i=i, dstb=dstb: e.tensor_copy(out=dstb[:], in_=st_[i][0:1, :]), reads=[b_st[i]], writes=[b_c2])
    S.op("pool", lambda e: e.memset(onesr[:], 1.0), writes=[b_c2])
    S.op("pool", lambda e: e.memset(onesc[:], 1.0), writes=[b_c2])

    KT = [A("KT%d" % i, [128, 8, 512], BF16) for i in range(2)]
    VV = [A("VV%d" % i, [128, 4, D], BF16) for i in range(2)]
    QQ = [A("QQ%d" % i, [128, 8, 512], BF16) for i in range(2)]
    b_KT, b_VV, b_QQ = [B("KT0"), B("KT1")], [B("VV0"), B("VV1")], [B("QQ0"), B("QQ1")]
    hT = A("hT", [128, 8, 512], BF16)
    b_hT = B("hT")
    aT = A("aT", [128, 8, 512], BF16)
    b_aT = B("aT")
    xt = [A("xt%d" % i, [128, D], F32) for i in range(2)]
    b_xt = [B("xt0"), B("xt1")]
    ot = [A("ot0", [128, D], F32)]
    b_ot = [B("ot0")]
    NEB = 3
    Eb = [A("Eb%d" % i, [128, 512], BF16) for i in range(NEB)]
    b_Eb = [B("Eb%d" % i) for i in range(NEB)]
    rz = [A("rz%d" % i, [128, 128], F32) for i in range(2)]
    b_rz = [B("rz0"), B("rz1")]
    norm = NormT(S, nc, gt, b_gt, idb, b_idb, "n", ntp=1)
    pp = [nc.alloc_psum_tensor("pp%d" % i, [128, 512], F32) for i in range(2)]
    b_pp = [B("pp0"), B("pp1")]
    sp_ = [nc.alloc_psum_tensor("sps%d" % i, [128, 512], F32) for i in range(3)]
    b_sp = [B("sps%d" % i) for i in range(3)]
    obank = nc.alloc_psum_tensor("obank", [128, 512], F32)
    zbank = nc.alloc_psum_tensor("zbank", [128, 512], F32)
    b_oz = [B("oz0"), B("oz1")]
    cnt = {"pp": 0, "sp": 0, "e": 0, "oz": 0, "x": 0, "o": 0}

    def nxt(k, n):
        v = cnt[k] % n
        cnt[k] += 1
        return v

    def project(kb):
        rb = kb % 2
        for t in range(4):
            j = nxt("x", 2)
            tok0 = kb * 512 + t * 128
            S.dma("sp", xt[j][:], xe[tok0:tok0 + 128, :], writes=[b_xt[j]])
            norm(xt[j][:], b_xt[j], hT[:, :, t * 128:(t + 1) * 128], b_hT)
        for c in range(8):
            pi = nxt("pp", 2)
            for k in range(8):
                S.op("pe", lambda e, pi=pi, k=k, c=c: e.matmul(pp[pi][:], lhsT=wqkv[:, k, D + c * 128:D + (c + 1) * 128], rhs=hT[:, k, :],
                                                              start=(k == 0), stop=(k == 7)),
                     reads=[b_wqkv, b_hT], writes=[b_pp[pi]])
            S.op("act", lambda e, pi=pi, c=c, rb=rb: e.activation(out=KT[rb][:, c, :], in_=pp[pi][:], func=AF.Identity,
                                                                 bias=bqk[:, 8 + c:9 + c], scale=1.0),
                 reads=[b_pp[pi], b_c], writes=[b_KT[rb]])
            pi = nxt("pp", 2)
            for k in range(8):
                S.op("pe", lambda e, pi=pi, k=k, c=c: e.matmul(pp[pi][:], lhsT=wqkv[:, k, c * 128:(c + 1) * 128], rhs=hT[:, k, :],
                                                              start=(k == 0), stop=(k == 7)),
                     reads=[b_wqkv, b_hT], writes=[b_pp[pi]])
            if kb >= 1:
                S.op("act", lambda e, pi=pi, c=c, kb=kb: e.activation(out=QQ[(kb - 1) % 2][:, c, 256:512], in_=pp[pi][:, 0:256], func=AF.Identity,
                                                                     bias=bq8[:, c:c + 1], scale=0.125),
                     reads=[b_pp[pi], b_c2], writes=[b_QQ[(kb - 1) % 2]])
            if kb <= 7:
                S.op("act", lambda e, pi=pi, c=c, kb=kb: e.activation(out=QQ[kb % 2][:, c, 0:256], in_=pp[pi][:, 256:512], func=AF.Identity,
                                                                     bias=bq8[:, c:c + 1], scale=0.125),
                     reads=[b_pp[pi], b_c2], writes=[b_QQ[kb % 2]])
        for t in range(4):
            for hf in range(2):
                pi = nxt("pp", 2)
                for k in range(8):
                    S.op("pe", lambda e, pi=pi, k=k, t=t, hf=hf: e.matmul(pp[pi][:], lhsT=hT[:, k, t * 128:(t + 1) * 128],
                                                                         rhs=wqkv[:, k, 2 * D + hf * 512:2 * D + (hf + 1) * 512],
                                                                         start=(k == 0), stop=False),
                         reads=[b_wqkv, b_hT], writes=[b_pp[pi]])
                S.op("pe", lambda e, pi=pi, hf=hf: e.matmul(pp[pi][:], lhsT=onesr[:], rhs=bvb[:, hf * 512:(hf + 1) * 512], start=False, stop=True),
                     reads=[b_c2], writes=[b_pp[pi]])
                S.op("act", lambda e, pi=pi, t=t, hf=hf, rb=rb: e.copy(out=VV[rb][:, t, hf * 512:(hf + 1) * 512], in_=pp[pi][:]),
                     reads=[b_pp[pi]], writes=[b_VV[rb]])

    def attend(bq):
        qb = bq % 2
        units = []
        for pl in range(4):
            p = 4 * bq + pl
            ml = na_mlist(p)
            for c in range(8):
                o_i = nxt("oz", 2)
                for hp in range(2):
                    for gi, grp in enumerate([ml[0:4], ml[4:]]):
                        units.append({"pl": pl, "p": p, "c": c, "hp": hp, "gi": gi, "grp": grp, "o_i": o_i, "nmm": len(ml),
                                      "last": hp == 1 and gi == 1})

        def u1(u):
            pl, c, hp, grp = u["pl"], u["c"], u["hp"], u["grp"]
            h = 2 * c + hp
            hs = slice(hp * 64, (hp + 1) * 64)
            si = nxt("sp", 3)
            u["si"] = si
            for mi, mm in enumerate(grp):
                blk, tl = bq + (pl + mm) // 4, (pl + mm) % 4
                S.op("pe", lambda e, si=si, mi=mi, h=h, mm=mm: e.matmul(sp_[si][:, mi * 128:(mi + 1) * 128], lhsT=idb[:], rhs=BTb[:, h, mm + 1, :],
                                                                      start=True, stop=False),
                     reads=[b_idb, b_BT], writes=[b_sp[si]])
                S.op("pe", lambda e, si=si, mi=mi, blk=blk, tl=tl, hs=hs, c=c, pl=pl: e.matmul(
                    sp_[si][:, mi * 128:(mi + 1) * 128], lhsT=KT[blk % 2][hs, c, tl * 128:(tl + 1) * 128],
                    rhs=QQ[qb][hs, c, pl * 128:(pl + 1) * 128], start=False, stop=True),
                     reads=[b_KT[blk % 2], b_QQ[qb]], writes=[b_sp[si]])

        def u2(u):
            p, gi, grp, si = u["p"], u["gi"], u["grp"], u["si"]
            ncol = len(grp) * 128
            ei = nxt("e", NEB)
            u["ei"] = ei
            S.op("act", lambda e, ei=ei, si=si, ncol=ncol: e.activation(out=Eb[ei][:, 0:ncol], in_=sp_[si][:, 0:ncol], func=AF.Exp),
                 reads=[b_sp[si]], writes=[b_Eb[ei]])
            nmask = 1 if (gi == 0 and 2 <= p <= 29) else len(grp)
            mi0 = 4 * gi
            S.op("dve", lambda e, ei=ei, p=p, mi0=mi0, n=nmask: e.tensor_tensor(
                out=Eb[ei][:, 0:n * 128].rearrange("p (a q w) -> p a q w", q=2, w=64),
                in0=Eb[ei][:, 0:n * 128].rearrange("p (a q w) -> p a q w", q=2, w=64),
                in1=rmask[:, p, mi0:mi0 + n, :].unsqueeze(3).broadcast_to([128, n, 2, 64]), op=ALU.mult),
                 reads=[b_Eb[ei], b_c], writes=[b_Eb[ei]])

        def u3(u):
            pl, c, hp, gi, grp, ei, o_i, nmm = u["pl"], u["c"], u["hp"], u["gi"], u["grp"], u["ei"], u["o_i"], u["nmm"]
            h = 2 * c + hp
            hs = slice(hp * 64, (hp + 1) * 64)
            for mi, mm in enumerate(grp):
                blk, tl = bq + (pl + mm) // 4, (pl + mm) % 4
                done = 4 * gi + mi + 1
                S.op("pe", lambda e, o_i=o_i, ei=ei, mi=mi, blk=blk, tl=tl, hs=hs, h=h, st=(done == 1), sp=(done == nmm): e.matmul(
                    obank[hs, o_i * 128:(o_i + 1) * 128], lhsT=VV[blk % 2][:, tl, h * 64:(h + 1) * 64], rhs=Eb[ei][:, mi * 128:(mi + 1) * 128],
                    start=st, stop=sp),
                     reads=[b_VV[blk % 2], b_Eb[ei]], writes=[b_oz[o_i]])
                S.op("pe", lambda e, o_i=o_i, ei=ei, mi=mi, hs=hs, st=(done == 1), sp=(done == nmm): e.matmul(
                    zbank[hs, o_i * 128:(o_i + 1) * 128], lhsT=onesc[:], rhs=Eb[ei][:, mi * 128:(mi + 1) * 128], start=st, stop=sp),
                     reads=[b_c2, b_Eb[ei]], writes=[b_oz[o_i]])
            if u["last"]:
                S.op("dve", lambda e, o_i=o_i: e.reciprocal(out=rz[o_i][:], in_=zbank[:, o_i * 128:(o_i + 1) * 128]), reads=[b_oz[o_i]], writes=[b_rz[o_i]])
                S.op("dve", lambda e, o_i=o_i, c=c, pl=pl: e.tensor_tensor(out=aT[:, c, pl * 128:(pl + 1) * 128], in0=obank[:, o_i * 128:(o_i + 1) * 128],
                                                                          in1=rz[o_i][:], op=ALU.mult),
                     reads=[b_oz[o_i], b_rz[o_i]], writes=[b_aT])

        run_pipeline(units, [u1, u2, u3])
        for t in range(4):
            j = nxt("x", 2)
            tok_e = (4 + 8 * bq) * 64 + t * 128
            tok_o = bq * 512 + t * 128
            S.dma("sp", xt[j][:], xe[tok_e:tok_e + 128, :], writes=[b_xt[j]])
            o = 0
            for hf in range(2):
                pi = nxt("pp", 2)
                for k in range(8):
                    S.op("pe", lambda e, pi=pi, k=k, t=t, hf=hf: e.matmul(pp[pi][:], lhsT=aT[:, k, t * 128:(t + 1) * 128],
                                                                         rhs=wo[:, k, hf * 512:(hf + 1) * 512], start=(k == 0), stop=False),
                         reads=[b_aT, b_wo], writes=[b_pp[pi]])
                S.op("pe", lambda e, pi=pi, hf=hf: e.matmul(pp[pi][:], lhsT=onesr[:], rhs=bob[:, hf * 512:(hf + 1) * 512], start=False, stop=True),
                     reads=[b_c2], writes=[b_pp[pi]])
                S.op("dve", lambda e, pi=pi, hf=hf, j=j, o=o: e.tensor_tensor(out=ot[o][:, hf * 512:(hf + 1) * 512], in0=pp[pi][:],
                                                                             in1=xt[j][:, hf * 512:(hf + 1) * 512], op=ALU.add),
                     reads=[b_pp[pi], b_xt[j]], writes=[b_ot[o]])
            S.dma("sp", x_out[tok_o:tok_o + 128, :], ot[o][:], reads=[b_ot[o]], is_output=True)

    for kb in range(9):
        project(kb)
        if kb >= 1:
            attend(kb - 1)


def d_io(nc, ident=None):
    DT = nc.dram_tensor
    return {"gnorm": DT("d_gnorm", [128, D], F32, kind="ExternalInput").ap(),
            "ident": ident if ident is not None else DT("ident", [128, 128], F32, kind="ExternalInput").ap(),
            "w_qkv": DT("w_qkv", [D, 3 * D], F32, kind="ExternalInput").ap(),
            "b_qk": DT("b_qk", [128, 16], F32, kind="ExternalInput").ap(),
            "b_v": DT("b_v", [1, D], F32, kind="ExternalInput").ap(),
            "w_o": DT("w_o", [D, D], F32, kind="ExternalInput").ap(),
            "b_o": DT("b_o", [1, D], F32, kind="ExternalInput").ap(),
            "bt": DT("bt", [128, 16, 7, 128], F32, kind="ExternalInput").ap(),
            "rowmask": DT("rowmask", [128, 32, 6, 2], F32, kind="ExternalInput").ap()}


def build_D():
    nc = bass.Bass("TRN2", target_bir_lowering=False)
    S = Sched(nc)
    io = d_io(nc)
    io["xe"] = nc.dram_tensor("xe", [NROWS_LOC * 64, D], F32, kind="ExternalInput").ap()
    io["x_out"] = nc.dram_tensor("x_out", [TOK, D], F32, kind="ExternalOutput").ap()
    phase_D(nc, S, io)
    S.finish()
    return nc


def run_D(inp, xb_full):
    nc = build_D()
    ident = np.eye(128, dtype=np.float32)
    bq = inp["na_b_qkv"][0]
    bqk = np.concatenate([bq[0:D].reshape(8, 128).T, bq[D:2 * D].reshape(8, 128).T], 1).astype(np.float32)
    base = {"d_gnorm": rep_rows(inp["norm_mix"][1]), "ident": ident, "w_qkv": np.ascontiguousarray(inp["na_w_qkv"][0]),
            "b_qk": np.ascontiguousarray(bqk), "b_v": np.ascontiguousarray(bq[2 * D:][None, :]),
            "w_o": np.ascontiguousarray(inp["na_w_o"][0]), "b_o": np.ascontiguousarray(inp["na_b_o"][0][None, :]),
            "bt": na_tables(np.asarray(inp["na_rpb"][0], np.float32))}
    in_maps = []
    for c in range(NCORES):
        b, q = divmod(c, 4)
        xe = np.zeros((NROWS_LOC * 64, D), np.float32)
        g0 = (64 * q - 4) * 64
        lo, hi = max(g0, 0), min(g0 + NROWS_LOC * 64, SEQ)
        xe[lo - g0:hi - g0] = xb_full[b, lo:hi]
        in_maps.append(dict(base, xe=xe, rowmask=na_rowmask(q)))
    res = _run(nc, in_maps, "D")
    return [r["x_out"] for r in res.results]


def _tok_shards(a):
    return [np.ascontiguousarray(a[c // 4, (c % 4) * TOK:(c % 4 + 1) * TOK]) for c in range(NCORES)]


def _assemble(shards):
    out = np.empty((BATCH, SEQ, D), np.float32)
    for c in range(NCORES):
        out[c // 4, (c % 4) * TOK:(c % 4 + 1) * TOK] = shards[c]
    return out


NEXT = NROWS_LOC * 64


def build_L2():
    nc0 = bass.Bass("TRN2", target_bir_lowering=False)
    S = Sched(nc0)
    nc = NCP(nc0)
    DT = nc0.dram_tensor
    ext = lambda name, shape: DT(name, shape, F32, kind="ExternalInput").ap()
    ident = ext("ident", [128, 128])
    xa_s = DT("xa_scr", [NEXT, D], F32).ap()
    xb_s = DT("xb_scr", [NEXT, D], F32).ap()
    xc_s = DT("xc_scr", [TOK, D], F32).ap()
    out = DT("out", [TOK, D], F32, kind="ExternalOutput").ap()
    ioC = {"x_in": ext("x_ext", [NEXT, D]), "yT": ext("yT", [D, NEXT]), "sT": ext("sT", [D, NEXT]), "x0T": ext("x0T", [D, NEXT]),
           "skip": ext("skip", [128, 8]), "w_out": ext("w_out", [D, D]), "b_out": ext("b_out", [1, D]), "x_out": xa_s}
    ioF0 = {"x_in": xa_s, "gnorm": ext("gn_f0", [128, D]), "ident": ident, "wg": ext("wg0", [D, FF]), "wu": ext("wu0", [D, FF]),
            "wd": ext("wd0", [FF, D]), "x_out": xb_s}
    ioD = d_io(nc0, ident)
    ioD["xe"] = xb_s
    ioD["x_out"] = xc_s
    ioF1 = {"x_in": xc_s, "gnorm": ext("gn_f1", [128, D]), "ident": ident, "wg": ext("wg1", [D, FF]), "wu": ext("wu1", [D, FF]),
            "wd": ext("wd1", [FF, D]), "gfin": ext("gfin", [128, D]), "x_out": out}
    S.pfx = "c_"
    phase_C(nc, S, ioC, NEXT)
    S.barrier()
    nc.reset()
    S.pfx = "f0_"
    phase_F(nc, S, ioF0, NEXT, False)
    S.barrier()
    nc.reset()
    S.pfx = "d_"
    phase_D(nc, S, ioD)
    S.barrier()
    nc.reset()
    S.pfx = "f1_"
    phase_F(nc, S, ioF1, TOK, True)
    S.finish()
    return nc0


def _ext_tok(a, c):
    b, q = divmod(c, 4)
    g0 = q * TOK - 256
    out = np.zeros((NEXT,) + a.shape[2:], np.float32)
    lo, hi = max(g0, 0), min(g0 + NEXT, SEQ)
    out[lo - g0:hi - g0] = a[b, lo:hi]
    return out


def run_L2(inp, x, y_full, s_full, x0_full):
    nc = build_L2()
    bq = inp["na_b_qkv"][0]
    bqk = np.concatenate([bq[0:D].reshape(8, 128).T, bq[D:2 * D].reshape(8, 128).T], 1).astype(np.float32)
    base = {"ident": np.eye(128, dtype=np.float32),
            "skip": np.ascontiguousarray(inp["hy_skip"][0].reshape(8, 128).T), "w_out": np.ascontiguousarray(inp["hy_w_out"][0]),
            "b_out": np.ascontiguousarray(inp["hy_b_out"][0][None, :]),
            "gn_f0": rep_rows(inp["norm_ffn"][0]), "wg0": np.ascontiguousarray(inp["ffn_w_gate"][0]),
            "wu0": np.ascontiguousarray(inp["ffn_w_up"][0]), "wd0": np.ascontiguousarray(inp["ffn_w_down"][0]),
            "gn_f1": rep_rows(inp["norm_ffn"][1]), "wg1": np.ascontiguousarray(inp["ffn_w_gate"][1]),
            "wu1": np.ascontiguousarray(inp["ffn_w_up"][1]), "wd1": np.ascontiguousarray(inp["ffn_w_down"][1]),
            "gfin": rep_rows(inp["norm_final"]),
            "d_gnorm": rep_rows(inp["norm_mix"][1]), "w_qkv": np.ascontiguousarray(inp["na_w_qkv"][0]),
            "b_qk": np.ascontiguousarray(bqk), "b_v": np.ascontiguousarray(bq[2 * D:][None, :]),
            "w_o": np.ascontiguousarray(inp["na_w_o"][0]), "b_o": np.ascontiguousarray(inp["na_b_o"][0][None, :]),
            "bt": na_tables(np.asarray(inp["na_rpb"][0], np.float32))}
    in_maps = []
    for c in range(NCORES):
        m = dict(base)
        m["x_ext"] = _ext_tok(x, c)
        m["yT"] = np.ascontiguousarray(_ext_tok(y_full, c).T)
        m["sT"] = np.ascontiguousarray(_ext_tok(s_full, c).T)
        m["x0T"] = np.ascontiguousarray(_ext_tok(x0_full, c).T)
        m["rowmask"] = na_rowmask(c % 4)
        in_maps.append(m)
    res = _run(nc, in_maps, "L2")
    return [r["out"] for r in res.results]


def kernel(**inp):
    inp = {k: np.asarray(v, dtype=np.float32) for k, v in inp.items()}
    x = inp["x"]
    resA = run_A(inp)
    sT = [r["sT"] for r in resA]
    x0T = [r["x0T"] for r in resA]
    s_cs = [np.empty((2, 128, SEQ), np.float32) for _ in range(NCORES)]
    for c in range(NCORES):
        b, q = divmod(c, 4)
        for cg in range(NCORES):
            s_cs[cg][b, :, q * TOK:(q + 1) * TOK] = sT[c][cg * 128:(cg + 1) * 128]
    resB = run_B(inp, s_cs)
    y_full = np.empty((BATCH, SEQ, D), np.float32)
    for cg in range(NCORES):
        y_full[:, :, cg * 128:(cg + 1) * 128] = resB[cg]["y_out"].transpose(0, 2, 1)
    s_full = _assemble([t.T for t in sT])
    x0_full = _assemble([t.T for t in x0T])
    out = run_L2(inp, x, y_full, s_full, x0_full)
    return _assemble(out)
```

```python
import math
import numpy as np
import ml_dtypes
import concourse.bass as bass
import concourse.mybir as mybir
from concourse.bass_utils import run_bass_kernel_spmd
import os

F32 = mybir.dt.float32
BF16 = mybir.dt.bfloat16
AF = mybir.ActivationFunctionType
ALU = mybir.AluOpType
AX = mybir.AxisListType

D = 1024
SEQ = 16384
BATCH = 2
NCORES = 8
TOK = 4096
FF = 2816
EPS = 1e-6


class Buf:
    __slots__ = ("name", "last_w", "readers", "sem", "dma_cnt")

    def __init__(self, name):
        self.name = name
        self.last_w = None
        self.readers = []
        self.sem = None
        self.dma_cnt = 0


class Ins:
    __slots__ = ("eng", "fn", "deps", "milestone", "ms", "dma_sem", "dma_val")

    def __init__(self, eng, fn):
        self.eng = eng
        self.fn = fn
        self.deps = []
        self.milestone = False
        self.ms = 0
        self.dma_sem = None
        self.dma_val = 0


class Sched:
    ENGS = ("pe", "act", "dve", "pool", "sp")

    def __init__(self, nc):
        self.nc = nc
        self.q = {e: [] for e in self.ENGS}
        self.esem = {e: nc.alloc_semaphore("prog_" + e) for e in self.ENGS}
        self.nbuf = 0
        self.out_events = []
        self.pfx = ""
        self.pending_barrier = {}
        self.all_dma = []

    def buf(self, name=None):
        self.nbuf += 1
        return Buf(self.pfx + (name or ("b%d" % self.nbuf)))

    def barrier(self):
        deps = [self.q[e][-1] for e in self.ENGS if self.q[e] and self.q[e][-1].fn is not None]
        last = {}
        for d in self.all_dma:
            last[id(d.dma_sem)] = d
        deps += list(last.values())
        self.pending_barrier = {e: list(deps) for e in self.ENGS}

    def _push(self, eng, ins):
        pb = self.pending_barrier.pop(eng, None)
        if pb:
            for d in pb:
                if d is not ins and d not in ins.deps:
                    ins.deps.append(d)
        self.q[eng].append(ins)

    def _collect(self, ins, reads, writes):
        deps = []
        for b in reads:
            if b.last_w is not None:
                deps.append(b.last_w)
        for b in writes:
            if b.last_w is not None:
                deps.append(b.last_w)
            deps.extend(b.readers)
        for d in deps:
            if d is ins:
                continue
            if d.dma_sem is None and d.eng == "pe" and ins.eng == "pe":
                continue
            ins.deps.append(d)
        for b in writes:
            b.last_w = ins
            b.readers = []
        for b in reads:
            if b not in writes:
                b.readers.append(ins)

    def op(self, eng, fn, reads=(), writes=()):
        ins = Ins(eng, fn)
        self._collect(ins, list(reads), list(writes))
        self._push(eng, ins)
        return ins

    def dma(self, eng, out_ap, in_ap, reads=(), writes=(), sem_buf=None, is_output=False, **kw):
        if sem_buf is None:
            sem_buf = (list(writes) + list(reads))[0]
        if sem_buf.sem is None:
            sem_buf.sem = self.nc.alloc_semaphore("dma_" + sem_buf.name)
        ins = Ins(eng, lambda e: e.dma_start(out=out_ap, in_=in_ap, **kw))
        self._collect(ins, list(reads), list(writes))
        sem_buf.dma_cnt += 16
        ins.dma_sem = sem_buf.sem
        ins.dma_val = sem_buf.dma_cnt
        self.all_dma.append(ins)
        self._push(eng, ins)
        if is_output:
            self.out_events.append(ins)
        return ins

    def coll(self, kind, out_ap, in_ap, reads=(), writes=(), sem_buf=None):
        if sem_buf is None:
            sem_buf = (list(writes) + list(reads))[0]
        if sem_buf.sem is None:
            sem_buf.sem = self.nc.alloc_semaphore("dma_" + sem_buf.name)
        groups = [list(range(NCORES))]
        ins = Ins("pool", lambda e: e.collective_compute(kind, ALU.bypass, replica_groups=groups, ins=[in_ap], outs=[out_ap]))
        self._collect(ins, list(reads), list(writes))
        sem_buf.dma_cnt += 16
        ins.dma_sem = sem_buf.sem
        ins.dma_val = sem_buf.dma_cnt
        self.all_dma.append(ins)
        self._push("pool", ins)
        return ins

    def finish(self):
        fin = Ins("sp", None)
        fin.deps = list(self.out_events)
        self.q["sp"].append(fin)
        for e in self.ENGS:
            for ins in self.q[e]:
                for d in ins.deps:
                    if d.dma_sem is None:
                        d.milestone = True
        for e in self.ENGS:
            c = 0
            for ins in self.q[e]:
                if ins.milestone:
                    c += 1
                    ins.ms = c
        nc = self.nc
        engobj = {"pe": "tensor", "act": "scalar", "dve": "vector", "pool": "gpsimd", "sp": "sync"}

        def emit(ename, e):
            waited = {}
            for ins in self.q[ename]:
                need = {}
                for d in ins.deps:
                    if d.dma_sem is not None:
                        s, v = d.dma_sem, d.dma_val
                    else:
                        s, v = self.esem[d.eng], d.ms
                    k = id(s)
                    if waited.get(k, 0) >= v:
                        continue
                    if k not in need or need[k][1] < v:
                        need[k] = (s, v)
                for k, (s, v) in need.items():
                    e.wait_ge(s, v)
                    waited[k] = v
                if ins.fn is None:
                    continue
                r = ins.fn(e)
                if ins.dma_sem is not None:
                    r.then_inc(ins.dma_sem, 16)
                elif ins.milestone:
                    r.then_inc(self.esem[ename], 1)

        with nc.Block() as block:
            for ename in self.ENGS:
                if not self.q[ename]:
                    continue
                getattr(block, engobj[ename])(lambda e, en=ename: emit(en, e))


ARENA_BYTES = 212800


class NCP:
    _DTB = None

    def __init__(self, nc):
        self._nc = nc
        self._arena = nc.alloc_sbuf_tensor("arena", [128, ARENA_BYTES // 4], F32)
        self._banks = [nc.alloc_psum_tensor("bank%d" % i, [128, 512], F32) for i in range(8)]
        self.reset()

    def reset(self):
        self._off = 0
        self._nbank = 0

    @staticmethod
    def _view(ap2d, shape, dt):
        if dt != F32:
            ap2d = ap2d.bitcast(dt)
        if len(shape) == 2:
            return ap2d
        names = " ".join("d%d" % i for i in range(1, len(shape)))
        kw = {"d%d" % i: shape[i] for i in range(1, len(shape))}
        return ap2d.rearrange("p (%s) -> p %s" % (names, names), **kw)

    def alloc_sbuf_tensor(self, name, shape, dt):
        esz = 2 if dt == BF16 else 4
        n = 1
        for d in shape[1:]:
            n *= d
        nbytes = (n * esz + 31) // 32 * 32
        assert self._off + nbytes <= ARENA_BYTES, "arena overflow at %s (%d + %d)" % (name, self._off, nbytes)
        o4 = self._off // 4
        self._off += nbytes
        return self._view(self._arena[0:shape[0], o4:o4 + (n * esz + 3) // 4], shape, dt)

    def alloc_psum_tensor(self, name, shape, dt):
        esz = 2 if dt == BF16 else 4
        n = 1
        for d in shape[1:]:
            n *= d
        assert n * esz <= 2048 and self._nbank < 8, "psum overflow at " + name
        bk = self._banks[self._nbank]
        self._nbank += 1
        return self._view(bk[0:shape[0], 0:(n * esz + 3) // 4], shape, dt)

    def __getattr__(self, k):
        return getattr(self._nc, k)


def _run(nc, in_maps, tag=""):
    tr = bool(os.environ.get("K_TRACE"))
    res = run_bass_kernel_spmd(nc, in_maps, core_ids=list(range(NCORES)), **({"trace": True} if tr else {}))
    if tr:
        print("K_TRACE", tag, "exec_time_ns", res.exec_time_ns, flush=True)
    return res


def run_pipeline(items, stages):
    n, m = len(items), len(stages)
    for t in range(n + m - 1):
        for k in range(m):
            i = t - k
            if 0 <= i < n:
                stages[k](items[i])


def bcast_rows(ap_row, nparts):
    return ap_row.broadcast(0, nparts) if hasattr(ap_row, "broadcast") else ap_row


def build_A():
    nc = bass.Bass("TRN2", target_bir_lowering=False)
    S = Sched(nc)
    NT = TOK // 128
    x_own = nc.dram_tensor("x_own", [TOK, D], F32, kind="ExternalInput").ap()
    x_halo = nc.dram_tensor("x_halo", [128, D], F32, kind="ExternalInput").ap()
    emask = nc.dram_tensor("emask", [128, 2], F32, kind="ExternalInput").ap()
    gnorm = nc.dram_tensor("gnorm", [128, D], F32, kind="ExternalInput").ap()
    ident_d = nc.dram_tensor("ident", [128, 128], F32, kind="ExternalInput").ap()
    w_in = nc.dram_tensor("w_in", [D, 3 * D], F32, kind="ExternalInput").ap()
    b_in = nc.dram_tensor("b_in", [128, 24], F32, kind="ExternalInput").ap()
    cw = nc.dram_tensor("cw", [128, 3 * 24], F32, kind="ExternalInput").ap()
    cb = nc.dram_tensor("cb", [128, 24], F32, kind="ExternalInput").ap()
    sT = nc.dram_tensor("sT", [D, TOK], F32, kind="ExternalOutput").ap()
    x0T = nc.dram_tensor("x0T", [D, TOK], F32, kind="ExternalOutput").ap()

    A = nc.alloc_sbuf_tensor
    hT = A("hT", [128, 8, TOK + 2], BF16)
    wbf = A("wbf", [128, 8, 3 * D], BF16)
    wst = [A("wst%d" % i, [128, 768], F32) for i in range(2)]
    xt = [A("xt%d" % i, [128, D], F32) for i in range(2)]
    sq = A("sq", [128, D], F32)
    hb = [A("hb%d" % i, [128, D], BF16) for i in range(2)]
    ss = [A("ss%d" % i, [128, 1], F32) for i in range(2)]
    rs = [A("rs%d" % i, [128, 1], F32) for i in range(2)]
    gt = A("gt", [128, D], F32)
    idf = A("idf", [128, 128], F32)
    idb = A("idb", [128, 128], BF16)
    em = A("em", [128, 2], F32)
    bi = A("bi", [128, 24], F32)
    cwt = A("cwt", [128, 72], F32)
    cbt = A("cbt", [128, 24], F32)
    zb = [A("zb%d" % i, [128, TOK + 2], F32) for i in range(2)]
    acc = [A("acc%d" % i, [128, TOK], F32) for i in range(2)]
    tp = [nc.alloc_psum_tensor("tp%d" % i, [128, 8 * 128], BF16) for i in range(2)]
    mp = [nc.alloc_psum_tensor("mp%d" % i, [128, 512], F32) for i in range(4)]
    hp = nc.alloc_psum_tensor("hp", [128, 2], F32)

    B = S.buf
    b_hT, b_wbf = B("hT"), B("wbf")
    b_wst = [B("wst0"), B("wst1")]
    b_xt = [B("xt0"), B("xt1")]
    b_sq = B("sq")
    b_hb = [B("hb0"), B("hb1")]
    b_ss = [B("ss0"), B("ss1")]
    b_rs = [B("rs0"), B("rs1")]
    b_c = B("consts")
    b_idb = B("idb")
    b_zb = [B("zb0"), B("zb1")]
    b_acc = [B("acc0"), B("acc1")]
    b_tp = [B("tp0"), B("tp1")]
    b_mp = [B("mp%d" % i) for i in range(4)]
    b_hp = B("hp")

    for dst, src in ((gt, gnorm), (idf, ident_d), (em, emask), (bi, b_in), (cwt, cw), (cbt, cb)):
        S.dma("sp", dst[:], src, writes=[b_c], sem_buf=b_c)
    S.op("dve", lambda e: e.tensor_copy(out=idb[:], in_=idf[:]), reads=[b_c], writes=[b_idb])

    for k2 in range(32):
        k, hf = divmod(k2, 4)
        S.dma("pool", wst[k2 % 2][:], w_in[k * 128:(k + 1) * 128, hf * 768:(hf + 1) * 768], writes=[b_wst[k2 % 2]])
        S.op("pool", lambda e, k=k, hf=hf, k2=k2: e.tensor_copy(out=wbf[:, k, hf * 768:(hf + 1) * 768], in_=wst[k2 % 2][:]),
             reads=[b_wst[k2 % 2]], writes=[b_wbf])

    for i in range(NT + 1):
        j = i % 2
        src = x_own[i * 128:(i + 1) * 128, :] if i < NT else x_halo
        S.dma("sp", xt[j][:], src, writes=[b_xt[j]])
        S.op("act", lambda e, j=j: e.activation(out=sq[:], in_=xt[j][:], func=AF.Square, accum_out=ss[j][:]),
             reads=[b_xt[j]], writes=[b_sq, b_ss[j]])
        S.op("act", lambda e, j=j: e.activation(out=rs[j][:], in_=ss[j][:], func=AF.Sqrt, scale=1.0 / D, bias=EPS),
             reads=[b_ss[j]], writes=[b_rs[j]])
        S.op("dve", lambda e, j=j: e.reciprocal(out=rs[j][:], in_=rs[j][:]), reads=[b_rs[j]], writes=[b_rs[j]])
        S.op("dve", lambda e, j=j: e.scalar_tensor_tensor(out=hb[j][:], in0=xt[j][:], scalar=rs[j][:, 0:1], in1=gt[:],
                                                          op0=ALU.mult, op1=ALU.mult),
             reads=[b_xt[j], b_rs[j], b_c], writes=[b_hb[j]])
        for k in range(8):
            S.op("pe", lambda e, j=j, k=k: e.transpose(out=tp[j][:, k * 128:(k + 1) * 128],
                                                        in_=hb[j][:, k * 128:(k + 1) * 128], identity=idb[:]),
                 reads=[b_hb[j], b_idb], writes=[b_tp[j]])
        if i < NT:
            S.op("act", lambda e, j=j, i=i: e.copy(out=hT[:, :, 1 + i * 128:1 + (i + 1) * 128],
                                                   in_=tp[j][:].rearrange("p (k t) -> p k t", k=8)),
                 reads=[b_tp[j]], writes=[b_hT])
        else:
            S.op("act", lambda e, j=j: e.copy(out=hT[:, :, 0:TOK + 2:TOK + 1],
                                              in_=tp[j][:].rearrange("p (k t) -> p k t", k=8)[:, :, 0:2]),
                 reads=[b_tp[j]], writes=[b_hT])

    def proj_conv(cc, zi, ai):
        for jg in range(8):
            m = (cc * 8 + jg) % 4
            for k in range(8):
                S.op("pe", lambda e, m=m, k=k, jg=jg: e.matmul(mp[m][:], lhsT=wbf[:, k, cc * 128:(cc + 1) * 128],
                                                               rhs=hT[:, k, 1 + jg * 512:1 + (jg + 1) * 512],
                                                               start=(k == 0), stop=(k == 7)),
                     reads=[b_wbf, b_hT], writes=[b_mp[m]])
            S.op("act", lambda e, m=m, jg=jg: e.activation(out=zb[zi][:, 1 + jg * 512:1 + (jg + 1) * 512], in_=mp[m][:],
                                                           func=AF.Identity, bias=bi[:, cc:cc + 1], scale=1.0),
                 reads=[b_mp[m], b_c], writes=[b_zb[zi]])
        for k in range(8):
            S.op("pe", lambda e, k=k: e.matmul(hp[:], lhsT=wbf[:, k, cc * 128:(cc + 1) * 128],
                                               rhs=hT[:, k, 0:TOK + 2:TOK + 1], start=(k == 0), stop=(k == 7)),
                 reads=[b_wbf, b_hT], writes=[b_hp])
        S.op("act", lambda e: e.activation(out=zb[zi][:, 0:TOK + 2:TOK + 1], in_=hp[:], func=AF.Identity,
                                           bias=bi[:, cc:cc + 1], scale=1.0),
             reads=[b_hp, b_c], writes=[b_zb[zi]])
        S.op("dve", lambda e: e.tensor_tensor(out=zb[zi][:, 0:TOK + 2:TOK + 1], in0=zb[zi][:, 0:TOK + 2:TOK + 1],
                                              in1=em[:], op=ALU.mult),
             reads=[b_zb[zi], b_c], writes=[b_zb[zi]])
        eng = "dve"
        S.op(eng, lambda e: e.tensor_scalar(out=acc[ai][:], in0=zb[zi][:, 0:TOK], scalar1=cwt[:, cc:cc + 1],
                                            scalar2=cbt[:, cc:cc + 1], op0=ALU.mult, op1=ALU.add),
             reads=[b_zb[zi], b_c], writes=[b_acc[ai]])
        for t in (1, 2):
            S.op(eng, lambda e, t=t: e.scalar_tensor_tensor(out=acc[ai][:], in0=zb[zi][:, t:t + TOK],
                                                           scalar=cwt[:, t * 24 + cc:t * 24 + cc + 1], in1=acc[ai][:],
                                                           op0=ALU.mult, op1=ALU.add),
                 reads=[b_zb[zi], b_c, b_acc[ai]], writes=[b_acc[ai]])

    for c in range(8):
        proj_conv(c, 0, 0)
        S.dma("sp", x0T[c * 128:(c + 1) * 128, :], acc[0][:], reads=[b_acc[0]], is_output=True)
        proj_conv(8 + c, 1, 1)
        proj_conv(16 + c, 0, 0)
        S.op("pool", lambda e: e.tensor_tensor(out=acc[0][:], in0=acc[0][:], in1=acc[1][:], op=ALU.mult),
             reads=[b_acc[1], b_acc[0]], writes=[b_acc[0]])
        S.dma("sp", sT[c * 128:(c + 1) * 128, :], acc[0][:], reads=[b_acc[0]], is_output=True)
    S.finish()
    return nc


def run_A(inp):
    x = np.ascontiguousarray(inp["x"], dtype=np.float32)
    nc = build_A()
    g = np.ascontiguousarray(np.broadcast_to(inp["norm_mix"][0][None, :], (128, D)), dtype=np.float32)
    ident = np.eye(128, dtype=np.float32)
    w_in = np.ascontiguousarray(inp["hy_w_in"][0], dtype=np.float32)
    b_in = np.ascontiguousarray(inp["hy_b_in"][0].reshape(24, 128).T)
    cwv = np.ascontiguousarray(inp["hy_conv_w"][0].reshape(3, 24, 128).transpose(2, 0, 1).reshape(128, 72))
    cbv = np.ascontiguousarray(inp["hy_conv_b"][0].reshape(24, 128).T)
    in_maps = []
    for c in range(NCORES):
        b, q = divmod(c, 4)
        t0 = q * TOK
        halo = np.zeros((128, D), np.float32)
        em = np.zeros((128, 2), np.float32)
        if q > 0:
            halo[0] = x[b, t0 - 1]
            em[:, 0] = 1.0
        if q < 3:
            halo[1] = x[b, t0 + TOK]
            em[:, 1] = 1.0
        in_maps.append({"x_own": np.ascontiguousarray(x[b, t0:t0 + TOK]), "x_halo": halo, "emask": em, "gnorm": g,
                        "ident": ident, "w_in": w_in, "b_in": b_in, "cw": cwv, "cb": cbv})
    res = _run(nc, in_maps, "A")
    return res.results


NFFT = 2 * SEQ
CG = 8
NG = 128 // CG


def fft_consts():
    i128 = np.arange(128, dtype=np.float64)
    i256 = np.arange(256, dtype=np.float64)
    c = {}
    a = 2 * np.pi * np.outer(i128, i256) / 256.0
    c["FA"] = np.concatenate([np.cos(a), -np.sin(a)], 1)
    c["FB"] = np.concatenate([np.sin(a), np.cos(a)], 1)
    t = 2 * np.pi * np.outer(i128, i256) / NFFT
    c["TW"] = np.stack([np.cos(t), -np.sin(t)], 1)
    f = 2 * np.pi * np.outer(i128, i128) / 128.0
    c["F128"] = np.stack([np.cos(f), -np.sin(f), np.sin(f)], 1)
    c["GA"] = np.concatenate([np.cos(f), np.sin(f)], 1)
    c["GB"] = np.concatenate([-np.sin(f), np.cos(f)], 1)
    k1 = (128 * np.arange(2)[None, :, None] + i128[:, None, None])
    it = 2 * np.pi * k1 * i128[None, None, :] / NFFT
    c["ITW"] = np.stack([np.cos(it), np.sin(it)], 2)
    h = 2 * np.pi * k1 * i128[None, None, :] / 256.0
    c["H"] = np.stack([np.cos(h), np.sin(h), -np.sin(h)], 2)
    pos = 128 * i128[:, None] + i128[None, :]
    tl = np.linspace(0.0, 1.0, SEQ, dtype=np.float32)
    c["NEGT"] = -tl[pos.astype(np.int64)]
    return {k: np.ascontiguousarray(v, dtype=np.float32) for k, v in c.items()}


def filter_feat():
    f32 = np.float32
    L = SEQ
    t = np.linspace(0.0, 1.0, L, dtype=f32)[:, None]
    w = (f32(2.0 * math.pi) * np.arange(L, dtype=f32)[:, None] / f32(L)).astype(f32)
    bands = np.linspace(1e-4, 15, 16, dtype=f32)[None, :]
    bw = (bands * w).astype(f32)
    feat = np.concatenate([t, np.cos(bw), -np.sin(bw)], axis=-1).astype(f32)
    return np.ascontiguousarray(feat.T)


def build_B():
    nc = bass.Bass("TRN2", target_bir_lowering=False)
    S = Sched(nc)
    DT = nc.dram_tensor
    s_in = DT("s_in", [2, 128, SEQ], F32, kind="ExternalInput").ap()
    featT = DT("featT", [33, SEQ], F32, kind="ExternalInput").ap()
    w1d = DT("f_w1", [33, 64], F32, kind="ExternalInput").ap()
    w2d = DT("f_w2", [64, 64], F32, kind="ExternalInput").ap()
    w3d = DT("f_w3", [64, 64], F32, kind="ExternalInput").ap()
    fbd = DT("f_bf", [64, 4], F32, kind="ExternalInput").ap()
    woutd = DT("f_wout", [64, NG * 2 * CG], F32, kind="ExternalInput").ap()
    decd = DT("decay", [1, NG * 2 * CG], F32, kind="ExternalInput").ap()
    cd = {}
    shapes = {"FA": [128, 512], "FB": [128, 512], "TW": [128, 2, 256], "F128": [128, 3, 128], "GA": [128, 256],
              "GB": [128, 256], "ITW": [128, 2, 2, 128], "H": [128, 2, 3, 128], "NEGT": [128, 128]}
    for k, sh in shapes.items():
        cd[k] = DT("c_" + k, sh, F32, kind="ExternalInput").ap()
    y_out = DT("y_out", [2, 128, SEQ], F32, kind="ExternalOutput").ap()

    A = nc.alloc_sbuf_tensor
    B = S.buf
    cf = {k: A("cf_" + k, sh, F32) for k, sh in shapes.items()}
    cb = {k: A("cb_" + k, shapes[k], BF16) for k in ("FA", "FB", "F128", "GA", "GB", "H")}
    b_const = B("const")
    for k in shapes:
        S.dma("sp", cf[k][:], cd[k], writes=[b_const], sem_buf=b_const)
    b_cb = B("constbf")
    for k in cb:
        S.op("pool", lambda e, k=k: e.tensor_copy(out=cb[k][:], in_=cf[k][:]), reads=[b_const], writes=[b_cb])
    w1s, w2s, w3s = A("w1s", [33, 64], F32), A("w2s", [64, 64], F32), A("w3s", [64, 64], F32)
    fb = A("fb", [64, 4], F32)
    fbb = A("fbb", [64, 3], F32)
    wout = A("wout", [64, NG * 2 * CG], F32)
    absdec = A("absdec", [128, NG * 2 * CG], F32)
    ones = A("ones", [128, 128], F32)
    b_fw = B("fw")
    for dst, src in ((w1s, w1d), (w2s, w2d), (w3s, w3d), (fb, fbd), (wout, woutd)):
        S.dma("sp", dst[:], src, writes=[b_fw], sem_buf=b_fw)
    S.dma("sp", absdec[:], decd.broadcast_to([128, NG * 2 * CG]), writes=[b_fw], sem_buf=b_fw)
    b_fw2 = B("fw2")
    S.op("act", lambda e: e.activation(out=absdec[:], in_=absdec[:], func=AF.Abs),
         reads=[b_fw], writes=[b_fw])
    S.op("dve", lambda e: e.tensor_tensor(out=fbb[:], in0=fb[:, 0:3], in1=fb[:, 3:4].broadcast_to([64, 3]), op=ALU.mult),
         reads=[b_fw], writes=[b_fw2])
    S.op("pool", lambda e: e.memset(ones[:], 1.0), writes=[b_fw2])

    h3 = A("h3", [64, SEQ], F32)
    b_h3 = B("h3")
    NR = 3
    ft = [A("ft%d" % i, [33, 512], F32) for i in range(NR)]
    b_ft = [B("ft%d" % i) for i in range(NR)]
    NAR = 3
    arg = [A("arg%d" % i, [64, 512], F32) for i in range(NAR)]
    b_arg = [B("arg%d" % i) for i in range(NAR)]
    NH = 2
    hh = [[A("hh%d_%d" % (l, i), [64, 512], F32) for i in range(NH)] for l in range(2)]
    b_hh = [[B("hh%d_%d" % (l, i)) for i in range(NH)] for l in range(2)]
    NPS = 8
    ps = [nc.alloc_psum_tensor("ps%d" % i, [128, 512], F32) for i in range(NPS)]
    b_ps = [B("ps%d" % i) for i in range(NPS)]
    TWO_PI = 2.0 * math.pi
    cnt = [0]
    fs = A("fs", [64, 1], F32)
    fu = A("fu", [64, 3], F32)
    S.op("dve", lambda e: e.tensor_scalar(out=fs[:], in0=fb[:, 3:4], scalar1=1.0 / TWO_PI, scalar2=None, op0=ALU.mult),
         reads=[b_fw], writes=[b_fw2])
    S.op("dve", lambda e: e.tensor_scalar(out=fu[:], in0=fbb[:], scalar1=1.0 / TWO_PI, scalar2=4.5, op0=ALU.mult, op1=ALU.add),
         reads=[b_fw2], writes=[b_fw2])
    ki = [A("ki%d" % i, [64, 512], mybir.dt.int32) for i in range(NAR)]
    kf = [A("kf%d" % i, [64, 512], F32) for i in range(NAR)]
    negpi = A("negpi", [64, 1], F32)
    S.op("pool", lambda e: e.memset(negpi[:], -3.1415925), writes=[b_fw2])

    def mlp_layer(pi, lhsT, rhs_ap, rhs_buf, li, out_ap, out_buf):
        a = cnt[0] % NAR
        cnt[0] += 1
        S.op("pe", lambda e: e.matmul(ps[pi][0:64, :], lhsT=lhsT, rhs=rhs_ap, start=True, stop=True),
             reads=[b_fw, rhs_buf], writes=[b_ps[pi]])
        S.op("dve", lambda e: e.tensor_scalar(out=arg[a][:], in0=ps[pi][0:64, :], scalar1=fs[:, 0:1], scalar2=fu[:, li:li + 1],
                                              op0=ALU.mult, op1=ALU.add),
             reads=[b_ps[pi], b_fw, b_fw2], writes=[b_arg[a]])
        S.op("dve", lambda e: e.tensor_copy(out=ki[a][:], in_=arg[a][:]), reads=[b_arg[a]], writes=[b_arg[a]])
        S.op("dve", lambda e: e.tensor_copy(out=kf[a][:], in_=ki[a][:]), reads=[b_arg[a]], writes=[b_arg[a]])
        S.op("dve", lambda e: e.tensor_tensor(out=arg[a][:], in0=arg[a][:], in1=kf[a][:], op=ALU.subtract),
             reads=[b_arg[a]], writes=[b_arg[a]])
        S.op("dve", lambda e: e.scalar_tensor_tensor(out=arg[a][:], in0=arg[a][:], scalar=0.0, in1=arg[a][:],
                                                     op0=ALU.is_lt, op1=ALU.add),
             reads=[b_arg[a]], writes=[b_arg[a]])
        S.op("act", lambda e: e.activation(out=out_ap, in_=arg[a][:], func=AF.Sin, bias=negpi[:, 0:1], scale=6.283185),
             reads=[b_arg[a], b_fw2], writes=[out_buf])

    def m1(pg):
        j = pg % NR
        S.dma("sp", ft[j][:], featT[:, pg * 512:(pg + 1) * 512], writes=[b_ft[j]])
        mlp_layer(pg % 2, w1s[:], ft[j][:], b_ft[j], 0, hh[0][pg % NH][:], b_hh[0][pg % NH])

    def m2(pg):
        j = pg % NH
        mlp_layer(2 + pg % 2, w2s[:], hh[0][j][:], b_hh[0][j], 1, hh[1][j][:], b_hh[1][j])

    def m3(pg):
        j = pg % NH
        mlp_layer(4 + pg % 2, w3s[:], hh[1][j][:], b_hh[1][j], 2, h3[:, pg * 512:(pg + 1) * 512], b_h3)

    run_pipeline(list(range(SEQ // 512)), [m1, m2, m3])

    G1f = A("G1f", [128, 2 * CG, 128], F32)
    win = A("win", [128, 128, 2 * CG], F32)
    pm = A("pm", [128, 2, CG, 128], BF16)
    Gh = A("Gh", [128, CG, 2, 256], F32)
    part = A("part", [128, 2 * CG], F32)
    tot = A("tot", [128, 2 * CG], F32)
    rn = A("rn", [128, CG], F32)
    D1f = A("D1f", [128, CG, 2, 128], F32)
    D1b = A("D1b", [128, CG, 2, 128], BF16)
    NBB, NPB, NPBB = 6, 3, 3
    Bb = [A("Bb%d" % i, [128, 2, 256], BF16) for i in range(NBB)]
    P1 = [A("P1_%d" % i, [128, 512], F32) for i in range(NPB)]
    P2 = [A("P2_%d" % i, [128, 512], F32) for i in range(NPB)]
    Pb = [A("Pb%d" % i, [128, 2, 256], BF16) for i in range(NPBB)]
    ring = {}

    def nxt(key, n):
        v = ring.get(key, 0)
        ring[key] = v + 1
        return v % n

    Cb = [A("Cb%d" % i, [128, 2, 2, 4, 128], BF16) for i in range(2)]
    yb = [A("yb%d" % i, [128, 4, 2, 128], F32) for i in range(2)]
    b_G1f, b_win, b_pm, b_Gh, b_part, b_rn = B("G1f"), B("win"), B("pm"), B("Gh"), B("part"), B("rn")
    b_D1f, b_D1b = B("D1f"), B("D1b")
    b_Bb = [B("Bb%d" % i) for i in range(NBB)]
    b_P1 = [B("P1_%d" % i) for i in range(NPB)]
    b_P2 = [B("P2_%d" % i) for i in range(NPB)]
    b_Pb = [B("Pb%d" % i) for i in range(NPBB)]
    b_Cb = [B("Cb0"), B("Cb1")]
    b_yb = [B("yb0"), B("yb1")]
    pctr = [0]
    cctr = [0]

    def next_ps():
        pctr[0] += 1
        return pctr[0] % NPS

    def cmul(psv, t1, t2, o_re, o_im, shp):
        k = cctr[0] % 2
        cctr[0] += 1
        p1 = P1[k][:].rearrange(shp[0], **shp[1])
        p2 = P2[k][:].rearrange(shp[0], **shp[1])
        return k, p1, p2

    def fwd_stage(lhs_re, lhs_im, lhs_buf, bsel):
        pi = next_ps()
        S.op("pe", lambda e: e.matmul(ps[pi][:], lhsT=lhs_re, rhs=cb["FA"][:], start=True, stop=(lhs_im is None)),
             reads=[lhs_buf, b_cb], writes=[b_ps[pi]])
        if lhs_im is not None:
            S.op("pe", lambda e: e.matmul(ps[pi][:], lhsT=lhs_im, rhs=cb["FB"][:], start=False, stop=True),
                 reads=[lhs_buf, b_cb], writes=[b_ps[pi]])
        k = cctr[0] % 2
        cctr[0] += 1
        pv = ps[pi][:].rearrange("p (r k) -> p r k", r=2)
        p1 = P1[k][:].rearrange("p (r k) -> p r k", r=2)
        p2 = P2[k][:].rearrange("p (r k) -> p r k", r=2)
        tre = cf["TW"][:, 0:1, :].broadcast_to([128, 2, 256])
        tim = cf["TW"][:, 1:2, :].broadcast_to([128, 2, 256])
        S.op("dve", lambda e: e.tensor_tensor(out=p1, in0=pv, in1=tre, op=ALU.mult),
             reads=[b_ps[pi], b_const], writes=[b_P[k]])
        S.op("dve", lambda e: e.tensor_tensor(out=p2, in0=pv, in1=tim, op=ALU.mult),
             reads=[b_ps[pi], b_const], writes=[b_P[k]])
        S.op("pool", lambda e: e.tensor_tensor(out=Bb[bsel][:, 0, :], in0=P1[k][:, 0:256], in1=P2[k][:, 256:512], op=ALU.subtract),
             reads=[b_P[k]], writes=[b_Bb[bsel]])
        S.op("pool", lambda e: e.tensor_tensor(out=Bb[bsel][:, 1, :], in0=P2[k][:, 0:256], in1=P1[k][:, 256:512], op=ALU.add),
             reads=[b_P[k]], writes=[b_Bb[bsel]])

    for g in range(NG):
        c0 = g * CG
        gs = slice(g * 2 * CG, (g + 1) * 2 * CG)
        S.op("dve", lambda e, gs=gs: e.tensor_tensor(out=win[:], in0=cf["NEGT"][:].unsqueeze(2).broadcast_to([128, 128, 2 * CG]),
                                                     in1=absdec[:, gs].unsqueeze(1).broadcast_to([128, 128, 2 * CG]), op=ALU.mult),
             reads=[b_const, b_fw], writes=[b_win])
        S.op("act", lambda e: e.activation(out=win[:], in_=win[:], func=AF.Exp), reads=[b_win], writes=[b_win])
        for a16 in range(8):
            pi = next_ps()
            fo = ps[pi][:, 0:16 * 2 * CG].rearrange("p (i s) -> p i s", i=16)
            for i in range(16):
                n2 = a16 * 16 + i
                S.op("pe", lambda e, n2=n2, i=i, pi=pi, gs=gs: e.matmul(ps[pi][:, i * 2 * CG:(i + 1) * 2 * CG], lhsT=h3[:, n2:SEQ:128],
                                                                       rhs=wout[:, gs], start=True, stop=True),
                     reads=[b_h3, b_fw], writes=[b_ps[pi]])
            S.op("dve", lambda e, a16=a16, fo=fo: e.tensor_tensor(
                out=G1f[:, :, a16 * 16:(a16 + 1) * 16].rearrange("p s i -> p i s"), in0=fo,
                in1=win[:, a16 * 16:(a16 + 1) * 16, :], op=ALU.mult),
                 reads=[b_ps[pi], b_win], writes=[b_G1f])
        S.op("pool", lambda e: e.memset(G1f[0:1, CG:2 * CG, 0:1], 0.0), writes=[b_G1f])
        wv = win[:].rearrange("p a s -> p (a s)").rearrange("p (s n) -> p s n", n=128)
        S.op("act", lambda e, wv=wv: e.activation(out=wv, in_=G1f[:], func=AF.Abs),
             reads=[b_G1f], writes=[b_win])
        S.op("dve", lambda e, wv=wv: e.tensor_reduce(out=part[:], in_=wv, axis=AX.X, op=ALU.add),
             reads=[b_win], writes=[b_part])
        pi = next_ps()
        S.op("pe", lambda e, pi=pi: e.matmul(ps[pi][:, 0:2 * CG], lhsT=ones[:], rhs=part[:], start=True, stop=True),
             reads=[b_part, b_fw2], writes=[b_ps[pi]])
        S.op("act", lambda e, pi=pi: e.copy(out=tot[:], in_=ps[pi][:, 0:2 * CG]), reads=[b_ps[pi]], writes=[b_part])
        S.op("dve", lambda e: e.tensor_tensor(out=rn[:], in0=tot[:, 0:CG], in1=tot[:, CG:2 * CG], op=ALU.add),
             reads=[b_part], writes=[b_rn])
        S.op("dve", lambda e: e.tensor_scalar(out=rn[:], in0=rn[:], scalar1=float(NFFT), scalar2=None, op0=ALU.mult),
             reads=[b_rn], writes=[b_rn])
        S.op("dve", lambda e: e.reciprocal(out=rn[:], in_=rn[:]), reads=[b_rn], writes=[b_rn])
        S.op("pool", lambda e: e.tensor_tensor(out=pm[:, 0], in0=G1f[:, 0:CG, :], in1=G1f[:, CG:2 * CG, :], op=ALU.add),
             reads=[b_G1f], writes=[b_pm])
        S.op("pool", lambda e: e.tensor_tensor(out=pm[:, 1], in0=G1f[:, 0:CG, :], in1=G1f[:, CG:2 * CG, :], op=ALU.subtract),
             reads=[b_G1f], writes=[b_pm])
        for b in range(2):
            S.dma("sp", D1f[:, :, b, :], s_in[b, c0:c0 + CG, :].rearrange("c (n1 n2) -> n1 c n2", n2=128),
                  writes=[b_D1f])
        S.op("act", lambda e: e.copy(out=D1b[:], in_=D1f[:]), reads=[b_D1f], writes=[b_D1b])
        F = cb["F128"]
        H = cb["H"]
        tre = cf["TW"][:, 0:1, :].broadcast_to([128, 2, 256])
        tim = cf["TW"][:, 1:2, :].broadcast_to([128, 2, 256])

        def cmul_tw(pi, bsel):
            k = nxt("P", NPB)
            pv = ps[pi][:].rearrange("p (r k) -> p r k", r=2)
            p1 = P1[k][:].rearrange("p (r k) -> p r k", r=2)
            p2 = P2[k][:].rearrange("p (r k) -> p r k", r=2)
            S.op("dve", lambda e: e.tensor_tensor(out=p1, in0=pv, in1=tre, op=ALU.mult), reads=[b_ps[pi], b_const], writes=[b_P1[k]])
            S.op("dve", lambda e: e.tensor_tensor(out=p2, in0=pv, in1=tim, op=ALU.mult), reads=[b_ps[pi], b_const], writes=[b_P2[k]])
            S.op("pool", lambda e: e.tensor_tensor(out=Bb[bsel][:, 0, :], in0=P1[k][:, 0:256], in1=P2[k][:, 256:512], op=ALU.subtract),
                 reads=[b_P1[k], b_P2[k]], writes=[b_Bb[bsel]])
            S.op("pool", lambda e: e.tensor_tensor(out=Bb[bsel][:, 1, :], in0=P2[k][:, 0:256], in1=P1[k][:, 256:512], op=ALU.add),
                 reads=[b_P1[k], b_P2[k]], writes=[b_Bb[bsel]])

        def st1(it):
            if it["kind"] == "f":
                cl = it["cl"]
                it["pa"] = [next_ps(), next_ps()]
                it["bb"] = [nxt("Bb", NBB), nxt("Bb", NBB)]
                for z in range(2):
                    pi = it["pa"][z]
                    S.op("pe", lambda e, pi=pi, z=z, cl=cl: e.matmul(ps[pi][:], lhsT=pm[:, z, cl, :], rhs=cb["FA"][:], start=True, stop=True),
                         reads=[b_pm, b_cb], writes=[b_ps[pi]])
                for z in range(2):
                    cmul_tw(it["pa"][z], it["bb"][z])
            else:
                cl = it["cl"]
                pi = next_ps()
                it["bb"] = [nxt("Bb", NBB)]
                S.op("pe", lambda e, pi=pi, cl=cl: e.matmul(ps[pi][:], lhsT=D1b[:, cl, 0, :], rhs=cb["FA"][:], start=True, stop=False),
                     reads=[b_D1b, b_cb], writes=[b_ps[pi]])
                S.op("pe", lambda e, pi=pi, cl=cl: e.matmul(ps[pi][:], lhsT=D1b[:, cl, 1, :], rhs=cb["FB"][:], start=False, stop=True),
                     reads=[b_D1b, b_cb], writes=[b_ps[pi]])
                cmul_tw(pi, it["bb"][0])

        def st2(it):
            cl = it["cl"]
            pi = next_ps()
            b0 = it["bb"][0]
            b1 = it["bb"][-1]
            S.op("pe", lambda e, pi=pi, b0=b0: e.matmul(ps[pi][:, 0:256], lhsT=F[:, 0, :], rhs=Bb[b0][:, 0, :], start=True, stop=False),
                 reads=[b_Bb[b0], b_cb], writes=[b_ps[pi]])
            S.op("pe", lambda e, pi=pi, b0=b0: e.matmul(ps[pi][:, 0:256], lhsT=F[:, 2, :], rhs=Bb[b0][:, 1, :], start=False, stop=True),
                 reads=[b_Bb[b0], b_cb], writes=[b_ps[pi]])
            S.op("pe", lambda e, pi=pi, b1=b1: e.matmul(ps[pi][:, 256:512], lhsT=F[:, 1, :], rhs=Bb[b1][:, 0, :], start=True, stop=False),
                 reads=[b_Bb[b1], b_cb], writes=[b_ps[pi]])
            S.op("pe", lambda e, pi=pi, b1=b1: e.matmul(ps[pi][:, 256:512], lhsT=F[:, 0, :], rhs=Bb[b1][:, 1, :], start=False, stop=True),
                 reads=[b_Bb[b1], b_cb], writes=[b_ps[pi]])
            if it["kind"] == "f":
                S.op("act", lambda e, pi=pi, cl=cl: e.activation(out=Gh[:, cl].rearrange("p r k -> p (r k)"), in_=ps[pi][:],
                                                                func=AF.Copy, scale=rn[:, cl:cl + 1]),
                     reads=[b_ps[pi], b_rn], writes=[b_Gh])
                return
            k = nxt("P", NPB)
            pk = nxt("Pb", NPBB)
            it["pk"] = pk
            pv = ps[pi][:].rearrange("p (r k) -> p r k", r=2)
            p1 = P1[k][:].rearrange("p (r k) -> p r k", r=2)
            p2 = P2[k][:].rearrange("p (r k) -> p r k", r=2)
            S.op("dve", lambda e, pv=pv, p1=p1, cl=cl: e.tensor_tensor(out=p1, in0=pv, in1=Gh[:, cl, 0:1, :].broadcast_to([128, 2, 256]), op=ALU.mult),
                 reads=[b_ps[pi], b_Gh], writes=[b_P1[k]])
            S.op("dve", lambda e, pv=pv, p2=p2, cl=cl: e.tensor_tensor(out=p2, in0=pv, in1=Gh[:, cl, 1:2, :].broadcast_to([128, 2, 256]), op=ALU.mult),
                 reads=[b_ps[pi], b_Gh], writes=[b_P2[k]])
            S.op("pool", lambda e, k=k, pk=pk: e.tensor_tensor(out=Pb[pk][:, 0, :], in0=P1[k][:, 0:256], in1=P2[k][:, 256:512], op=ALU.subtract),
                 reads=[b_P1[k], b_P2[k]], writes=[b_Pb[pk]])
            S.op("pool", lambda e, k=k, pk=pk: e.tensor_tensor(out=Pb[pk][:, 1, :], in0=P2[k][:, 0:256], in1=P1[k][:, 256:512], op=ALU.add),
                 reads=[b_P1[k], b_P2[k]], writes=[b_Pb[pk]])

        def st3(it):
            if it["kind"] == "f":
                return
            cl, pk = it["cl"], it["pk"]
            q4, ci = divmod(cl, 4)
            cbi = (g * (CG // 4) + q4) % 2
            pi2 = next_ps()
            for j in range(2):
                S.op("pe", lambda e, pi2=pi2, j=j, pk=pk: e.matmul(ps[pi2][:, j * 256:(j + 1) * 256], lhsT=Pb[pk][:, 0, j * 128:(j + 1) * 128],
                                                                 rhs=cb["GA"][:], start=True, stop=False),
                     reads=[b_Pb[pk], b_cb], writes=[b_ps[pi2]])
                S.op("pe", lambda e, pi2=pi2, j=j, pk=pk: e.matmul(ps[pi2][:, j * 256:(j + 1) * 256], lhsT=Pb[pk][:, 1, j * 128:(j + 1) * 128],
                                                                 rhs=cb["GB"][:], start=False, stop=True),
                     reads=[b_Pb[pk], b_cb], writes=[b_ps[pi2]])
            k = nxt("P", NPB)
            cv = ps[pi2][:].rearrange("p (j r n) -> p j r n", j=2, r=2)
            p1 = P1[k][:].rearrange("p (j r n) -> p j r n", j=2, r=2)
            p2 = P2[k][:].rearrange("p (j r n) -> p j r n", j=2, r=2)
            S.op("dve", lambda e, cv=cv, p1=p1: e.tensor_tensor(out=p1, in0=cv, in1=cf["ITW"][:, :, 0:1, :].broadcast_to([128, 2, 2, 128]), op=ALU.mult),
                 reads=[b_ps[pi2], b_const], writes=[b_P1[k]])
            S.op("dve", lambda e, cv=cv, p2=p2: e.tensor_tensor(out=p2, in0=cv, in1=cf["ITW"][:, :, 1:2, :].broadcast_to([128, 2, 2, 128]), op=ALU.mult),
                 reads=[b_ps[pi2], b_const], writes=[b_P2[k]])
            S.op("pool", lambda e, p1=p1, p2=p2, ci=ci, cbi=cbi: e.tensor_tensor(out=Cb[cbi][:, :, 0, ci, :], in0=p1[:, :, 0, :], in1=p2[:, :, 1, :], op=ALU.subtract),
                 reads=[b_P1[k], b_P2[k]], writes=[b_Cb[cbi]])
            S.op("pool", lambda e, p1=p1, p2=p2, ci=ci, cbi=cbi: e.tensor_tensor(out=Cb[cbi][:, :, 1, ci, :], in0=p2[:, :, 0, :], in1=p1[:, :, 1, :], op=ALU.add),
                 reads=[b_P1[k], b_P2[k]], writes=[b_Cb[cbi]])
            if ci != 3:
                return
            pr, pim = next_ps(), next_ps()
            seq = [(pr, 0, 0, True), (pr, 2, 1, False), (pim, 1, 0, True), (pim, 0, 1, False)]
            for (pp, hsel, ri, first) in seq:
                for j in range(2):
                    S.op("pe", lambda e, pp=pp, hsel=hsel, ri=ri, j=j, first=first, cbi=cbi: e.matmul(
                        ps[pp][:], lhsT=H[:, j, hsel, :], rhs=Cb[cbi][:, j, ri, :, :].rearrange("p c n -> p (c n)"),
                        start=(first and j == 0), stop=((not first) and j == 1)),
                         reads=[b_Cb[cbi], b_cb], writes=[b_ps[pp]])
            S.op("act", lambda e, pr=pr, cbi=cbi: e.copy(out=yb[cbi][:, :, 0, :], in_=ps[pr][:].rearrange("p (c n) -> p c n", c=4)),
                 reads=[b_ps[pr]], writes=[b_yb[cbi]])
            S.op("act", lambda e, pim=pim, cbi=cbi: e.copy(out=yb[cbi][:, :, 1, :], in_=ps[pim][:].rearrange("p (c n) -> p c n", c=4)),
                 reads=[b_ps[pim]], writes=[b_yb[cbi]])
            for b in range(2):
                cc = c0 + q4 * 4
                S.dma("sp", y_out[b, cc:cc + 4, :].rearrange("c (n1 n2) -> n1 c n2", n2=128), yb[cbi][:, :, b, :],
                      reads=[b_yb[cbi]], is_output=True)

        items = [{"kind": "f", "cl": cl} for cl in range(CG)] + [{"kind": "d", "cl": cl} for cl in range(CG)]
        run_pipeline(items, [st1, st2, st3])
    S.finish()
    return nc


def run_B(inp, s_cs):
    nc = build_B()
    consts = fft_consts()
    featT = filter_feat()
    fbf = np.stack([inp["hy_f_b1"][0], inp["hy_f_b2"][0], inp["hy_f_b3"][0], inp["hy_f_freq"][0]], 1).astype(np.float32)
    in_maps = []
    for c in range(NCORES):
        wo = inp["hy_f_wout"][0].reshape(64, 2, 8, NG, CG)[:, :, c]
        wo = np.ascontiguousarray(wo.transpose(0, 2, 1, 3).reshape(64, NG * 2 * CG))
        de = inp["hy_decay"][0].reshape(2, 8, NG, CG)[:, c]
        de = np.ascontiguousarray(de.transpose(1, 0, 2).reshape(1, NG * 2 * CG))
        m = {"s_in": np.ascontiguousarray(s_cs[c]), "featT": featT,
             "f_w1": np.ascontiguousarray(inp["hy_f_w1"][0]), "f_w2": np.ascontiguousarray(inp["hy_f_w2"][0]),
             "f_w3": np.ascontiguousarray(inp["hy_f_w3"][0]), "f_bf": np.ascontiguousarray(fbf),
             "f_wout": wo, "decay": de}
        for k, v in consts.items():
            m["c_" + k] = v
        in_maps.append(m)
    res = _run(nc, in_maps, "B")
    return res.results


def load_weight_bf16(S, nc, w_dram, rows, cols, name, bufname, qeng="pool", ceng="pool", stage=None, scols=1024):
    nk = rows // 128
    wt = nc.alloc_sbuf_tensor(name + "_sb", [128, nk, cols], BF16)
    b_w = S.buf(bufname)
    if stage is None:
        st = [nc.alloc_sbuf_tensor(name + "_st%d" % i, [128, scols], F32) for i in range(2)]
        b_st = [S.buf(name + "_st0"), S.buf(name + "_st1")]
        stage = (st, b_st, [0], scols)
    st, b_st, ctr = stage[0], stage[1], stage[2]
    scols = stage[3] if len(stage) > 3 else 1024
    for k in range(nk):
        for c0 in range(0, cols, scols):
            cw = min(scols, cols - c0)
            i = ctr[0] % 2
            ctr[0] += 1
            S.dma(qeng, st[i][:, 0:cw], w_dram[k * 128:(k + 1) * 128, c0:c0 + cw], writes=[b_st[i]])
            ce = ceng if isinstance(ceng, str) else ceng[ctr[0] % len(ceng)]
            if ce == "act":
                S.op("act", lambda e, i=i, k=k, c0=c0, cw=cw: e.copy(out=wt[:, k, c0:c0 + cw], in_=st[i][:, 0:cw]),
                     reads=[b_st[i]], writes=[b_w])
            else:
                S.op(ce, lambda e, i=i, k=k, c0=c0, cw=cw: e.tensor_copy(out=wt[:, k, c0:c0 + cw], in_=st[i][:, 0:cw]),
                     reads=[b_st[i]], writes=[b_w])
    return wt, b_w, stage


class NormT:
    def __init__(self, S, nc, gt, b_gt, idb, b_idb, tag, ntp=2, alias_sq=False):
        A = nc.alloc_sbuf_tensor
        self.S, self.nc = S, nc
        self.ntp = ntp
        self.alias_sq = alias_sq
        self.gt, self.b_gt, self.idb, self.b_idb = gt, b_gt, idb, b_idb
        self.hb = [A(tag + "hb%d" % i, [128, D], BF16) for i in range(2)]
        self.b_hb = [S.buf(tag + "hb0"), S.buf(tag + "hb1")]
        if not alias_sq:
            self.sq = A(tag + "sq", [128, D], F32)
            self.b_sq = S.buf(tag + "sq")
        self.ss = [A(tag + "ss%d" % i, [128, 1], F32) for i in range(2)]
        self.rs = [A(tag + "rs%d" % i, [128, 1], F32) for i in range(2)]
        self.b_s = [S.buf(tag + "s0"), S.buf(tag + "s1")]
        self.tp = [nc.alloc_psum_tensor(tag + "tp%d" % i, [128, 8 * 128], BF16) for i in range(ntp)]
        self.b_tp = [S.buf(tag + "tp%d" % i) for i in range(ntp)]
        self.n = 0

    def rstd(self, x_ap, b_x, j):
        S = self.S
        ss, rs = self.ss[j], self.rs[j]
        sq, b_sq = (self.hb[j], self.b_hb[j]) if self.alias_sq else (self.sq, self.b_sq)
        S.op("act", lambda e: e.activation(out=sq[:], in_=x_ap, func=AF.Square, accum_out=ss[:]),
             reads=[b_x], writes=[b_sq, self.b_s[j]])
        S.op("act", lambda e: e.activation(out=rs[:], in_=ss[:], func=AF.Sqrt, scale=1.0 / D, bias=EPS),
             reads=[self.b_s[j]], writes=[self.b_s[j]])
        S.op("dve", lambda e: e.reciprocal(out=rs[:], in_=rs[:]), reads=[self.b_s[j]], writes=[self.b_s[j]])
        return rs

    def __call__(self, x_ap, b_x, hT_dst, b_hT):
        S = self.S
        j = self.n % 2
        self.n += 1
        rs = self.rstd(x_ap, b_x, j)
        hb, tp = self.hb[j], self.tp[j % self.ntp]
        b_tp = self.b_tp[j % self.ntp]
        gt, idb = self.gt, self.idb
        S.op("dve", lambda e: e.scalar_tensor_tensor(out=hb[:], in0=x_ap, scalar=rs[:, 0:1], in1=gt[:], op0=ALU.mult, op1=ALU.mult),
             reads=[b_x, self.b_s[j], self.b_gt], writes=[self.b_hb[j]])
        for k in range(8):
            S.op("pe", lambda e, k=k: e.transpose(out=tp[:, k * 128:(k + 1) * 128], in_=hb[:, k * 128:(k + 1) * 128], identity=idb[:]),
                 reads=[self.b_hb[j], self.b_idb], writes=[b_tp])
        S.op("act", lambda e: e.copy(out=hT_dst, in_=tp[:].rearrange("p (k t) -> p k t", k=8)),
             reads=[b_tp], writes=[b_hT])


def load_consts_common(S, nc, gnorm_d, ident_d):
    A = nc.alloc_sbuf_tensor
    gt = A("gt", [128, D], F32)
    idf = A("idf", [128, 128], F32)
    idb = A("idb", [128, 128], BF16)
    b_gt, b_idf, b_idb = S.buf("gt"), S.buf("idf"), S.buf("idb")
    S.dma("sp", gt[:], gnorm_d, writes=[b_gt])
    S.dma("sp", idf[:], ident_d, writes=[b_idf])
    S.op("dve", lambda e: e.tensor_copy(out=idb[:], in_=idf[:]), reads=[b_idf], writes=[b_idb])
    return gt, b_gt, idb, b_idb


FBLK = 512


def phase_F(nc, S, io, ntok, final_norm):
    x_in, gnorm, ident_d, wg_d, wu_d, wd_d, x_out = (io[k] for k in ("x_in", "gnorm", "ident", "wg", "wu", "wd", "x_out"))
    gfin_d = io.get("gfin")
    A = nc.alloc_sbuf_tensor
    gt, b_gt, idb, b_idb = load_consts_common(S, nc, gnorm, ident_d)
    if final_norm:
        gf = A("gf", [128, D], F32)
        b_gf = S.buf("gf")
        S.dma("sp", gf[:], gfin_d, writes=[b_gf])
    wg, b_wg, stg = load_weight_bf16(S, nc, wg_d, D, FF, "wg", "wg", ceng=("pool", "act"), scols=512)
    wu, b_wu, stg = load_weight_bf16(S, nc, wu_d, D, FF, "wu", "wu", ceng=("pool", "act"), stage=stg)
    wd, b_wd, stg = load_weight_bf16(S, nc, wd_d, FF, D, "wd", "wd", ceng=("pool", "act"), stage=stg)
    NF = FF // 128
    nt = FBLK // 128
    NXT = nt
    xt = [A("xt%d" % i, [128, D], F32) for i in range(NXT)]
    b_xt = [S.buf("xt%d" % i) for i in range(NXT)]
    hT = A("hT", [128, 8, FBLK], BF16)
    b_hT = S.buf("hT")
    aT = A("aT", [128, NF, FBLK], BF16)
    b_aT = S.buf("aT")
    sg = [A("sg%d" % i, [128, FBLK], F32) for i in range(2)]
    b_sg = [S.buf("sg0"), S.buf("sg1")]
    norm = NormT(S, nc, gt, b_gt, idb, b_idb, "n", alias_sq=True)
    gp = [nc.alloc_psum_tensor("gp%d" % i, [128, 512], F32) for i in range(4)]
    b_gp = [S.buf("gp%d" % i) for i in range(4)]
    dp = [nc.alloc_psum_tensor("dp%d" % i, [128, 512], F32) for i in range(2)]
    b_dp = [S.buf("dp0"), S.buf("dp1")]
    tctr = 0
    for blk in range(ntok // FBLK):
        slots = []
        for t in range(nt):
            tok0 = blk * FBLK + t * 128
            sl = tctr % NXT
            tctr += 1
            slots.append(sl)
            S.dma("sp", xt[sl][:], x_in[tok0:tok0 + 128, :], writes=[b_xt[sl]])
            norm(xt[sl][:], b_xt[sl], hT[:, :, t * 128:(t + 1) * 128], b_hT)
        for f in range(NF):
            g_i, u_i = (2 * f) % 4, (2 * f + 1) % 4
            for k in range(8):
                S.op("pe", lambda e, f=f, k=k, g_i=g_i: e.matmul(gp[g_i][:, 0:FBLK], lhsT=wg[:, k, f * 128:(f + 1) * 128], rhs=hT[:, k, :],
                                                                start=(k == 0), stop=(k == 7)),
                     reads=[b_wg, b_hT], writes=[b_gp[g_i]])
            for k in range(8):
                S.op("pe", lambda e, f=f, k=k, u_i=u_i: e.matmul(gp[u_i][:, 0:FBLK], lhsT=wu[:, k, f * 128:(f + 1) * 128], rhs=hT[:, k, :],
                                                                start=(k == 0), stop=(k == 7)),
                     reads=[b_wu, b_hT], writes=[b_gp[u_i]])
            si = f % 2
            S.op("act", lambda e, g_i=g_i, si=si: e.activation(out=sg[si][:], in_=gp[g_i][:, 0:FBLK], func=AF.Silu),
                 reads=[b_gp[g_i]], writes=[b_sg[si]])
            S.op("dve", lambda e, f=f, u_i=u_i, si=si: e.tensor_tensor(out=aT[:, f, :], in0=sg[si][:], in1=gp[u_i][:, 0:FBLK], op=ALU.mult),
                 reads=[b_sg[si], b_gp[u_i]], writes=[b_aT])
        for t in range(nt):
            tok0 = blk * FBLK + t * 128
            sl = slots[t]
            for hf in range(2):
                for f in range(NF):
                    S.op("pe", lambda e, f=f, hf=hf, t=t: e.matmul(dp[hf][:], lhsT=aT[:, f, t * 128:(t + 1) * 128], rhs=wd[:, f, hf * 512:(hf + 1) * 512],
                                                                  start=(f == 0), stop=(f == NF - 1)),
                         reads=[b_aT, b_wd], writes=[b_dp[hf]])
            for hf in range(2):
                S.op("dve", lambda e, hf=hf, sl=sl: e.tensor_tensor(out=xt[sl][:, hf * 512:(hf + 1) * 512], in0=dp[hf][:],
                                                                   in1=xt[sl][:, hf * 512:(hf + 1) * 512], op=ALU.add),
                     reads=[b_dp[hf], b_xt[sl]], writes=[b_xt[sl]])
            if final_norm:
                o = t % 2
                rs = norm.rstd(xt[sl][:], b_xt[sl], o)
                S.op("dve", lambda e, sl=sl, rs=rs: e.scalar_tensor_tensor(out=xt[sl][:], in0=xt[sl][:], scalar=rs[:, 0:1], in1=gf[:],
                                                                          op0=ALU.mult, op1=ALU.mult),
                     reads=[b_xt[sl], norm.b_s[o], b_gf], writes=[b_xt[sl]])
            S.dma("sp", x_out[tok0:tok0 + 128, :], xt[sl][:], reads=[b_xt[sl]], is_output=True)


def build_F(final_norm):
    nc = bass.Bass("TRN2", target_bir_lowering=False)
    S = Sched(nc)
    DT = nc.dram_tensor
    io = {"x_in": DT("x_in", [TOK, D], F32, kind="ExternalInput").ap(),
          "gnorm": DT("gnorm", [128, D], F32, kind="ExternalInput").ap(),
          "ident": DT("ident", [128, 128], F32, kind="ExternalInput").ap(),
          "wg": DT("wg", [D, FF], F32, kind="ExternalInput").ap(),
          "wu": DT("wu", [D, FF], F32, kind="ExternalInput").ap(),
          "wd": DT("wd", [FF, D], F32, kind="ExternalInput").ap()}
    if final_norm:
        io["gfin"] = DT("gfin", [128, D], F32, kind="ExternalInput").ap()
    io["x_out"] = DT("x_out", [TOK, D], F32, kind="ExternalOutput").ap()
    phase_F(nc, S, io, TOK, final_norm)
    S.finish()
    return nc


def rep_rows(v):
    return np.ascontiguousarray(np.broadcast_to(np.asarray(v, np.float32)[None, :], (128, v.shape[0])))


def run_F(inp, layer, x_tok, final_norm):
    nc = build_F(final_norm)
    ident = np.eye(128, dtype=np.float32)
    base = {"gnorm": rep_rows(inp["norm_ffn"][layer]), "ident": ident,
            "wg": np.ascontiguousarray(inp["ffn_w_gate"][layer]), "wu": np.ascontiguousarray(inp["ffn_w_up"][layer]),
            "wd": np.ascontiguousarray(inp["ffn_w_down"][layer])}
    if final_norm:
        base["gfin"] = rep_rows(inp["norm_final"])
    in_maps = [dict(base, x_in=np.ascontiguousarray(x_tok[c])) for c in range(NCORES)]
    res = _run(nc, in_maps, "F")
    return [r["x_out"] for r in res.results]


def phase_C(nc, S, io, ntok):
    x_in, yT, sT, x0T, skip_d, wo_d, bo_d, x_out = (io[k] for k in ("x_in", "yT", "sT", "x0T", "skip", "w_out", "b_out", "x_out"))
    A = nc.alloc_sbuf_tensor
    wo, b_wo, _ = load_weight_bf16(S, nc, wo_d, D, D, "wo", "wo")
    skip = A("skip_sb", [128, 8], F32)
    bof = A("bof", [1, D], F32)
    bob = A("bob", [1, D], BF16)
    onesb = A("onesb", [1, 128], BF16)
    b_c = S.buf("c")
    S.dma("sp", skip[:], skip_d, writes=[b_c], sem_buf=b_c)
    S.dma("sp", bof[:], bo_d, writes=[b_c], sem_buf=b_c)
    b_c2 = S.buf("c2")
    S.op("dve", lambda e: e.tensor_copy(out=bob[:], in_=bof[:]), reads=[b_c], writes=[b_c2])
    S.op("pool", lambda e: e.memset(onesb[:], 1.0), writes=[b_c2])
    BL = 512
    yt = [A("yt%d" % i, [128, BL], F32) for i in range(2)]
    st_ = [A("st%d" % i, [128, BL], F32) for i in range(2)]
    x0t = [A("x0t%d" % i, [128, BL], F32) for i in range(2)]
    b_in = [S.buf("in0"), S.buf("in1")]
    tmp = [A("tmp%d" % i, [128, BL], F32) for i in range(2)]
    b_tmp = [S.buf("tmp0"), S.buf("tmp1")]
    uT = [A("uT%d" % i, [128, 8, BL], BF16) for i in range(2)]
    b_uT = [S.buf("uT0"), S.buf("uT1")]
    xt = [A("xt%d" % i, [128, D], F32) for i in range(2)]
    b_xt = [S.buf("xt0"), S.buf("xt1")]
    ot = [A("ot%d" % i, [128, D], F32) for i in range(2)]
    b_ot = [S.buf("ot0"), S.buf("ot1")]
    mp = [nc.alloc_psum_tensor("mp%d" % i, [128, 512], F32) for i in range(4)]
    b_mp = [S.buf("mp%d" % i) for i in range(4)]
    n = 0
    tc_ = 0
    for blk in range(ntok // BL):
        ub = blk % 2
        cs = slice(blk * BL, (blk + 1) * BL)
        for k in range(8):
            i = n % 2
            n += 1
            rs_ = slice(k * 128, (k + 1) * 128)
            S.dma("sp", yt[i][:], yT[rs_, cs], writes=[b_in[i]], sem_buf=b_in[i])
            S.dma("sp", st_[i][:], sT[rs_, cs], writes=[b_in[i]], sem_buf=b_in[i])
            S.dma("sp", x0t[i][:], x0T[rs_, cs], writes=[b_in[i]], sem_buf=b_in[i])
            S.op("dve", lambda e, i=i, k=k: e.scalar_tensor_tensor(out=tmp[i][:], in0=st_[i][:], scalar=skip[:, k:k + 1], in1=yt[i][:],
                                                                  op0=ALU.mult, op1=ALU.add),
                 reads=[b_in[i], b_c], writes=[b_tmp[i]])
            S.op("pool", lambda e, i=i, k=k, ub=ub: e.tensor_tensor(out=uT[ub][:, k, :], in0=tmp[i][:], in1=x0t[i][:], op=ALU.mult),
                 reads=[b_tmp[i], b_in[i]], writes=[b_uT[ub]])
        for t in range(BL // 128):
            tok0 = blk * BL + t * 128
            j = tc_ % 2
            tc_ += 1
            S.dma("sp", xt[j][:], x_in[tok0:tok0 + 128, :], writes=[b_xt[j]])
            for hf in range(2):
                m = 2 * j + hf
                for k in range(8):
                    S.op("pe", lambda e, m=m, k=k, hf=hf, t=t, ub=ub: e.matmul(mp[m][:], lhsT=uT[ub][:, k, t * 128:(t + 1) * 128],
                                                                              rhs=wo[:, k, hf * 512:(hf + 1) * 512], start=(k == 0), stop=False),
                         reads=[b_uT[ub], b_wo], writes=[b_mp[m]])
                S.op("pe", lambda e, m=m, hf=hf: e.matmul(mp[m][:], lhsT=onesb[:], rhs=bob[:, hf * 512:(hf + 1) * 512], start=False, stop=True),
                     reads=[b_c2], writes=[b_mp[m]])
                S.op("dve", lambda e, m=m, hf=hf, j=j: e.tensor_tensor(out=ot[j][:, hf * 512:(hf + 1) * 512], in0=mp[m][:],
                                                                      in1=xt[j][:, hf * 512:(hf + 1) * 512], op=ALU.add),
                     reads=[b_mp[m], b_xt[j]], writes=[b_ot[j]])
            S.dma("sp", x_out[tok0:tok0 + 128, :], ot[j][:], reads=[b_ot[j]], is_output=True)


def build_C():
    nc = bass.Bass("TRN2", target_bir_lowering=False)
    S = Sched(nc)
    DT = nc.dram_tensor
    io = {"x_in": DT("x_in", [TOK, D], F32, kind="ExternalInput").ap(),
          "yT": DT("yT", [D, TOK], F32, kind="ExternalInput").ap(),
          "sT": DT("sT", [D, TOK], F32, kind="ExternalInput").ap(),
          "x0T": DT("x0T", [D, TOK], F32, kind="ExternalInput").ap(),
          "skip": DT("skip", [128, 8], F32, kind="ExternalInput").ap(),
          "w_out": DT("w_out", [D, D], F32, kind="ExternalInput").ap(),
          "b_out": DT("b_out", [1, D], F32, kind="ExternalInput").ap(),
          "x_out": DT("x_out", [TOK, D], F32, kind="ExternalOutput").ap()}
    phase_C(nc, S, io, TOK)
    S.finish()
    return nc


def run_C(inp, x_tok, yT, sT, x0T):
    nc = build_C()
    base = {"skip": np.ascontiguousarray(inp["hy_skip"][0].reshape(8, 128).T), "w_out": np.ascontiguousarray(inp["hy_w_out"][0]),
            "b_out": np.ascontiguousarray(inp["hy_b_out"][0][None, :])}
    in_maps = [dict(base, x_in=np.ascontiguousarray(x_tok[c]), yT=np.ascontiguousarray(yT[c]), sT=np.ascontiguousarray(sT[c]),
                    x0T=np.ascontiguousarray(x0T[c])) for c in range(NCORES)]
    res = _run(nc, in_maps, "C")
    return [r["x_out"] for r in res.results]


NROWS_LOC = 72
NEG = -30000.0


def na_tables(rpb):
    H = 16
    j = np.arange(64)
    w = np.arange(64)
    cs = np.clip(w - 8, 0, 48)
    colok = (j[:, None] >= cs[None, :]) & (j[:, None] < cs[None, :] + 16)
    coff = np.clip(j[:, None] - w[None, :] + 15, 0, 30)
    out = np.full((2, 64, H, 7, 2, 64), NEG, np.float32)
    for idx in range(7):
        d0 = -6 + 2 * idx
        for i2 in range(2):
            for q2 in range(2):
                dl = d0 + i2 - q2
                if abs(dl) > 7:
                    continue
                g = rpb[:, dl + 7, :][:, coff]
                g = np.where(colok[None], g, np.float32(NEG))
                out[i2, :, :, idx, q2, :] = g.transpose(1, 0, 2)
    return np.ascontiguousarray(out.reshape(128, H, 7, 128))


def na_mlist(p):
    if p == 0:
        return list(range(0, 6))
    if p == 31:
        return list(range(-1, 5))
    return list(range(0, 5))


def na_rowmask(q):
    R0 = 64 * q
    m = np.zeros((128, 32, 6, 2), np.float32)
    for p in range(32):
        for mi, mm in enumerate(na_mlist(p)):
            for i2 in range(2):
                for q2 in range(2):
                    gr = R0 + 2 * p + q2
                    rs = min(max(gr - 4, 0), 248)
                    kr = R0 - 4 + 2 * p + 2 * mm + i2
                    if rs <= kr < rs + 8:
                        m[i2 * 64:(i2 + 1) * 64, p, mi, q2] = 1.0
    return m


def phase_D(nc, S, io):
    xe, gnorm, ident_d, wqkv_d, bqk_d, bv_d, wo_d, bo_d, bt_d, rm_d, x_out = (io[k] for k in (
        "xe", "gnorm", "ident", "w_qkv", "b_qk", "b_v", "w_o", "b_o", "bt", "rowmask", "x_out"))
    A = nc.alloc_sbuf_tensor
    B = S.buf
    gt, b_gt, idb, b_idb = load_consts_common(S, nc, gnorm, ident_d)
    st = [A("wst%d" % i, [128, 1024], F32) for i in range(2)]
    stage = (st, [B("wst0"), B("wst1")], [0])
    wqkv, b_wqkv, stage = load_weight_bf16(S, nc, wqkv_d, D, 3 * D, "wqkv", "wqkv", ceng=("pool", "act"), stage=stage)
    wo, b_wo, stage = load_weight_bf16(S, nc, wo_d, D, D, "wo", "wo", ceng=("pool", "act"), stage=stage)
    BTb = A("BTb", [128, 16, 7, 128], BF16)
    b_BT = B("BT")
    st_, b_st, ctr = stage
    for h in range(16):
        i = ctr[0] % 2
        ctr[0] += 1
        S.dma("pool", st_[i][:, 0:896], bt_d[:, h].rearrange("p a b -> p (a b)"), writes=[b_st[i]])
        S.op("pool", lambda e, i=i, h=h: e.tensor_copy(out=BTb[:, h].rearrange("p a b -> p (a b)"), in_=st_[i][:, 0:896]),
             reads=[b_st[i]], writes=[b_BT])
    rmask = A("rmask", [128, 32, 6, 2], F32)
    bqk = A("bqk", [128, 16], F32)
    bq8 = A("bq8", [128, 8], F32)
    bvb = A("bvb", [1, D], BF16)
    bob = A("bob", [1, D], BF16)
    onesr = A("onesr", [1, 128], BF16)
    onesc = A("onesc", [128, 64], BF16)
    b_c, b_c2 = B("c"), B("c2")
    for dst, src in ((rmask, rm_d), (bqk, bqk_d)):
        S.dma("sp", dst[:], src, writes=[b_c], sem_buf=b_c)
    S.op("dve", lambda e: e.tensor_scalar(out=bq8[:], in0=bqk[:, 0:8], scalar1=0.125, scalar2=None, op0=ALU.mult), reads=[b_c], writes=[b_c2])
    for dstb, src in ((bvb, bv_d), (bob, bo_d)):
        i = ctr[0] % 2
        ctr[0] += 1
        S.dma("pool", st_[i][0:1, :], src, writes=[b_st[i]])
        S.op("pool", lambda e, i=i, dstb=dstb: e.tensor_copy(out=dstb[:], in_=st_[i][0:1, :]), reads=[b_st[i]], writes=[b_c2])
    S.op("pool", lambda e: e.memset(onesr[:], 1.0), writes=[b_c2])
    S.op("pool", lambda e: e.memset(onesc[:], 1.0), writes=[b_c2])

    KT = [A("KT%d" % i, [128, 8, 512], BF16) for i in range(2)]
    VV = [A("VV%d" % i, [128, 4, D], BF16) for i in range(2)]
    QQ = [A("QQ%d" % i, [128, 8, 512], BF16) for i in range(2)]
    b_KT, b_VV, b_QQ = [B("KT0"), B("KT1")], [B("VV0"), B("VV1")], [B("QQ0"), B("QQ1")]
    hT = A("hT", [128, 8, 512], BF16)
    b_hT = B("hT")
    aT = A("aT", [128, 8, 512], BF16)
    b_aT = B("aT")
    xt = [A("xt%d" % i, [128, D], F32) for i in range(2)]
    b_xt = [B("xt0"), B("xt1")]
    ot = [A("ot0", [128, D], F32)]
    b_ot = [B("ot0")]
    NEB = 3
    Eb = [A("Eb%d" % i, [128, 512], BF16) for i in range(NEB)]
    b_Eb = [B("Eb%d" % i) for i in range(NEB)]
    rz = [A("rz%d" % i, [128, 128], F32) for i in range(2)]
    b_rz = [B("rz0"), B("rz1")]
    norm = NormT(S, nc, gt, b_gt, idb, b_idb, "n", ntp=1)
    pp = [nc.alloc_psum_tensor("pp%d" % i, [128, 512], F32) for i in range(2)]
    b_pp = [B("pp0"), B("pp1")]
    sp_ = [nc.alloc_psum_tensor("sps%d" % i, [128, 512], F32) for i in range(3)]
    b_sp = [B("sps%d" % i) for i in range(3)]
    obank = nc.alloc_psum_tensor("obank", [128, 512], F32)
    zbank = nc.alloc_psum_tensor("zbank", [128, 512], F32)
    b_oz = [B("oz0"), B("oz1")]
    cnt = {"pp": 0, "sp": 0, "e": 0, "oz": 0, "x": 0, "o": 0}

    def nxt(k, n):
        v = cnt[k] % n
        cnt[k] += 1
        return v

    def project(kb):
        rb = kb % 2
        for t in range(4):
            j = nxt("x", 2)
            tok0 = kb * 512 + t * 128
            S.dma("sp", xt[j][:], xe[tok0:tok0 + 128, :], writes=[b_xt[j]])
            norm(xt[j][:], b_xt[j], hT[:, :, t * 128:(t + 1) * 128], b_hT)
        for c in range(8):
            pi = nxt("pp", 2)
            for k in range(8):
                S.op("pe", lambda e, pi=pi, k=k, c=c: e.matmul(pp[pi][:], lhsT=wqkv[:, k, D + c * 128:D + (c + 1) * 128], rhs=hT[:, k, :],
                                                              start=(k == 0), stop=(k == 7)),
                     reads=[b_wqkv, b_hT], writes=[b_pp[pi]])
            S.op("act", lambda e, pi=pi, c=c, rb=rb: e.activation(out=KT[rb][:, c, :], in_=pp[pi][:], func=AF.Identity,
                                                                 bias=bqk[:, 8 + c:9 + c], scale=1.0),
                 reads=[b_pp[pi], b_c], writes=[b_KT[rb]])
            pi = nxt("pp", 2)
            for k in range(8):
                S.op("pe", lambda e, pi=pi, k=k, c=c: e.matmul(pp[pi][:], lhsT=wqkv[:, k, c * 128:(c + 1) * 128], rhs=hT[:, k, :],
                                                              start=(k == 0), stop=(k == 7)),
                     reads=[b_wqkv, b_hT], writes=[b_pp[pi]])
            if kb >= 1:
                S.op("act", lambda e, pi=pi, c=c, kb=kb: e.activation(out=QQ[(kb - 1) % 2][:, c, 256:512], in_=pp[pi][:, 0:256], func=AF.Identity,
                                                                     bias=bq8[:, c:c + 1], scale=0.125),
                     reads=[b_pp[pi], b_c2], writes=[b_QQ[(kb - 1) % 2]])
            if kb <= 7:
                S.op("act", lambda e, pi=pi, c=c, kb=kb: e.activation(out=QQ[kb % 2][:, c, 0:256], in_=pp[pi][:, 256:512], func=AF.Identity,
                                                                     bias=bq8[:, c:c + 1], scale=0.125),
                     reads=[b_pp[pi], b_c2], writes=[b_QQ[kb % 2]])
        for t in range(4):
            for hf in range(2):
                pi = nxt("pp", 2)
                for k in range(8):
                    S.op("pe", lambda e, pi=pi, k=k, t=t, hf=hf: e.matmul(pp[pi][:], lhsT=hT[:, k, t * 128:(t + 1) * 128],
                                                                         rhs=wqkv[:, k, 2 * D + hf * 512:2 * D + (hf + 1) * 512],
                                                                         start=(k == 0), stop=False),
                         reads=[b_wqkv, b_hT], writes=[b_pp[pi]])
                S.op("pe", lambda e, pi=pi, hf=hf: e.matmul(pp[pi][:], lhsT=onesr[:], rhs=bvb[:, hf * 512:(hf + 1) * 512], start=False, stop=True),
                     reads=[b_c2], writes=[b_pp[pi]])
                S.op("act", lambda e, pi=pi, t=t, hf=hf, rb=rb: e.copy(out=VV[rb][:, t, hf * 512:(hf + 1) * 512], in_=pp[pi][:]),
                     reads=[b_pp[pi]], writes=[b_VV[rb]])

    def attend(bq):
        qb = bq % 2
        units = []
        for pl in range(4):
            p = 4 * bq + pl
            ml = na_mlist(p)
            for c in range(8):
                o_i = nxt("oz", 2)
                for hp in range(2):
                    for gi, grp in enumerate([ml[0:4], ml[4:]]):
                        units.append({"pl": pl, "p": p, "c": c, "hp": hp, "gi": gi, "grp": grp, "o_i": o_i, "nmm": len(ml),
                                      "last": hp == 1 and gi == 1})

        def u1(u):
            pl, c, hp, grp = u["pl"], u["c"], u["hp"], u["grp"]
            h = 2 * c + hp
            hs = slice(hp * 64, (hp + 1) * 64)
            si = nxt("sp", 3)
            u["si"] = si
            for mi, mm in enumerate(grp):
                blk, tl = bq + (pl + mm) // 4, (pl + mm) % 4
                S.op("pe", lambda e, si=si, mi=mi, h=h, mm=mm: e.matmul(sp_[si][:, mi * 128:(mi + 1) * 128], lhsT=idb[:], rhs=BTb[:, h, mm + 1, :],
                                                                      start=True, stop=False),
                     reads=[b_idb, b_BT], writes=[b_sp[si]])
                S.op("pe", lambda e, si=si, mi=mi, blk=blk, tl=tl, hs=hs, c=c, pl=pl: e.matmul(
                    sp_[si][:, mi * 128:(mi + 1) * 128], lhsT=KT[blk % 2][hs, c, tl * 128:(tl + 1) * 128],
                    rhs=QQ[qb][hs, c, pl * 128:(pl + 1) * 128], start=False, stop=True),
                     reads=[b_KT[blk % 2], b_QQ[qb]], writes=[b_sp[si]])

        def u2(u):
            p, gi, grp, si = u["p"], u["gi"], u["grp"], u["si"]
            ncol = len(grp) * 128
            ei = nxt("e", NEB)
            u["ei"] = ei
            S.op("act", lambda e, ei=ei, si=si, ncol=ncol: e.activation(out=Eb[ei][:, 0:ncol], in_=sp_[si][:, 0:ncol], func=AF.Exp),
                 reads=[b_sp[si]], writes=[b_Eb[ei]])
            nmask = 1 if (gi == 0 and 2 <= p <= 29) else len(grp)
            mi0 = 4 * gi
            S.op("dve", lambda e, ei=ei, p=p, mi0=mi0, n=nmask: e.tensor_tensor(
                out=Eb[ei][:, 0:n * 128].rearrange("p (a q w) -> p a q w", q=2, w=64),
                in0=Eb[ei][:, 0:n * 128].rearrange("p (a q w) -> p a q w", q=2, w=64),
                in1=rmask[:, p, mi0:mi0 + n, :].unsqueeze(3).broadcast_to([128, n, 2, 64]), op=ALU.mult),
                 reads=[b_Eb[ei], b_c], writes=[b_Eb[ei]])

        def u3(u):
            pl, c, hp, gi, grp, ei, o_i, nmm = u["pl"], u["c"], u["hp"], u["gi"], u["grp"], u["ei"], u["o_i"], u["nmm"]
            h = 2 * c + hp
            hs = slice(hp * 64, (hp + 1) * 64)
            for mi, mm in enumerate(grp):
                blk, tl = bq + (pl + mm) // 4, (pl + mm) % 4
                done = 4 * gi + mi + 1
                S.op("pe", lambda e, o_i=o_i, ei=ei, mi=mi, blk=blk, tl=tl, hs=hs, h=h, st=(done == 1), sp=(done == nmm): e.matmul(
                    obank[hs, o_i * 128:(o_i + 1) * 128], lhsT=VV[blk % 2][:, tl, h * 64:(h + 1) * 64], rhs=Eb[ei][:, mi * 128:(mi + 1) * 128],
                    start=st, stop=sp),
                     reads=[b_VV[blk % 2], b_Eb[ei]], writes=[b_oz[o_i]])
                S.op("pe", lambda e, o_i=o_i, ei=ei, mi=mi, hs=hs, st=(done == 1), sp=(done == nmm): e.matmul(
                    zbank[hs, o_i * 128:(o_i + 1) * 128], lhsT=onesc[:], rhs=Eb[ei][:, mi * 128:(mi + 1) * 128], start=st, stop=sp),
                     reads=[b_c2, b_Eb[ei]], writes=[b_oz[o_i]])
            if u["last"]:
                S.op("dve", lambda e, o_i=o_i: e.reciprocal(out=rz[o_i][:], in_=zbank[:, o_i * 128:(o_i + 1) * 128]), reads=[b_oz[o_i]], writes=[b_rz[o_i]])
                S.op("dve", lambda e, o_i=o_i, c=c, pl=pl: e.tensor_tensor(out=aT[:, c, pl * 128:(pl + 1) * 128], in0=obank[:, o_i * 128:(o_i + 1) * 128],
                                                                          in1=rz[o_i][:], op=ALU.mult),
                     reads=[b_oz[o_i], b_rz[o_i]], writes=[b_aT])

        run_pipeline(units, [u1, u2, u3])
        for t in range(4):
            j = nxt("x", 2)
            tok_e = (4 + 8 * bq) * 64 + t * 128
            tok_o = bq * 512 + t * 128
            S.dma("sp", xt[j][:], xe[tok_e:tok_e + 128, :], writes=[b_xt[j]])
            o = 0
            for hf in range(2):
                pi = nxt("pp", 2)
                for k in range(8):
                    S.op("pe", lambda e, pi=pi, k=k, t=t, hf=hf: e.matmul(pp[pi][:], lhsT=aT[:, k, t * 128:(t + 1) * 128],
                                                                         rhs=wo[:, k, hf * 512:(hf + 1) * 512], start=(k == 0), stop=False),
                         reads=[b_aT, b_wo], writes=[b_pp[pi]])
                S.op("pe", lambda e, pi=pi, hf=hf: e.matmul(pp[pi][:], lhsT=onesr[:], rhs=bob[:, hf * 512:(hf + 1) * 512], start=False, stop=True),
                     reads=[b_c2], writes=[b_pp[pi]])
                S.op("dve", lambda e, pi=pi, hf=hf, j=j, o=o: e.tensor_tensor(out=ot[o][:, hf * 512:(hf + 1) * 512], in0=pp[pi][:],
                                                                             in1=xt[j][:, hf * 512:(hf + 1) * 512], op=ALU.add),
                     reads=[b_pp[pi], b_xt[j]], writes=[b_ot[o]])
            S.dma("sp", x_out[tok_o:tok_o + 128, :], ot[o][:], reads=[b_ot[o]], is_output=True)

    for kb in range(9):
        project(kb)
        if kb >= 1:
            attend(kb - 1)


def d_io(nc, ident=None):
    DT = nc.dram_tensor
    return {"gnorm": DT("d_gnorm", [128, D], F32, kind="ExternalInput").ap(),
            "ident": ident if ident is not None else DT("ident", [128, 128], F32, kind="ExternalInput").ap(),
            "w_qkv": DT("w_qkv", [D, 3 * D], F32, kind="ExternalInput").ap(),
            "b_qk": DT("b_qk", [128, 16], F32, kind="ExternalInput").ap(),
            "b_v": DT("b_v", [1, D], F32, kind="ExternalInput").ap(),
            "w_o": DT("w_o", [D, D], F32, kind="ExternalInput").ap(),
            "b_o": DT("b_o", [1, D], F32, kind="ExternalInput").ap(),
            "bt": DT("bt", [128, 16, 7, 128], F32, kind="ExternalInput").ap(),
            "rowmask": DT("rowmask", [128, 32, 6, 2], F32, kind="ExternalInput").ap()}


def build_D():
    nc = bass.Bass("TRN2", target_bir_lowering=False)
    S = Sched(nc)
    io = d_io(nc)
    io["xe"] = nc.dram_tensor("xe", [NROWS_LOC * 64, D], F32, kind="ExternalInput").ap()
    io["x_out"] = nc.dram_tensor("x_out", [TOK, D], F32, kind="ExternalOutput").ap()
    phase_D(nc, S, io)
    S.finish()
    return nc


def run_D(inp, xb_full):
    nc = build_D()
    ident = np.eye(128, dtype=np.float32)
    bq = inp["na_b_qkv"][0]
    bqk = np.concatenate([bq[0:D].reshape(8, 128).T, bq[D:2 * D].reshape(8, 128).T], 1).astype(np.float32)
    base = {"d_gnorm": rep_rows(inp["norm_mix"][1]), "ident": ident, "w_qkv": np.ascontiguousarray(inp["na_w_qkv"][0]),
            "b_qk": np.ascontiguousarray(bqk), "b_v": np.ascontiguousarray(bq[2 * D:][None, :]),
            "w_o": np.ascontiguousarray(inp["na_w_o"][0]), "b_o": np.ascontiguousarray(inp["na_b_o"][0][None, :]),
            "bt": na_tables(np.asarray(inp["na_rpb"][0], np.float32))}
    in_maps = []
    for c in range(NCORES):
        b, q = divmod(c, 4)
        xe = np.zeros((NROWS_LOC * 64, D), np.float32)
        g0 = (64 * q - 4) * 64
        lo, hi = max(g0, 0), min(g0 + NROWS_LOC * 64, SEQ)
        xe[lo - g0:hi - g0] = xb_full[b, lo:hi]
        in_maps.append(dict(base, xe=xe, rowmask=na_rowmask(q)))
    res = _run(nc, in_maps, "D")
    return [r["x_out"] for r in res.results]


def _tok_shards(a):
    return [np.ascontiguousarray(a[c // 4, (c % 4) * TOK:(c % 4 + 1) * TOK]) for c in range(NCORES)]


def _assemble(shards):
    out = np.empty((BATCH, SEQ, D), np.float32)
    for c in range(NCORES):
        out[c // 4, (c % 4) * TOK:(c % 4 + 1) * TOK] = shards[c]
    return out


NEXT = NROWS_LOC * 64


def build_L2():
    nc0 = bass.Bass("TRN2", target_bir_lowering=False)
    S = Sched(nc0)
    nc = NCP(nc0)
    DT = nc0.dram_tensor
    ext = lambda name, shape: DT(name, shape, F32, kind="ExternalInput").ap()
    ident = ext("ident", [128, 128])
    xa_s = DT("xa_scr", [NEXT, D], F32).ap()
    xb_s = DT("xb_scr", [NEXT, D], F32).ap()
    xc_s = DT("xc_scr", [TOK, D], F32).ap()
    out = DT("out", [TOK, D], F32, kind="ExternalOutput").ap()
    ioC = {"x_in": ext("x_ext", [NEXT, D]), "yT": ext("yT", [D, NEXT]), "sT": ext("sT", [D, NEXT]), "x0T": ext("x0T", [D, NEXT]),
           "skip": ext("skip", [128, 8]), "w_out": ext("w_out", [D, D]), "b_out": ext("b_out", [1, D]), "x_out": xa_s}
    ioF0 = {"x_in": xa_s, "gnorm": ext("gn_f0", [128, D]), "ident": ident, "wg": ext("wg0", [D, FF]), "wu": ext("wu0", [D, FF]),
            "wd": ext("wd0", [FF, D]), "x_out": xb_s}
    ioD = d_io(nc0, ident)
    ioD["xe"] = xb_s
    ioD["x_out"] = xc_s
    ioF1 = {"x_in": xc_s, "gnorm": ext("gn_f1", [128, D]), "ident": ident, "wg": ext("wg1", [D, FF]), "wu": ext("wu1", [D, FF]),
            "wd": ext("wd1", [FF, D]), "gfin": ext("gfin", [128, D]), "x_out": out}
    S.pfx = "c_"
    phase_C(nc, S, ioC, NEXT)
    S.barrier()
    nc.reset()
    S.pfx = "f0_"
    phase_F(nc, S, ioF0, NEXT, False)
    S.barrier()
    nc.reset()
    S.pfx = "d_"
    phase_D(nc, S, ioD)
    S.barrier()
    nc.reset()
    S.pfx = "f1_"
    phase_F(nc, S, ioF1, TOK, True)
    S.finish()
    return nc0


def _ext_tok(a, c):
    b, q = divmod(c, 4)
    g0 = q * TOK - 256
    out = np.zeros((NEXT,) + a.shape[2:], np.float32)
    lo, hi = max(g0, 0), min(g0 + NEXT, SEQ)
    out[lo - g0:hi - g0] = a[b, lo:hi]
    return out


def run_L2(inp, x, y_full, s_full, x0_full):
    nc = build_L2()
    bq = inp["na_b_qkv"][0]
    bqk = np.concatenate([bq[0:D].reshape(8, 128).T, bq[D:2 * D].reshape(8, 128).T], 1).astype(np.float32)
    base = {"ident": np.eye(128, dtype=np.float32),
            "skip": np.ascontiguousarray(inp["hy_skip"][0].reshape(8, 128).T), "w_out": np.ascontiguousarray(inp["hy_w_out"][0]),
            "b_out": np.ascontiguousarray(inp["hy_b_out"][0][None, :]),
            "gn_f0": rep_rows(inp["norm_ffn"][0]), "wg0": np.ascontiguousarray(inp["ffn_w_gate"][0]),
            "wu0": np.ascontiguousarray(inp["ffn_w_up"][0]), "wd0": np.ascontiguousarray(inp["ffn_w_down"][0]),
            "gn_f1": rep_rows(inp["norm_ffn"][1]), "wg1": np.ascontiguousarray(inp["ffn_w_gate"][1]),
            "wu1": np.ascontiguousarray(inp["ffn_w_up"][1]), "wd1": np.ascontiguousarray(inp["ffn_w_down"][1]),
            "gfin": rep_rows(inp["norm_final"]),
            "d_gnorm": rep_rows(inp["norm_mix"][1]), "w_qkv": np.ascontiguousarray(inp["na_w_qkv"][0]),
            "b_qk": np.ascontiguousarray(bqk), "b_v": np.ascontiguousarray(bq[2 * D:][None, :]),
            "w_o": np.ascontiguousarray(inp["na_w_o"][0]), "b_o": np.ascontiguousarray(inp["na_b_o"][0][None, :]),
            "bt": na_tables(np.asarray(inp["na_rpb"][0], np.float32))}
    in_maps = []
    for c in range(NCORES):
        m = dict(base)
        m["x_ext"] = _ext_tok(x, c)
        m["yT"] = np.ascontiguousarray(_ext_tok(y_full, c).T)
        m["sT"] = np.ascontiguousarray(_ext_tok(s_full, c).T)
        m["x0T"] = np.ascontiguousarray(_ext_tok(x0_full, c).T)
        m["rowmask"] = na_rowmask(c % 4)
        in_maps.append(m)
    res = _run(nc, in_maps, "L2")
    return [r["out"] for r in res.results]


def kernel(**inp):
    inp = {k: np.asarray(v, dtype=np.float32) for k, v in inp.items()}
    x = inp["x"]
    resA = run_A(inp)
    sT = [r["sT"] for r in resA]
    x0T = [r["x0T"] for r in resA]
    s_cs = [np.empty((2, 128, SEQ), np.float32) for _ in range(NCORES)]
    for c in range(NCORES):
        b, q = divmod(c, 4)
        for cg in range(NCORES):
            s_cs[cg][b, :, q * TOK:(q + 1) * TOK] = sT[c][cg * 128:(cg + 1) * 128]
    resB = run_B(inp, s_cs)
    y_full = np.empty((BATCH, SEQ, D), np.float32)
    for cg in range(NCORES):
        y_full[:, :, cg * 128:(cg + 1) * 128] = resB[cg]["y_out"].transpose(0, 2, 1)
    s_full = _assemble([t.T for t in sT])
    x0_full = _assemble([t.T for t in x0T])
    out = run_L2(inp, x, y_full, s_full, x0_full)
    return _assemble(out)
```

```python
import math
import numpy as np
import ml_dtypes
import concourse.bass as bass
import concourse.mybir as mybir
from concourse.bass_utils import run_bass_kernel_spmd
import os

F32 = mybir.dt.float32
BF16 = mybir.dt.bfloat16
AF = mybir.ActivationFunctionType
ALU = mybir.AluOpType
AX = mybir.AxisListType

D = 1024
SEQ = 16384
BATCH = 2
NCORES = 8
TOK = 4096
FF = 2816
EPS = 1e-6


class Buf:
    __slots__ = ("name", "last_w", "readers", "sem", "dma_cnt")

    def __init__(self, name):
        self.name = name
        self.last_w = None
        self.readers = []
        self.sem = None
        self.dma_cnt = 0


class Ins:
    __slots__ = ("eng", "fn", "deps", "milestone", "ms", "dma_sem", "dma_val")

    def __init__(self, eng, fn):
        self.eng = eng
        self.fn = fn
        self.deps = []
        self.milestone = False
        self.ms = 0
        self.dma_sem = None
        self.dma_val = 0


class Sched:
    ENGS = ("pe", "act", "dve", "pool", "sp")

    def __init__(self, nc):
        self.nc = nc
        self.q = {e: [] for e in self.ENGS}
        self.esem = {e: nc.alloc_semaphore("prog_" + e) for e in self.ENGS}
        self.nbuf = 0
        self.out_events = []
        self.pfx = ""
        self.pending_barrier = {}
        self.all_dma = []

    def buf(self, name=None):
        self.nbuf += 1
        return Buf(self.pfx + (name or ("b%d" % self.nbuf)))

    def barrier(self):
        deps = [self.q[e][-1] for e in self.ENGS if self.q[e] and self.q[e][-1].fn is not None]
        last = {}
        for d in self.all_dma:
            last[id(d.dma_sem)] = d
        deps += list(last.values())
        self.pending_barrier = {e: list(deps) for e in self.ENGS}

    def _push(self, eng, ins):
        pb = self.pending_barrier.pop(eng, None)
        if pb:
            for d in pb:
                if d is not ins and d not in ins.deps:
                    ins.deps.append(d)
        self.q[eng].append(ins)

    def _collect(self, ins, reads, writes):
        deps = []
        for b in reads:
            if b.last_w is not None:
                deps.append(b.last_w)
        for b in writes:
            if b.last_w is not None:
                deps.append(b.last_w)
            deps.extend(b.readers)
        for d in deps:
            if d is ins:
                continue
            if d.dma_sem is None and d.eng == "pe" and ins.eng == "pe":
                continue
            ins.deps.append(d)
        for b in writes:
            b.last_w = ins
            b.readers = []
        for b in reads:
            if b not in writes:
                b.readers.append(ins)

    def op(self, eng, fn, reads=(), writes=()):
        ins = Ins(eng, fn)
        self._collect(ins, list(reads), list(writes))
        self._push(eng, ins)
        return ins

    def dma(self, eng, out_ap, in_ap, reads=(), writes=(), sem_buf=None, is_output=False, **kw):
        if sem_buf is None:
            sem_buf = (list(writes) + list(reads))[0]
        if sem_buf.sem is None:
            sem_buf.sem = self.nc.alloc_semaphore("dma_" + sem_buf.name)
        ins = Ins(eng, lambda e: e.dma_start(out=out_ap, in_=in_ap, **kw))
        self._collect(ins, list(reads), list(writes))
        sem_buf.dma_cnt += 16
        ins.dma_sem = sem_buf.sem
        ins.dma_val = sem_buf.dma_cnt
        self.all_dma.append(ins)
        self._push(eng, ins)
        if is_output:
            self.out_events.append(ins)
        return ins

    def coll(self, kind, out_ap, in_ap, reads=(), writes=(), sem_buf=None):
        if sem_buf is None:
            sem_buf = (list(writes) + list(reads))[0]
        if sem_buf.sem is None:
            sem_buf.sem = self.nc.alloc_semaphore("dma_" + sem_buf.name)
        groups = [list(range(NCORES))]
        ins = Ins("pool", lambda e: e.collective_compute(kind, ALU.bypass, replica_groups=groups, ins=[in_ap], outs=[out_ap]))
        self._collect(ins, list(reads), list(writes))
        sem_buf.dma_cnt += 16
        ins.dma_sem = sem_buf.sem
        ins.dma_val = sem_buf.dma_cnt
        self.all_dma.append(ins)
        self._push("pool", ins)
        return ins

    def finish(self):
        fin = Ins("sp", None)
        fin.deps = list(self.out_events)
        self.q["sp"].append(fin)
        for e in self.ENGS:
            for ins in self.q[e]:
                for d in ins.deps:
                    if d.dma_sem is None:
                        d.milestone = True
        for e in self.ENGS:
            c = 0
            for ins in self.q[e]:
                if ins.milestone:
                    c += 1
                    ins.ms = c
        nc = self.nc
        engobj = {"pe": "tensor", "act": "scalar", "dve": "vector", "pool": "gpsimd", "sp": "sync"}

        def emit(ename, e):
            waited = {}
            for ins in self.q[ename]:
                need = {}
                for d in ins.deps:
                    if d.dma_sem is not None:
                        s, v = d.dma_sem, d.dma_val
                    else:
                        s, v = self.esem[d.eng], d.ms
                    k = id(s)
                    if waited.get(k, 0) >= v:
                        continue
                    if k not in need or need[k][1] < v:
                        need[k] = (s, v)
                for k, (s, v) in need.items():
                    e.wait_ge(s, v)
                    waited[k] = v
                if ins.fn is None:
                    continue
                r = ins.fn(e)
                if ins.dma_sem is not None:
                    r.then_inc(ins.dma_sem, 16)
                elif ins.milestone:
                    r.then_inc(self.esem[ename], 1)

        with nc.Block() as block:
            for ename in self.ENGS:
                if not self.q[ename]:
                    continue
                getattr(block, engobj[ename])(lambda e, en=ename: emit(en, e))


ARENA_BYTES = 212800


class NCP:
    _DTB = None

    def __init__(self, nc):
        self._nc = nc
        self._arena = nc.alloc_sbuf_tensor("arena", [128, ARENA_BYTES // 4], F32)
        self._banks = [nc.alloc_psum_tensor("bank%d" % i, [128, 512], F32) for i in range(8)]
        self.reset()

    def reset(self):
        self._off = 0
        self._nbank = 0

    @staticmethod
    def _view(ap2d, shape, dt):
        if dt != F32:
            ap2d = ap2d.bitcast(dt)
        if len(shape) == 2:
            return ap2d
        names = " ".join("d%d" % i for i in range(1, len(shape)))
        kw = {"d%d" % i: shape[i] for i in range(1, len(shape))}
        return ap2d.rearrange("p (%s) -> p %s" % (names, names), **kw)

    def alloc_sbuf_tensor(self, name, shape, dt):
        esz = 2 if dt == BF16 else 4
        n = 1
        for d in shape[1:]:
            n *= d
        nbytes = (n * esz + 31) // 32 * 32
        assert self._off + nbytes <= ARENA_BYTES, "arena overflow at %s (%d + %d)" % (name, self._off, nbytes)
        o4 = self._off // 4
        self._off += nbytes
        return self._view(self._arena[0:shape[0], o4:o4 + (n * esz + 3) // 4], shape, dt)

    def alloc_psum_tensor(self, name, shape, dt):
        esz = 2 if dt == BF16 else 4
        n = 1
        for d in shape[1:]:
            n *= d
        assert n * esz <= 2048 and self._nbank < 8, "psum overflow at " + name
        bk = self._banks[self._nbank]
        self._nbank += 1
        return self._view(bk[0:shape[0], 0:(n * esz + 3) // 4], shape, dt)

    def __getattr__(self, k):
        return getattr(self._nc, k)


def _run(nc, in_maps, tag=""):
    tr = bool(os.environ.get("K_TRACE"))
    res = run_bass_kernel_spmd(nc, in_maps, core_ids=list(range(NCORES)), **({"trace": True} if tr else {}))
    if tr:
        print("K_TRACE", tag, "exec_time_ns", res.exec_time_ns, flush=True)
    return res


def run_pipeline(items, stages):
    n, m = len(items), len(stages)
    for t in range(n + m - 1):
        for k in range(m):
            i = t - k
            if 0 <= i < n:
                stages[k](items[i])


def bcast_rows(ap_row, nparts):
    return ap_row.broadcast(0, nparts) if hasattr(ap_row, "broadcast") else ap_row


def build_A():
    nc = bass.Bass("TRN2", target_bir_lowering=False)
    S = Sched(nc)
    NT = TOK // 128
    x_own = nc.dram_tensor("x_own", [TOK, D], F32, kind="ExternalInput").ap()
    x_halo = nc.dram_tensor("x_halo", [128, D], F32, kind="ExternalInput").ap()
    emask = nc.dram_tensor("emask", [128, 2], F32, kind="ExternalInput").ap()
    gnorm = nc.dram_tensor("gnorm", [128, D], F32, kind="ExternalInput").ap()
    ident_d = nc.dram_tensor("ident", [128, 128], F32, kind="ExternalInput").ap()
    w_in = nc.dram_tensor("w_in", [D, 3 * D], F32, kind="ExternalInput").ap()
    b_in = nc.dram_tensor("b_in", [128, 24], F32, kind="ExternalInput").ap()
    cw = nc.dram_tensor("cw", [128, 3 * 24], F32, kind="ExternalInput").ap()
    cb = nc.dram_tensor("cb", [128, 24], F32, kind="ExternalInput").ap()
    sT = nc.dram_tensor("sT", [D, TOK], F32, kind="ExternalOutput").ap()
    x0T = nc.dram_tensor("x0T", [D, TOK], F32, kind="ExternalOutput").ap()

    A = nc.alloc_sbuf_tensor
    hT = A("hT", [128, 8, TOK + 2], BF16)
    wbf = A("wbf", [128, 8, 3 * D], BF16)
    wst = [A("wst%d" % i, [128, 768], F32) for i in range(2)]
    xt = [A("xt%d" % i, [128, D], F32) for i in range(2)]
    sq = A("sq", [128, D], F32)
    hb = [A("hb%d" % i, [128, D], BF16) for i in range(2)]
    ss = [A("ss%d" % i, [128, 1], F32) for i in range(2)]
    rs = [A("rs%d" % i, [128, 1], F32) for i in range(2)]
    gt = A("gt", [128, D], F32)
    idf = A("idf", [128, 128], F32)
    idb = A("idb", [128, 128], BF16)
    em = A("em", [128, 2], F32)
    bi = A("bi", [128, 24], F32)
    cwt = A("cwt", [128, 72], F32)
    cbt = A("cbt", [128, 24], F32)
    zb = [A("zb%d" % i, [128, TOK + 2], F32) for i in range(2)]
    acc = [A("acc%d" % i, [128, TOK], F32) for i in range(2)]
    tp = [nc.alloc_psum_tensor("tp%d" % i, [128, 8 * 128], BF16) for i in range(2)]
    mp = [nc.alloc_psum_tensor("mp%d" % i, [128, 512], F32) for i in range(4)]
    hp = nc.alloc_psum_tensor("hp", [128, 2], F32)

    B = S.buf
    b_hT, b_wbf = B("hT"), B("wbf")
    b_wst = [B("wst0"), B("wst1")]
    b_xt = [B("xt0"), B("xt1")]
    b_sq = B("sq")
    b_hb = [B("hb0"), B("hb1")]
    b_ss = [B("ss0"), B("ss1")]
    b_rs = [B("rs0"), B("rs1")]
    b_c = B("consts")
    b_idb = B("idb")
    b_zb = [B("zb0"), B("zb1")]
    b_acc = [B("acc0"), B("acc1")]
    b_tp = [B("tp0"), B("tp1")]
    b_mp = [B("mp%d" % i) for i in range(4)]
    b_hp = B("hp")

    for dst, src in ((gt, gnorm), (idf, ident_d), (em, emask), (bi, b_in), (cwt, cw), (cbt, cb)):
        S.dma("sp", dst[:], src, writes=[b_c], sem_buf=b_c)
    S.op("dve", lambda e: e.tensor_copy(out=idb[:], in_=idf[:]), reads=[b_c], writes=[b_idb])

    for k2 in range(32):
        k, hf = divmod(k2, 4)
        S.dma("pool", wst[k2 % 2][:], w_in[k * 128:(k + 1) * 128, hf * 768:(hf + 1) * 768], writes=[b_wst[k2 % 2]])
        S.op("pool", lambda e, k=k, hf=hf, k2=k2: e.tensor_copy(out=wbf[:, k, hf * 768:(hf + 1) * 768], in_=wst[k2 % 2][:]),
             reads=[b_wst[k2 % 2]], writes=[b_wbf])

    for i in range(NT + 1):
        j = i % 2
        src = x_own[i * 128:(i + 1) * 128, :] if i < NT else x_halo
        S.dma("sp", xt[j][:], src, writes=[b_xt[j]])
        S.op("act", lambda e, j=j: e.activation(out=sq[:], in_=xt[j][:], func=AF.Square, accum_out=ss[j][:]),
             reads=[b_xt[j]], writes=[b_sq, b_ss[j]])
        S.op("act", lambda e, j=j: e.activation(out=rs[j][:], in_=ss[j][:], func=AF.Sqrt, scale=1.0 / D, bias=EPS),
             reads=[b_ss[j]], writes=[b_rs[j]])
        S.op("dve", lambda e, j=j: e.reciprocal(out=rs[j][:], in_=rs[j][:]), reads=[b_rs[j]], writes=[b_rs[j]])
        S.op("dve", lambda e, j=j: e.scalar_tensor_tensor(out=hb[j][:], in0=xt[j][:], scalar=rs[j][:, 0:1], in1=gt[:],
                                                          op0=ALU.mult, op1=ALU.mult),
             reads=[b_xt[j], b_rs[j], b_c], writes=[b_hb[j]])
        for k in range(8):
            S.op("pe", lambda e, j=j, k=k: e.transpose(out=tp[j][:, k * 128:(k + 1) * 128],
                                                        in_=hb[j][:, k * 128:(k + 1) * 128], identity=idb[:]),
                 reads=[b_hb[j], b_idb], writes=[b_tp[j]])
        if i < NT:
            S.op("act", lambda e, j=j, i=i: e.copy(out=hT[:, :, 1 + i * 128:1 + (i + 1) * 128],
                                                   in_=tp[j][:].rearrange("p (k t) -> p k t", k=8)),
                 reads=[b_tp[j]], writes=[b_hT])
        else:
            S.op("act", lambda e, j=j: e.copy(out=hT[:, :, 0:TOK + 2:TOK + 1],
                                              in_=tp[j][:].rearrange("p (k t) -> p k t", k=8)[:, :, 0:2]),
                 reads=[b_tp[j]], writes=[b_hT])

    def proj_conv(cc, zi, ai):
        for jg in range(8):
            m = (cc * 8 + jg) % 4
            for k in range(8):
                S.op("pe", lambda e, m=m, k=k, jg=jg: e.matmul(mp[m][:], lhsT=wbf[:, k, cc * 128:(cc + 1) * 128],
                                                               rhs=hT[:, k, 1 + jg * 512:1 + (jg + 1) * 512],
                                                               start=(k == 0), stop=(k == 7)),
                     reads=[b_wbf, b_hT], writes=[b_mp[m]])
            S.op("act", lambda e, m=m, jg=jg: e.activation(out=zb[zi][:, 1 + jg * 512:1 + (jg + 1) * 512], in_=mp[m][:],
                                                           func=AF.Identity, bias=bi[:, cc:cc + 1], scale=1.0),
                 reads=[b_mp[m], b_c], writes=[b_zb[zi]])
        for k in range(8):
            S.op("pe", lambda e, k=k: e.matmul(hp[:], lhsT=wbf[:, k, cc * 128:(cc + 1) * 128],
                                               rhs=hT[:, k, 0:TOK + 2:TOK + 1], start=(k == 0), stop=(k == 7)),
                 reads=[b_wbf, b_hT], writes=[b_hp])
        S.op("act", lambda e: e.activation(out=zb[zi][:, 0:TOK + 2:TOK + 1], in_=hp[:], func=AF.Identity,
                                           bias=bi[:, cc:cc + 1], scale=1.0),
             reads=[b_hp, b_c], writes=[b_zb[zi]])
        S.op("dve", lambda e: e.tensor_tensor(out=zb[zi][:, 0:TOK + 2:TOK + 1], in0=zb[zi][:, 0:TOK + 2:TOK + 1],
                                              in1=em[:], op=ALU.mult),
             reads=[b_zb[zi], b_c], writes=[b_zb[zi]])
        eng = "dve"
        S.op(eng, lambda e: e.tensor_scalar(out=acc[ai][:], in0=zb[zi][:, 0:TOK], scalar1=cwt[:, cc:cc + 1],
                                            scalar2=cbt[:, cc:cc + 1], op0=ALU.mult, op1=ALU.add),
             reads=[b_zb[zi], b_c], writes=[b_acc[ai]])
        for t in (1, 2):
            S.op(eng, lambda e, t=t: e.scalar_tensor_tensor(out=acc[ai][:], in0=zb[zi][:, t:t + TOK],
                                                           scalar=cwt[:, t * 24 + cc:t * 24 + cc + 1], in1=acc[ai][:],
                                                           op0=ALU.mult, op1=ALU.add),
                 reads=[b_zb[zi], b_c, b_acc[ai]], writes=[b_acc[ai]])

    for c in range(8):
        proj_conv(c, 0, 0)
        S.dma("sp", x0T[c * 128:(c + 1) * 128, :], acc[0][:], reads=[b_acc[0]], is_output=True)
        proj_conv(8 + c, 1, 1)
        proj_conv(16 + c, 0, 0)
        S.op("pool", lambda e: e.tensor_tensor(out=acc[0][:], in0=acc[0][:], in1=acc[1][:], op=ALU.mult),
             reads=[b_acc[1], b_acc[0]], writes=[b_acc[0]])
        S.dma("sp", sT[c * 128:(c + 1) * 128, :], acc[0][:], reads=[b_acc[0]], is_output=True)
    S.finish()
    return nc


def run_A(inp):
    x = np.ascontiguousarray(inp["x"], dtype=np.float32)
    nc = build_A()
    g = np.ascontiguousarray(np.broadcast_to(inp["norm_mix"][0][None, :], (128, D)), dtype=np.float32)
    ident = np.eye(128, dtype=np.float32)
    w_in = np.ascontiguousarray(inp["hy_w_in"][0], dtype=np.float32)
    b_in = np.ascontiguousarray(inp["hy_b_in"][0].reshape(24, 128).T)
    cwv = np.ascontiguousarray(inp["hy_conv_w"][0].reshape(3, 24, 128).transpose(2, 0, 1).reshape(128, 72))
    cbv = np.ascontiguousarray(inp["hy_conv_b"][0].reshape(24, 128).T)
    in_maps = []
    for c in range(NCORES):
        b, q = divmod(c, 4)
        t0 = q * TOK
        halo = np.zeros((128, D), np.float32)
        em = np.zeros((128, 2), np.float32)
        if q > 0:
            halo[0] = x[b, t0 - 1]
            em[:, 0] = 1.0
        if q < 3:
            halo[1] = x[b, t0 + TOK]
            em[:, 1] = 1.0
        in_maps.append({"x_own": np.ascontiguousarray(x[b, t0:t0 + TOK]), "x_halo": halo, "emask": em, "gnorm": g,
                        "ident": ident, "w_in": w_in, "b_in": b_in, "cw": cwv, "cb": cbv})
    res = _run(nc, in_maps, "A")
    return res.results


NFFT = 2 * SEQ
CG = 8
NG = 128 // CG


def fft_consts():
    i128 = np.arange(128, dtype=np.float64)
    i256 = np.arange(256, dtype=np.float64)
    c = {}
    a = 2 * np.pi * np.outer(i128, i256) / 256.0
    c["FA"] = np.concatenate([np.cos(a), -np.sin(a)], 1)
    c["FB"] = np.concatenate([np.sin(a), np.cos(a)], 1)
    t = 2 * np.pi * np.outer(i128, i256) / NFFT
    c["TW"] = np.stack([np.cos(t), -np.sin(t)], 1)
    f = 2 * np.pi * np.outer(i128, i128) / 128.0
    c["F128"] = np.stack([np.cos(f), -np.sin(f), np.sin(f)], 1)
    c["GA"] = np.concatenate([np.cos(f), np.sin(f)], 1)
    c["GB"] = np.concatenate([-np.sin(f), np.cos(f)], 1)
    k1 = (128 * np.arange(2)[None, :, None] + i128[:, None, None])
    it = 2 * np.pi * k1 * i128[None, None, :] / NFFT
    c["ITW"] = np.stack([np.cos(it), np.sin(it)], 2)
    h = 2 * np.pi * k1 * i128[None, None, :] / 256.0
    c["H"] = np.stack([np.cos(h), np.sin(h), -np.sin(h)], 2)
    pos = 128 * i128[:, None] + i128[None, :]
    tl = np.linspace(0.0, 1.0, SEQ, dtype=np.float32)
    c["NEGT"] = -tl[pos.astype(np.int64)]
    return {k: np.ascontiguousarray(v, dtype=np.float32) for k, v in c.items()}


def filter_feat():
    f32 = np.float32
    L = SEQ
    t = np.linspace(0.0, 1.0, L, dtype=f32)[:, None]
    w = (f32(2.0 * math.pi) * np.arange(L, dtype=f32)[:, None] / f32(L)).astype(f32)
    bands = np.linspace(1e-4, 15, 16, dtype=f32)[None, :]
    bw = (bands * w).astype(f32)
    feat = np.concatenate([t, np.cos(bw), -np.sin(bw)], axis=-1).astype(f32)
    return np.ascontiguousarray(feat.T)


def build_B():
    nc = bass.Bass("TRN2", target_bir_lowering=False)
    S = Sched(nc)
    DT = nc.dram_tensor
    s_in = DT("s_in", [2, 128, SEQ], F32, kind="ExternalInput").ap()
    featT = DT("featT", [33, SEQ], F32, kind="ExternalInput").ap()
    w1d = DT("f_w1", [33, 64], F32, kind="ExternalInput").ap()
    w2d = DT("f_w2", [64, 64], F32, kind="ExternalInput").ap()
    w3d = DT("f_w3", [64, 64], F32, kind="ExternalInput").ap()
    fbd = DT("f_bf", [64, 4], F32, kind="ExternalInput").ap()
    woutd = DT("f_wout", [64, NG * 2 * CG], F32, kind="ExternalInput").ap()
    decd = DT("decay", [1, NG * 2 * CG], F32, kind="ExternalInput").ap()
    cd = {}
    shapes = {"FA": [128, 512], "FB": [128, 512], "TW": [128, 2, 256], "F128": [128, 3, 128], "GA": [128, 256],
              "GB": [128, 256], "ITW": [128, 2, 2, 128], "H": [128, 2, 3, 128], "NEGT": [128, 128]}
    for k, sh in shapes.items():
        cd[k] = DT("c_" + k, sh, F32, kind="ExternalInput").ap()
    y_out = DT("y_out", [2, 128, SEQ], F32, kind="ExternalOutput").ap()

    A = nc.alloc_sbuf_tensor
    B = S.buf
    cf = {k: A("cf_" + k, sh, F32) for k, sh in shapes.items()}
    cb = {k: A("cb_" + k, shapes[k], BF16) for k in ("FA", "FB", "F128", "GA", "GB", "H")}
    b_const = B("const")
    for k in shapes:
        S.dma("sp", cf[k][:], cd[k], writes=[b_const], sem_buf=b_const)
    b_cb = B("constbf")
    for k in cb:
        S.op("pool", lambda e, k=k: e.tensor_copy(out=cb[k][:], in_=cf[k][:]), reads=[b_const], writes=[b_cb])
    w1s, w2s, w3s = A("w1s", [33, 64], F32), A("w2s", [64, 64], F32), A("w3s", [64, 64], F32)
    fb = A("fb", [64, 4], F32)
    fbb = A("fbb", [64, 3], F32)
    wout = A("wout", [64, NG * 2 * CG], F32)
    absdec = A("absdec", [128, NG * 2 * CG], F32)
    ones = A("ones", [128, 128], F32)
    b_fw = B("fw")
    for dst, src in ((w1s, w1d), (w2s, w2d), (w3s, w3d), (fb, fbd), (wout, woutd)):
        S.dma("sp", dst[:], src, writes=[b_fw], sem_buf=b_fw)
    S.dma("sp", absdec[:], decd.broadcast_to([128, NG * 2 * CG]), writes=[b_fw], sem_buf=b_fw)
    b_fw2 = B("fw2")
    S.op("act", lambda e: e.activation(out=absdec[:], in_=absdec[:], func=AF.Abs),
         reads=[b_fw], writes=[b_fw])
    S.op("dve", lambda e: e.tensor_tensor(out=fbb[:], in0=fb[:, 0:3], in1=fb[:, 3:4].broadcast_to([64, 3]), op=ALU.mult),
         reads=[b_fw], writes=[b_fw2])
    S.op("pool", lambda e: e.memset(ones[:], 1.0), writes=[b_fw2])

    h3 = A("h3", [64, SEQ], F32)
    b_h3 = B("h3")
    NR = 3
    ft = [A("ft%d" % i, [33, 512], F32) for i in range(NR)]
    b_ft = [B("ft%d" % i) for i in range(NR)]
    NAR = 3
    arg = [A("arg%d" % i, [64, 512], F32) for i in range(NAR)]
    b_arg = [B("arg%d" % i) for i in range(NAR)]
    NH = 2
    hh = [[A("hh%d_%d" % (l, i), [64, 512], F32) for i in range(NH)] for l in range(2)]
    b_hh = [[B("hh%d_%d" % (l, i)) for i in range(NH)] for l in range(2)]
    NPS = 8
    ps = [nc.alloc_psum_tensor("ps%d" % i, [128, 512], F32) for i in range(NPS)]
    b_ps = [B("ps%d" % i) for i in range(NPS)]
    TWO_PI = 2.0 * math.pi
    cnt = [0]
    fs = A("fs", [64, 1], F32)
    fu = A("fu", [64, 3], F32)
    S.op("dve", lambda e: e.tensor_scalar(out=fs[:], in0=fb[:, 3:4], scalar1=1.0 / TWO_PI, scalar2=None, op0=ALU.mult),
         reads=[b_fw], writes=[b_fw2])
    S.op("dve", lambda e: e.tensor_scalar(out=fu[:], in0=fbb[:], scalar1=1.0 / TWO_PI, scalar2=4.5, op0=ALU.mult, op1=ALU.add),
         reads=[b_fw2], writes=[b_fw2])
    ki = [A("ki%d" % i, [64, 512], mybir.dt.int32) for i in range(NAR)]
    kf = [A("kf%d" % i, [64, 512], F32) for i in range(NAR)]
    negpi = A("negpi", [64, 1], F32)
    S.op("pool", lambda e: e.memset(negpi[:], -3.1415925), writes=[b_fw2])

    def mlp_layer(pi, lhsT, rhs_ap, rhs_buf, li, out_ap, out_buf):
        a = cnt[0] % NAR
        cnt[0] += 1
        S.op("pe", lambda e: e.matmul(ps[pi][0:64, :], lhsT=lhsT, rhs=rhs_ap, start=True, stop=True),
             reads=[b_fw, rhs_buf], writes=[b_ps[pi]])
        S.op("dve", lambda e: e.tensor_scalar(out=arg[a][:], in0=ps[pi][0:64, :], scalar1=fs[:, 0:1], scalar2=fu[:, li:li + 1],
                                              op0=ALU.mult, op1=ALU.add),
             reads=[b_ps[pi], b_fw, b_fw2], writes=[b_arg[a]])
        S.op("dve", lambda e: e.tensor_copy(out=ki[a][:], in_=arg[a][:]), reads=[b_arg[a]], writes=[b_arg[a]])
        S.op("dve", lambda e: e.tensor_copy(out=kf[a][:], in_=ki[a][:]), reads=[b_arg[a]], writes=[b_arg[a]])
        S.op("dve", lambda e: e.tensor_tensor(out=arg[a][:], in0=arg[a][:], in1=kf[a][:], op=ALU.subtract),
             reads=[b_arg[a]], writes=[b_arg[a]])
        S.op("dve", lambda e: e.scalar_tensor_tensor(out=arg[a][:], in0=arg[a][:], scalar=0.0, in1=arg[a][:],
                                                     op0=ALU.is_lt, op1=ALU.add),
             reads=[b_arg[a]], writes=[b_arg[a]])
        S.op("act", lambda e: e.activation(out=out_ap, in_=arg[a][:], func=AF.Sin, bias=negpi[:, 0:1], scale=6.283185),
             reads=[b_arg[a], b_fw2], writes=[out_buf])

    def m1(pg):
        j = pg % NR
        S.dma("sp", ft[j][:], featT[:, pg * 512:(pg + 1) * 512], writes=[b_ft[j]])
        mlp_layer(pg % 2, w1s[:], ft[j][:], b_ft[j], 0, hh[0][pg % NH][:], b_hh[0][pg % NH])

    def m2(pg):
        j = pg % NH
        mlp_layer(2 + pg % 2, w2s[:], hh[0][j][:], b_hh[0][j], 1, hh[1][j][:], b_hh[1][j])

    def m3(pg):
        j = pg % NH
        mlp_layer(4 + pg % 2, w3s[:], hh[1][j][:], b_hh[1][j], 2, h3[:, pg * 512:(pg + 1) * 512], b_h3)

    run_pipeline(list(range(SEQ // 512)), [m1, m2, m3])

    G1f = A("G1f", [128, 2 * CG, 128], F32)
    win = A("win", [128, 128, 2 * CG], F32)
    pm = A("pm", [128, 2, CG, 128], BF16)
    Gh = A("Gh", [128, CG, 2, 256], F32)
    part = A("part", [128, 2 * CG], F32)
    tot = A("tot", [128, 2 * CG], F32)
    rn = A("rn", [128, CG], F32)
    D1f = A("D1f", [128, CG, 2, 128], F32)
    D1b = A("D1b", [128, CG, 2, 128], BF16)
    NBB, NPB, NPBB = 6, 3, 3
    Bb = [A("Bb%d" % i, [128, 2, 256], BF16) for i in range(NBB)]
    P1 = [A("P1_%d" % i, [128, 512], F32) for i in range(NPB)]
    P2 = [A("P2_%d" % i, [128, 512], F32) for i in range(NPB)]
    Pb = [A("Pb%d" % i, [128, 2, 256], BF16) for i in range(NPBB)]
    ring = {}

    def nxt(key, n):
        v = ring.get(key, 0)
        ring[key] = v + 1
        return v % n

    Cb = [A("Cb%d" % i, [128, 2, 2, 4, 128], BF16) for i in range(2)]
    yb = [A("yb%d" % i, [128, 4, 2, 128], F32) for i in range(2)]
    b_G1f, b_win, b_pm, b_Gh, b_part, b_rn = B("G1f"), B("win"), B("pm"), B("Gh"), B("part"), B("rn")
    b_D1f, b_D1b = B("D1f"), B("D1b")
    b_Bb = [B("Bb%d" % i) for i in range(NBB)]
    b_P1 = [B("P1_%d" % i) for i in range(NPB)]
    b_P2 = [B("P2_%d" % i) for i in range(NPB)]
    b_Pb = [B("Pb%d" % i) for i in range(NPBB)]
    b_Cb = [B("Cb0"), B("Cb1")]
    b_yb = [B("yb0"), B("yb1")]
    pctr = [0]
    cctr = [0]

    def next_ps():
        pctr[0] += 1
        return pctr[0] % NPS

    def cmul(psv, t1, t2, o_re, o_im, shp):
        k = cctr[0] % 2
        cctr[0] += 1
        p1 = P1[k][:].rearrange(shp[0], **shp[1])
        p2 = P2[k][:].rearrange(shp[0], **shp[1])
        return k, p1, p2

    def fwd_stage(lhs_re, lhs_im, lhs_buf, bsel):
        pi = next_ps()
        S.op("pe", lambda e: e.matmul(ps[pi][:], lhsT=lhs_re, rhs=cb["FA"][:], start=True, stop=(lhs_im is None)),
             reads=[lhs_buf, b_cb], writes=[b_ps[pi]])
        if lhs_im is not None:
            S.op("pe", lambda e: e.matmul(ps[pi][:], lhsT=lhs_im, rhs=cb["FB"][:], start=False, stop=True),
                 reads=[lhs_buf, b_cb], writes=[b_ps[pi]])
        k = cctr[0] % 2
        cctr[0] += 1
        pv = ps[pi][:].rearrange("p (r k) -> p r k", r=2)
        p1 = P1[k][:].rearrange("p (r k) -> p r k", r=2)
        p2 = P2[k][:].rearrange("p (r k) -> p r k", r=2)
        tre = cf["TW"][:, 0:1, :].broadcast_to([128, 2, 256])
        tim = cf["TW"][:, 1:2, :].broadcast_to([128, 2, 256])
        S.op("dve", lambda e: e.tensor_tensor(out=p1, in0=pv, in1=tre, op=ALU.mult),
             reads=[b_ps[pi], b_const], writes=[b_P[k]])
        S.op("dve", lambda e: e.tensor_tensor(out=p2, in0=pv, in1=tim, op=ALU.mult),
             reads=[b_ps[pi], b_const], writes=[b_P[k]])
        S.op("pool", lambda e: e.tensor_tensor(out=Bb[bsel][:, 0, :], in0=P1[k][:, 0:256], in1=P2[k][:, 256:512], op=ALU.subtract),
             reads=[b_P[k]], writes=[b_Bb[bsel]])
        S.op("pool", lambda e: e.tensor_tensor(out=Bb[bsel][:, 1, :], in0=P2[k][:, 0:256], in1=P1[k][:, 256:512], op=ALU.add),
             reads=[b_P[k]], writes=[b_Bb[bsel]])

    for g in range(NG):
        c0 = g * CG
        gs = slice(g * 2 * CG, (g + 1) * 2 * CG)
        S.op("dve", lambda e, gs=gs: e.tensor_tensor(out=win[:], in0=cf["NEGT"][:].unsqueeze(2).broadcast_to([128, 128, 2 * CG]),
                                                     in1=absdec[:, gs].unsqueeze(1).broadcast_to([128, 128, 2 * CG]), op=ALU.mult),
             reads=[b_const, b_fw], writes=[b_win])
        S.op("act", lambda e: e.activation(out=win[:], in_=win[:], func=AF.Exp), reads=[b_win], writes=[b_win])
        for a16 in range(8):
            pi = next_ps()
            fo = ps[pi][:, 0:16 * 2 * CG].rearrange("p (i s) -> p i s", i=16)
            for i in range(16):
                n2 = a16 * 16 + i
                S.op("pe", lambda e, n2=n2, i=i, pi=pi, gs=gs: e.matmul(ps[pi][:, i * 2 * CG:(i + 1) * 2 * CG], lhsT=h3[:, n2:SEQ:128],
                                                                       rhs=wout[:, gs], start=True, stop=True),
                     reads=[b_h3, b_fw], writes=[b_ps[pi]])
            S.op("dve", lambda e, a16=a16, fo=fo: e.tensor_tensor(
                out=G1f[:, :, a16 * 16:(a16 + 1) * 16].rearrange("p s i -> p i s"), in0=fo,
                in1=win[:, a16 * 16:(a16 + 1) * 16, :], op=ALU.mult),
                 reads=[b_ps[pi], b_win], writes=[b_G1f])
        S.op("pool", lambda e: e.memset(G1f[0:1, CG:2 * CG, 0:1], 0.0), writes=[b_G1f])
        wv = win[:].rearrange("p a s -> p (a s)").rearrange("p (s n) -> p s n", n=128)
        S.op("act", lambda e, wv=wv: e.activation(out=wv, in_=G1f[:], func=AF.Abs),
             reads=[b_G1f], writes=[b_win])
        S.op("dve", lambda e, wv=wv: e.tensor_reduce(out=part[:], in_=wv, axis=AX.X, op=ALU.add),
             reads=[b_win], writes=[b_part])
        pi = next_ps()
        S.op("pe", lambda e, pi=pi: e.matmul(ps[pi][:, 0:2 * CG], lhsT=ones[:], rhs=part[:], start=True, stop=True),
             reads=[b_part, b_fw2], writes=[b_ps[pi]])
        S.op("act", lambda e, pi=pi: e.copy(out=tot[:], in_=ps[pi][:, 0:2 * CG]), reads=[b_ps[pi]], writes=[b_part])
        S.op("dve", lambda e: e.tensor_tensor(out=rn[:], in0=tot[:, 0:CG], in1=tot[:, CG:2 * CG], op=ALU.add),
             reads=[b_part], writes=[b_rn])
        S.op("dve", lambda e: e.tensor_scalar(out=rn[:], in0=rn[:], scalar1=float(NFFT), scalar2=None, op0=ALU.mult),
             reads=[b_rn], writes=[b_rn])
        S.op("dve", lambda e: e.reciprocal(out=rn[:], in_=rn[:]), reads=[b_rn], writes=[b_rn])
        S.op("pool", lambda e: e.tensor_tensor(out=pm[:, 0], in0=G1f[:, 0:CG, :], in1=G1f[:, CG:2 * CG, :], op=ALU.add),
             reads=[b_G1f], writes=[b_pm])
        S.op("pool", lambda e: e.tensor_tensor(out=pm[:, 1], in0=G1f[:, 0:CG, :], in1=G1f[:, CG:2 * CG, :], op=ALU.subtract),
             reads=[b_G1f], writes=[b_pm])
        for b in range(2):
            S.dma("sp", D1f[:, :, b, :], s_in[b, c0:c0 + CG, :].rearrange("c (n1 n2) -> n1 c n2", n2=128),
                  writes=[b_D1f])
        S.op("act", lambda e: e.copy(out=D1b[:], in_=D1f[:]), reads=[b_D1f], writes=[b_D1b])
        F = cb["F128"]
        H = cb["H"]
        tre = cf["TW"][:, 0:1, :].broadcast_to([128, 2, 256])
        tim = cf["TW"][:, 1:2, :].broadcast_to([128, 2, 256])

        def cmul_tw(pi, bsel):
            k = nxt("P", NPB)
            pv = ps[pi][:].rearrange("p (r k) -> p r k", r=2)
            p1 = P1[k][:].rearrange("p (r k) -> p r k", r=2)
            p2 = P2[k][:].rearrange("p (r k) -> p r k", r=2)
            S.op("dve", lambda e: e.tensor_tensor(out=p1, in0=pv, in1=tre, op=ALU.mult), reads=[b_ps[pi], b_const], writes=[b_P1[k]])
            S.op("dve", lambda e: e.tensor_tensor(out=p2, in0=pv, in1=tim, op=ALU.mult), reads=[b_ps[pi], b_const], writes=[b_P2[k]])
            S.op("pool", lambda e: e.tensor_tensor(out=Bb[bsel][:, 0, :], in0=P1[k][:, 0:256], in1=P2[k][:, 256:512], op=ALU.subtract),
                 reads=[b_P1[k], b_P2[k]], writes=[b_Bb[bsel]])
            S.op("pool", lambda e: e.tensor_tensor(out=Bb[bsel][:, 1, :], in0=P2[k][:, 0:256], in1=P1[k][:, 256:512], op=ALU.add),
                 reads=[b_P1[k], b_P2[k]], writes=[b_Bb[bsel]])

        def st1(it):
            if it["kind"] == "f":
                cl = it["cl"]
                it["pa"] = [next_ps(), next_ps()]
                it["bb"] = [nxt("Bb", NBB), nxt("Bb", NBB)]
                for z in range(2):
                    pi = it["pa"][z]
                    S.op("pe", lambda e, pi=pi, z=z, cl=cl: e.matmul(ps[pi][:], lhsT=pm[:, z, cl, :], rhs=cb["FA"][:], start=True, stop=True),
                         reads=[b_pm, b_cb], writes=[b_ps[pi]])
                for z in range(2):
                    cmul_tw(it["pa"][z], it["bb"][z])
            else:
                cl = it["cl"]
                pi = next_ps()
                it["bb"] = [nxt("Bb", NBB)]
                S.op("pe", lambda e, pi=pi, cl=cl: e.matmul(ps[pi][:], lhsT=D1b[:, cl, 0, :], rhs=cb["FA"][:], start=True, stop=False),
                     reads=[b_D1b, b_cb], writes=[b_ps[pi]])
                S.op("pe", lambda e, pi=pi, cl=cl: e.matmul(ps[pi][:], lhsT=D1b[:, cl, 1, :], rhs=cb["FB"][:], start=False, stop=True),
                     reads=[b_D1b, b_cb], writes=[b_ps[pi]])
                cmul_tw(pi, it["bb"][0])

        def st2(it):
            cl = it["cl"]
            pi = next_ps()
            b0 = it["bb"][0]
            b1 = it["bb"][-1]
            S.op("pe", lambda e, pi=pi, b0=b0: e.matmul(ps[pi][:, 0:256], lhsT=F[:, 0, :], rhs=Bb[b0][:, 0, :], start=True, stop=False),
                 reads=[b_Bb[b0], b_cb], writes=[b_ps[pi]])
            S.op("pe", lambda e, pi=pi, b0=b0: e.matmul(ps[pi][:, 0:256], lhsT=F[:, 2, :], rhs=Bb[b0][:, 1, :], start=False, stop=True),
                 reads=[b_Bb[b0], b_cb], writes=[b_ps[pi]])
            S.op("pe", lambda e, pi=pi, b1=b1: e.matmul(ps[pi][:, 256:512], lhsT=F[:, 1, :], rhs=Bb[b1][:, 0, :], start=True, stop=False),
                 reads=[b_Bb[b1], b_cb], writes=[b_ps[pi]])
            S.op("pe", lambda e, pi=pi, b1=b1: e.matmul(ps[pi][:, 256:512], lhsT=F[:, 0, :], rhs=Bb[b1][:, 1, :], start=False, stop=True),
                 reads=[b_Bb[b1], b_cb], writes=[b_ps[pi]])
            if it["kind"] == "f":
                S.op("act", lambda e, pi=pi, cl=cl: e.activation(out=Gh[:, cl].rearrange("p r k -> p (r k)"), in_=ps[pi][:],
                                                                func=AF.Copy, scale=rn[:, cl:cl + 1]),
                     reads=[b_ps[pi], b_rn], writes=[b_Gh])
                return
            k = nxt("P", NPB)
            pk = nxt("Pb", NPBB)
            it["pk"] = pk
            pv = ps[pi][:].rearrange("p (r k) -> p r k", r=2)
            p1 = P1[k][:].rearrange("p (r k) -> p r k", r=2)
            p2 = P2[k][:].rearrange("p (r k) -> p r k", r=2)
            S.op("dve", lambda e, pv=pv, p1=p1, cl=cl: e.tensor_tensor(out=p1, in0=pv, in1=Gh[:, cl, 0:1, :].broadcast_to([128, 2, 256]), op=ALU.mult),
                 reads=[b_ps[pi], b_Gh], writes=[b_P1[k]])
            S.op("dve", lambda e, pv=pv, p2=p2, cl=cl: e.tensor_tensor(out=p2, in0=pv, in1=Gh[:, cl, 1:2, :].broadcast_to([128, 2, 256]), op=ALU.mult),
                 reads=[b_ps[pi], b_Gh], writes=[b_P2[k]])
            S.op("pool", lambda e, k=k, pk=pk: e.tensor_tensor(out=Pb[pk][:, 0, :], in0=P1[k][:, 0:256], in1=P2[k][:, 256:512], op=ALU.subtract),
                 reads=[b_P1[k], b_P2[k]], writes=[b_Pb[pk]])
            S.op("pool", lambda e, k=k, pk=pk: e.tensor_tensor(out=Pb[pk][:, 1, :], in0=P2[k][:, 0:256], in1=P1[k][:, 256:512], op=ALU.add),
                 reads=[b_P1[k], b_P2[k]], writes=[b_Pb[pk]])

        def st3(it):
            if it["kind"] == "f":
                return
            cl, pk = it["cl"], it["pk"]
            q4, ci = divmod(cl, 4)
            cbi = (g * (CG // 4) + q4) % 2
            pi2 = next_ps()
            for j in range(2):
                S.op("pe", lambda e, pi2=pi2, j=j, pk=pk: e.matmul(ps[pi2][:, j * 256:(j + 1) * 256], lhsT=Pb[pk][:, 0, j * 128:(j + 1) * 128],
                                                                 rhs=cb["GA"][:], start=True, stop=False),
                     reads=[b_Pb[pk], b_cb], writes=[b_ps[pi2]])
                S.op("pe", lambda e, pi2=pi2, j=j, pk=pk: e.matmul(ps[pi2][:, j * 256:(j + 1) * 256], lhsT=Pb[pk][:, 1, j * 128:(j + 1) * 128],
                                                                 rhs=cb["GB"][:], start=False, stop=True),
                     reads=[b_Pb[pk], b_cb], writes=[b_ps[pi2]])
            k = nxt("P", NPB)
            cv = ps[pi2][:].rearrange("p (j r n) -> p j r n", j=2, r=2)
            p1 = P1[k][:].rearrange("p (j r n) -> p j r n", j=2, r=2)
            p2 = P2[k][:].rearrange("p (j r n) -> p j r n", j=2, r=2)
            S.op("dve", lambda e, cv=cv, p1=p1: e.tensor_tensor(out=p1, in0=cv, in1=cf["ITW"][:, :, 0:1, :].broadcast_to([128, 2, 2, 128]), op=ALU.mult),
                 reads=[b_ps[pi2], b_const], writes=[b_P1[k]])
            S.op("dve", lambda e, cv=cv, p2=p2: e.tensor_tensor(out=p2, in0=cv, in1=cf["ITW"][:, :, 1:2, :].broadcast_to([128, 2, 2, 128]), op=ALU.mult),
                 reads=[b_ps[pi2], b_const], writes=[b_P2[k]])
            S.op("pool", lambda e, p1=p1, p2=p2, ci=ci, cbi=cbi: e.tensor_tensor(out=Cb[cbi][:, :, 0, ci, :], in0=p1[:, :, 0, :], in1=p2[:, :, 1, :], op=ALU.subtract),
                 reads=[b_P1[k], b_P2[k]], writes=[b_Cb[cbi]])
            S.op("pool", lambda e, p1=p1, p2=p2, ci=ci, cbi=cbi: e.tensor_tensor(out=Cb[cbi][:, :, 1, ci, :], in0=p2[:, :, 0, :], in1=p1[:, :, 1, :], op=ALU.add),
                 reads=[b_P1[k], b_P2[k]], writes=[b_Cb[cbi]])
            if ci != 3:
                return
            pr, pim = next_ps(), next_ps()
            seq = [(pr, 0, 0, True), (pr, 2, 1, False), (pim, 1, 0, True), (pim, 0, 1, False)]
            for (pp, hsel, ri, first) in seq:
                for j in range(2):
                    S.op("pe", lambda e, pp=pp, hsel=hsel, ri=ri, j=j, first=first, cbi=cbi: e.matmul(
                        ps[pp][:], lhsT=H[:, j, hsel, :], rhs=Cb[cbi][:, j, ri, :, :].rearrange("p c n -> p (c n)"),
                        start=(first and j == 0), stop=((not first) and j == 1)),
                         reads=[b_Cb[cbi], b_cb], writes=[b_ps[pp]])
            S.op("act", lambda e, pr=pr, cbi=cbi: e.copy(out=yb[cbi][:, :, 0, :], in_=ps[pr][:].rearrange("p (c n) -> p c n", c=4)),
                 reads=[b_ps[pr]], writes=[b_yb[cbi]])
            S.op("act", lambda e, pim=pim, cbi=cbi: e.copy(out=yb[cbi][:, :, 1, :], in_=ps[pim][:].rearrange("p (c n) -> p c n", c=4)),
                 reads=[b_ps[pim]], writes=[b_yb[cbi]])
            for b in range(2):
                cc = c0 + q4 * 4
                S.dma("sp", y_out[b, cc:cc + 4, :].rearrange("c (n1 n2) -> n1 c n2", n2=128), yb[cbi][:, :, b, :],
                      reads=[b_yb[cbi]], is_output=True)

        items = [{"kind": "f", "cl": cl} for cl in range(CG)] + [{"kind": "d", "cl": cl} for cl in range(CG)]
        run_pipeline(items, [st1, st2, st3])
    S.finish()
    return nc


def run_B(inp, s_cs):
    nc = build_B()
    consts = fft_consts()
    featT = filter_feat()
    fbf = np.stack([inp["hy_f_b1"][0], inp["hy_f_b2"][0], inp["hy_f_b3"][0], inp["hy_f_freq"][0]], 1).astype(np.float32)
    in_maps = []
    for c in range(NCORES):
        wo = inp["hy_f_wout"][0].reshape(64, 2, 8, NG, CG)[:, :, c]
        wo = np.ascontiguousarray(wo.transpose(0, 2, 1, 3).reshape(64, NG * 2 * CG))
        de = inp["hy_decay"][0].reshape(2, 8, NG, CG)[:, c]
        de = np.ascontiguousarray(de.transpose(1, 0, 2).reshape(1, NG * 2 * CG))
        m = {"s_in": np.ascontiguousarray(s_cs[c]), "featT": featT,
             "f_w1": np.ascontiguousarray(inp["hy_f_w1"][0]), "f_w2": np.ascontiguousarray(inp["hy_f_w2"][0]),
             "f_w3": np.ascontiguousarray(inp["hy_f_w3"][0]), "f_bf": np.ascontiguousarray(fbf),
             "f_wout": wo, "decay": de}
        for k, v in consts.items():
            m["c_" + k] = v
        in_maps.append(m)
    res = _run(nc, in_maps, "B")
    return res.results


def load_weight_bf16(S, nc, w_dram, rows, cols, name, bufname, qeng="pool", ceng="pool", stage=None, scols=1024):
    nk = rows // 128
    wt = nc.alloc_sbuf_tensor(name + "_sb", [128, nk, cols], BF16)
    b_w = S.buf(bufname)
    for k in range(nk):
        S.dma("pool", wt[:, k, :], w_dram[k * 128:(k + 1) * 128, :], writes=[b_w], sem_buf=b_w)
    return wt, b_w, None


class NormT:
    def __init__(self, S, nc, gt, b_gt, idb, b_idb, tag, ntp=2, alias_sq=False):
        A = nc.alloc_sbuf_tensor
        self.S, self.nc = S, nc
        self.ntp = ntp
        self.alias_sq = alias_sq
        self.gt, self.b_gt, self.idb, self.b_idb = gt, b_gt, idb, b_idb
        self.hb = [A(tag + "hb%d" % i, [128, D], BF16) for i in range(2)]
        self.b_hb = [S.buf(tag + "hb0"), S.buf(tag + "hb1")]
        if not alias_sq:
            self.sq = A(tag + "sq", [128, D], F32)
            self.b_sq = S.buf(tag + "sq")
        self.ss = [A(tag + "ss%d" % i, [128, 1], F32) for i in range(2)]
        self.rs = [A(tag + "rs%d" % i, [128, 1], F32) for i in range(2)]
        self.b_s = [S.buf(tag + "s0"), S.buf(tag + "s1")]
        self.tp = [nc.alloc_psum_tensor(tag + "tp%d" % i, [128, 8 * 128], BF16) for i in range(ntp)]
        self.b_tp = [S.buf(tag + "tp%d" % i) for i in range(ntp)]
        self.n = 0

    def rstd(self, x_ap, b_x, j):
        S = self.S
        ss, rs = self.ss[j], self.rs[j]
        sq, b_sq = (self.hb[j], self.b_hb[j]) if self.alias_sq else (self.sq, self.b_sq)
        S.op("act", lambda e: e.activation(out=sq[:], in_=x_ap, func=AF.Square, accum_out=ss[:]),
             reads=[b_x], writes=[b_sq, self.b_s[j]])
        S.op("act", lambda e: e.activation(out=rs[:], in_=ss[:], func=AF.Sqrt, scale=1.0 / D, bias=EPS),
             reads=[self.b_s[j]], writes=[self.b_s[j]])
        S.op("dve", lambda e: e.reciprocal(out=rs[:], in_=rs[:]), reads=[self.b_s[j]], writes=[self.b_s[j]])
        return rs

    def __call__(self, x_ap, b_x, hT_dst, b_hT):
        S = self.S
        j = self.n % 2
        self.n += 1
        rs = self.rstd(x_ap, b_x, j)
        hb, tp = self.hb[j], self.tp[j % self.ntp]
        b_tp = self.b_tp[j % self.ntp]
        gt, idb = self.gt, self.idb
        S.op("dve", lambda e: e.scalar_tensor_tensor(out=hb[:], in0=x_ap, scalar=rs[:, 0:1], in1=gt[:], op0=ALU.mult, op1=ALU.mult),
             reads=[b_x, self.b_s[j], self.b_gt], writes=[self.b_hb[j]])
        for k in range(8):
            S.op("pe", lambda e, k=k: e.transpose(out=tp[:, k * 128:(k + 1) * 128], in_=hb[:, k * 128:(k + 1) * 128], identity=idb[:]),
                 reads=[self.b_hb[j], self.b_idb], writes=[b_tp])
        S.op("act", lambda e: e.copy(out=hT_dst, in_=tp[:].rearrange("p (k t) -> p k t", k=8)),
             reads=[b_tp], writes=[b_hT])


def load_consts_common(S, nc, gnorm_d, ident_d):
    A = nc.alloc_sbuf_tensor
    gt = A("gt", [128, D], F32)
    idf = A("idf", [128, 128], F32)
    idb = A("idb", [128, 128], BF16)
    b_gt, b_idf, b_idb = S.buf("gt"), S.buf("idf"), S.buf("idb")
    S.dma("sp", gt[:], gnorm_d, writes=[b_gt])
    S.dma("sp", idf[:], ident_d, writes=[b_idf])
    S.op("dve", lambda e: e.tensor_copy(out=idb[:], in_=idf[:]), reads=[b_idf], writes=[b_idb])
    return gt, b_gt, idb, b_idb


FBLK = 512


def phase_F(nc, S, io, ntok, final_norm):
    x_in, gnorm, ident_d, wg_d, wu_d, wd_d, x_out = (io[k] for k in ("x_in", "gnorm", "ident", "wg", "wu", "wd", "x_out"))
    gfin_d = io.get("gfin")
    A = nc.alloc_sbuf_tensor
    gt, b_gt, idb, b_idb = load_consts_common(S, nc, gnorm, ident_d)
    if final_norm:
        gf = A("gf", [128, D], F32)
        b_gf = S.buf("gf")
        S.dma("sp", gf[:], gfin_d, writes=[b_gf])
    wg, b_wg, stg = load_weight_bf16(S, nc, wg_d, D, FF, "wg", "wg", ceng=("pool", "act"), scols=512)
    wu, b_wu, stg = load_weight_bf16(S, nc, wu_d, D, FF, "wu", "wu", ceng=("pool", "act"), stage=stg)
    wd, b_wd, stg = load_weight_bf16(S, nc, wd_d, FF, D, "wd", "wd", ceng=("pool", "act"), stage=stg)
    NF = FF // 128
    nt = FBLK // 128
    NXT = nt
    xt = [A("xt%d" % i, [128, D], F32) for i in range(NXT)]
    b_xt = [S.buf("xt%d" % i) for i in range(NXT)]
    hT = A("hT", [128, 8, FBLK], BF16)
    b_hT = S.buf("hT")
    aT = A("aT", [128, NF, FBLK], BF16)
    b_aT = S.buf("aT")
    sg = [A("sg%d" % i, [128, FBLK], F32) for i in range(2)]
    b_sg = [S.buf("sg0"), S.buf("sg1")]
    norm = NormT(S, nc, gt, b_gt, idb, b_idb, "n", alias_sq=True)
    gp = [nc.alloc_psum_tensor("gp%d" % i, [128, 512], F32) for i in range(4)]
    b_gp = [S.buf("gp%d" % i) for i in range(4)]
    dp = [nc.alloc_psum_tensor("dp%d" % i, [128, 512], F32) for i in range(2)]
    b_dp = [S.buf("dp0"), S.buf("dp1")]
    tctr = 0
    for blk in range(ntok // FBLK):
        slots = []
        for t in range(nt):
            tok0 = blk * FBLK + t * 128
            sl = tctr % NXT
            tctr += 1
            slots.append(sl)
            S.dma("sp", xt[sl][:], x_in[tok0:tok0 + 128, :], writes=[b_xt[sl]])
            norm(xt[sl][:], b_xt[sl], hT[:, :, t * 128:(t + 1) * 128], b_hT)
        for f in range(NF):
            g_i, u_i = (2 * f) % 4, (2 * f + 1) % 4
            for k in range(8):
                S.op("pe", lambda e, f=f, k=k, g_i=g_i: e.matmul(gp[g_i][:, 0:FBLK], lhsT=wg[:, k, f * 128:(f + 1) * 128], rhs=hT[:, k, :],
                                                                start=(k == 0), stop=(k == 7)),
                     reads=[b_wg, b_hT], writes=[b_gp[g_i]])
            for k in range(8):
                S.op("pe", lambda e, f=f, k=k, u_i=u_i: e.matmul(gp[u_i][:, 0:FBLK], lhsT=wu[:, k, f * 128:(f + 1) * 128], rhs=hT[:, k, :],
                                                                start=(k == 0), stop=(k == 7)),
                     reads=[b_wu, b_hT], writes=[b_gp[u_i]])
            si = f % 2
            S.op("act", lambda e, g_i=g_i, si=si: e.activation(out=sg[si][:], in_=gp[g_i][:, 0:FBLK], func=AF.Silu),
                 reads=[b_gp[g_i]], writes=[b_sg[si]])
            S.op("dve", lambda e, f=f, u_i=u_i, si=si: e.tensor_tensor(out=aT[:, f, :], in0=sg[si][:], in1=gp[u_i][:, 0:FBLK], op=ALU.mult),
                 reads=[b_sg[si], b_gp[u_i]], writes=[b_aT])
        for t in range(nt):
            tok0 = blk * FBLK + t * 128
            sl = slots[t]
            for hf in range(2):
                for f in range(NF):
                    S.op("pe", lambda e, f=f, hf=hf, t=t: e.matmul(dp[hf][:], lhsT=aT[:, f, t * 128:(t + 1) * 128], rhs=wd[:, f, hf * 512:(hf + 1) * 512],
                                                                  start=(f == 0), stop=(f == NF - 1)),
                         reads=[b_aT, b_wd], writes=[b_dp[hf]])
            for hf in range(2):
                S.op("dve", lambda e, hf=hf, sl=sl: e.tensor_tensor(out=xt[sl][:, hf * 512:(hf + 1) * 512], in0=dp[hf][:],
                                                                   in1=xt[sl][:, hf * 512:(hf + 1) * 512], op=ALU.add),
                     reads=[b_dp[hf], b_xt[sl]], writes=[b_xt[sl]])
            if final_norm:
                o = t % 2
                rs = norm.rstd(xt[sl][:], b_xt[sl], o)
                S.op("dve", lambda e, sl=sl, rs=rs: e.scalar_tensor_tensor(out=xt[sl][:], in0=xt[sl][:], scalar=rs[:, 0:1], in1=gf[:],
                                                                          op0=ALU.mult, op1=ALU.mult),
                     reads=[b_xt[sl], norm.b_s[o], b_gf], writes=[b_xt[sl]])
            S.dma("pool", x_out[tok0:tok0 + 128, :], xt[sl][:], reads=[b_xt[sl]], is_output=True)


def build_F(final_norm):
    nc = bass.Bass("TRN2", target_bir_lowering=False)
    S = Sched(nc)
    DT = nc.dram_tensor
    io = {"x_in": DT("x_in", [TOK, D], F32, kind="ExternalInput").ap(),
          "gnorm": DT("gnorm", [128, D], F32, kind="ExternalInput").ap(),
          "ident": DT("ident", [128, 128], F32, kind="ExternalInput").ap(),
          "wg": DT("wg", [D, FF], F32, kind="ExternalInput").ap(),
          "wu": DT("wu", [D, FF], F32, kind="ExternalInput").ap(),
          "wd": DT("wd", [FF, D], F32, kind="ExternalInput").ap()}
    if final_norm:
        io["gfin"] = DT("gfin", [128, D], F32, kind="ExternalInput").ap()
    io["x_out"] = DT("x_out", [TOK, D], F32, kind="ExternalOutput").ap()
    phase_F(nc, S, io, TOK, final_norm)
    S.finish()
    return nc


def rep_rows(v):
    return np.ascontiguousarray(np.broadcast_to(np.asarray(v, np.float32)[None, :], (128, v.shape[0])))


def run_F(inp, layer, x_tok, final_norm):
    nc = build_F(final_norm)
    ident = np.eye(128, dtype=np.float32)
    base = {"gnorm": rep_rows(inp["norm_ffn"][layer]), "ident": ident,
            "wg": np.ascontiguousarray(inp["ffn_w_gate"][layer]), "wu": np.ascontiguousarray(inp["ffn_w_up"][layer]),
            "wd": np.ascontiguousarray(inp["ffn_w_down"][layer])}
    if final_norm:
        base["gfin"] = rep_rows(inp["norm_final"])
    in_maps = [dict(base, x_in=np.ascontiguousarray(x_tok[c])) for c in range(NCORES)]
    res = _run(nc, in_maps, "F")
    return [r["x_out"] for r in res.results]


def phase_C(nc, S, io, ntok):
    x_in, yT, sT, x0T, skip_d, wo_d, bo_d, x_out = (io[k] for k in ("x_in", "yT", "sT", "x0T", "skip", "w_out", "b_out", "x_out"))
    A = nc.alloc_sbuf_tensor
    wo, b_wo, _ = load_weight_bf16(S, nc, wo_d, D, D, "wo", "wo")
    skip = A("skip_sb", [128, 8], F32)
    bof = A("bof", [1, D], F32)
    bob = A("bob", [1, D], BF16)
    onesb = A("onesb", [1, 128], BF16)
    b_c = S.buf("c")
    S.dma("sp", skip[:], skip_d, writes=[b_c], sem_buf=b_c)
    S.dma("sp", bof[:], bo_d, writes=[b_c], sem_buf=b_c)
    b_c2 = S.buf("c2")
    S.op("dve", lambda e: e.tensor_copy(out=bob[:], in_=bof[:]), reads=[b_c], writes=[b_c2])
    S.op("pool", lambda e: e.memset(onesb[:], 1.0), writes=[b_c2])
    BL = 512
    NI = 4
    yt = [A("yt%d" % i, [128, BL], F32) for i in range(NI)]
    st_ = [A("st%d" % i, [128, BL], F32) for i in range(NI)]
    x0t = [A("x0t%d" % i, [128, BL], F32) for i in range(NI)]
    b_in = [S.buf("in%d" % i) for i in range(NI)]
    tmp = [A("tmp%d" % i, [128, BL], F32) for i in range(NI)]
    b_tmp = [S.buf("tmp%d" % i) for i in range(NI)]
    uT = [A("uT%d" % i, [128, 8, BL], BF16) for i in range(2)]
    b_uT = [S.buf("uT0"), S.buf("uT1")]
    xt = [A("xt%d" % i, [128, D], F32) for i in range(2)]
    b_xt = [S.buf("xt0"), S.buf("xt1")]
    ot = [A("ot%d" % i, [128, D], F32) for i in range(2)]
    b_ot = [S.buf("ot0"), S.buf("ot1")]
    mp = [nc.alloc_psum_tensor("mp%d" % i, [128, 512], F32) for i in range(4)]
    b_mp = [S.buf("mp%d" % i) for i in range(4)]
    n = 0
    tc_ = 0
    for blk in range(ntok // BL):
        ub = blk % 2
        cs = slice(blk * BL, (blk + 1) * BL)
        for k in range(8):
            i = n % NI
            n += 1
            rs_ = slice(k * 128, (k + 1) * 128)
            S.dma("sp", yt[i][:], yT[rs_, cs], writes=[b_in[i]], sem_buf=b_in[i])
            S.dma("sp", st_[i][:], sT[rs_, cs], writes=[b_in[i]], sem_buf=b_in[i])
            S.dma("sp", x0t[i][:], x0T[rs_, cs], writes=[b_in[i]], sem_buf=b_in[i])
            S.op("dve", lambda e, i=i, k=k: e.scalar_tensor_tensor(out=tmp[i][:], in0=st_[i][:], scalar=skip[:, k:k + 1], in1=yt[i][:],
                                                                  op0=ALU.mult, op1=ALU.add),
                 reads=[b_in[i], b_c], writes=[b_tmp[i]])
            S.op("pool", lambda e, i=i, k=k, ub=ub: e.tensor_tensor(out=uT[ub][:, k, :], in0=tmp[i][:], in1=x0t[i][:], op=ALU.mult),
                 reads=[b_tmp[i], b_in[i]], writes=[b_uT[ub]])
        for t in range(BL // 128):
            tok0 = blk * BL + t * 128
            j = tc_ % 2
            tc_ += 1
            S.dma("sp", xt[j][:], x_in[tok0:tok0 + 128, :], writes=[b_xt[j]])
            for hf in range(2):
                m = 2 * j + hf
                for k in range(8):
                    S.op("pe", lambda e, m=m, k=k, hf=hf, t=t, ub=ub: e.matmul(mp[m][:], lhsT=uT[ub][:, k, t * 128:(t + 1) * 128],
                                                                              rhs=wo[:, k, hf * 512:(hf + 1) * 512], start=(k == 0), stop=False),
                         reads=[b_uT[ub], b_wo], writes=[b_mp[m]])
                S.op("pe", lambda e, m=m, hf=hf: e.matmul(mp[m][:], lhsT=onesb[:], rhs=bob[:, hf * 512:(hf + 1) * 512], start=False, stop=True),
                     reads=[b_c2], writes=[b_mp[m]])
                S.op("dve", lambda e, m=m, hf=hf, j=j: e.tensor_tensor(out=ot[j][:, hf * 512:(hf + 1) * 512], in0=mp[m][:],
                                                                      in1=xt[j][:, hf * 512:(hf + 1) * 512], op=ALU.add),
                     reads=[b_mp[m], b_xt[j]], writes=[b_ot[j]])
            S.dma("pool", x_out[tok0:tok0 + 128, :], ot[j][:], reads=[b_ot[j]], is_output=True)


def build_C():
    nc = bass.Bass("TRN2", target_bir_lowering=False)
    S = Sched(nc)
    DT = nc.dram_tensor
    io = {"x_in": DT("x_in", [TOK, D], F32, kind="ExternalInput").ap(),
          "yT": DT("yT", [D, TOK], F32, kind="ExternalInput").ap(),
          "sT": DT("sT", [D, TOK], F32, kind="ExternalInput").ap(),
          "x0T": DT("x0T", [D, TOK], F32, kind="ExternalInput").ap(),
          "skip": DT("skip", [128, 8], F32, kind="ExternalInput").ap(),
          "w_out": DT("w_out", [D, D], F32, kind="ExternalInput").ap(),
          "b_out": DT("b_out", [1, D], F32, kind="ExternalInput").ap(),
          "x_out": DT("x_out", [TOK, D], F32, kind="ExternalOutput").ap()}
    phase_C(nc, S, io, TOK)
    S.finish()
    return nc


def run_C(inp, x_tok, yT, sT, x0T):
    nc = build_C()
    base = {"skip": np.ascontiguousarray(inp["hy_skip"][0].reshape(8, 128).T), "w_out": np.ascontiguousarray(inp["hy_w_out"][0]),
            "b_out": np.ascontiguousarray(inp["hy_b_out"][0][None, :])}
    in_maps = [dict(base, x_in=np.ascontiguousarray(x_tok[c]), yT=np.ascontiguousarray(yT[c]), sT=np.ascontiguousarray(sT[c]),
                    x0T=np.ascontiguousarray(x0T[c])) for c in range(NCORES)]
    res = _run(nc, in_maps, "C")
    return [r["x_out"] for r in res.results]


NROWS_LOC = 72
NEG = -30000.0


def na_tables(rpb):
    H = 16
    j = np.arange(64)
    w = np.arange(64)
    cs = np.clip(w - 8, 0, 48)
    colok = (j[:, None] >= cs[None, :]) & (j[:, None] < cs[None, :] + 16)
    coff = np.clip(j[:, None] - w[None, :] + 15, 0, 30)
    out = np.full((2, 64, H, 7, 2, 64), NEG, np.float32)
    for idx in range(7):
        d0 = -6 + 2 * idx
        for i2 in range(2):
            for q2 in range(2):
                dl = d0 + i2 - q2
                if abs(dl) > 7:
                    continue
                g = rpb[:, dl + 7, :][:, coff]
                g = np.where(colok[None], g, np.float32(NEG))
                out[i2, :, :, idx, q2, :] = g.transpose(1, 0, 2)
    return np.ascontiguousarray(out.reshape(128, H, 7, 128))


def na_mlist(p):
    if p == 0:
        return list(range(0, 6))
    if p == 31:
        return list(range(-1, 5))
    return list(range(0, 5))


def na_rowmask(q):
    R0 = 64 * q
    m = np.zeros((128, 32, 6, 2), np.float32)
    for p in range(32):
        for mi, mm in enumerate(na_mlist(p)):
            for i2 in range(2):
                for q2 in range(2):
                    gr = R0 + 2 * p + q2
                    rs = min(max(gr - 4, 0), 248)
                    kr = R0 - 4 + 2 * p + 2 * mm + i2
                    if rs <= kr < rs + 8:
                        m[i2 * 64:(i2 + 1) * 64, p, mi, q2] = 1.0
    return m


def phase_D(nc, S, io):
    xe, gnorm, ident_d, wqkv_d, bqk_d, bv_d, wo_d, bo_d, bt_d, rm_d, x_out = (io[k] for k in (
        "xe", "gnorm", "ident", "w_qkv", "b_qk", "b_v", "w_o", "b_o", "bt", "rowmask", "x_out"))
    A = nc.alloc_sbuf_tensor
    B = S.buf
    gt, b_gt, idb, b_idb = load_consts_common(S, nc, gnorm, ident_d)
    wqkv, b_wqkv, _ = load_weight_bf16(S, nc, wqkv_d, D, 3 * D, "wqkv", "wqkv")
    wo, b_wo, _ = load_weight_bf16(S, nc, wo_d, D, D, "wo", "wo")
    BTb = A("BTb", [128, 16, 7, 128], BF16)
    b_BT = B("BT")
    for h0 in range(0, 16, 4):
        S.dma("pool", BTb[:, h0:h0 + 4], bt_d[:, h0:h0 + 4], writes=[b_BT], sem_buf=b_BT)
    rmask = A("rmask", [128, 32, 6, 2], F32)
    bqk = A("bqk", [128, 16], F32)
    bq8 = A("bq8", [128, 8], F32)
    bvb = A("bvb", [1, D], BF16)
    bob = A("bob", [1, D], BF16)
    onesr = A("onesr", [1, 128], BF16)
    onesc = A("onesc", [128, 64], BF16)
    b_c, b_c2 = B("c"), B("c2")
    for dst, src in ((rmask, rm_d), (bqk, bqk_d)):
        S.dma("sp", dst[:], src, writes=[b_c], sem_buf=b_c)
    S.op("dve", lambda e: e.tensor_scalar(out=bq8[:], in0=bqk[:, 0:8], scalar1=0.125, scalar2=None, op0=ALU.mult), reads=[b_c], writes=[b_c2])
    for dstb, src in ((bvb, bv_d), (bob, bo_d)):
        S.dma("pool", dstb[:], src, writes=[b_c2], sem_buf=b_c2)
    S.op("pool", lambda e: e.memset(onesr[:], 1.0), writes=[b_c2])
    S.op("pool", lambda e: e.memset(onesc[:], 1.0), writes=[b_c2])

    KT = [A("KT%d" % i, [128, 8, 512], BF16) for i in range(2)]
    VV = [A("VV%d" % i, [128, 4, D], BF16) for i in range(2)]
    QQ = [A("QQ%d" % i, [128, 8, 512], BF16) for i in range(2)]
    b_KT, b_VV, b_QQ = [B("KT0"), B("KT1")], [B("VV0"), B("VV1")], [B("QQ0"), B("QQ1")]
    hT = A("hT", [128, 8, 512], BF16)
    b_hT = B("hT")
    aT = A("aT", [128, 8, 512], BF16)
    b_aT = B("aT")
    xt = [A("xt%d" % i, [128, D], F32) for i in range(2)]
    b_xt = [B("xt0"), B("xt1")]
    ot = [A("ot0", [128, D], F32)]
    b_ot = [B("ot0")]
    NEB = 3
    Eb = [A("Eb%d" % i, [128, 512], BF16) for i in range(NEB)]
    b_Eb = [B("Eb%d" % i) for i in range(NEB)]
    rz = [A("rz%d" % i, [128, 128], F32) for i in range(2)]
    b_rz = [B("rz0"), B("rz1")]
    norm = NormT(S, nc, gt, b_gt, idb, b_idb, "n", ntp=1)
    pp = [nc.alloc_psum_tensor("pp%d" % i, [128, 512], F32) for i in range(2)]
    b_pp = [B("pp0"), B("pp1")]
    sp_ = [nc.alloc_psum_tensor("sps%d" % i, [128, 512], F32) for i in range(3)]
    b_sp = [B("sps%d" % i) for i in range(3)]
    obank = nc.alloc_psum_tensor("obank", [128, 512], F32)
    zbank = nc.alloc_psum_tensor("zbank", [128, 512], F32)
    b_oz = [B("oz0"), B("oz1")]
    cnt = {"pp": 0, "sp": 0, "e": 0, "oz": 0, "x": 0, "o": 0}

    def nxt(k, n):
        v = cnt[k] % n
        cnt[k] += 1
        return v

    def project(kb):
        rb = kb % 2
        for t in range(4):
            j = nxt("x", 2)
            tok0 = kb * 512 + t * 128
            S.dma("sp", xt[j][:], xe[tok0:tok0 + 128, :], writes=[b_xt[j]])
            norm(xt[j][:], b_xt[j], hT[:, :, t * 128:(t + 1) * 128], b_hT)
        for c in range(8):
            pi = nxt("pp", 2)
            for k in range(8):
                S.op("pe", lambda e, pi=pi, k=k, c=c: e.matmul(pp[pi][:], lhsT=wqkv[:, k, D + c * 128:D + (c + 1) * 128], rhs=hT[:, k, :],
                                                              start=(k == 0), stop=(k == 7)),
                     reads=[b_wqkv, b_hT], writes=[b_pp[pi]])
            S.op("act", lambda e, pi=pi, c=c, rb=rb: e.activation(out=KT[rb][:, c, :], in_=pp[pi][:], func=AF.Identity,
                                                                 bias=bqk[:, 8 + c:9 + c], scale=1.0),
                 reads=[b_pp[pi], b_c], writes=[b_KT[rb]])
            pi = nxt("pp", 2)
            for k in range(8):
                S.op("pe", lambda e, pi=pi, k=k, c=c: e.matmul(pp[pi][:], lhsT=wqkv[:, k, c * 128:(c + 1) * 128], rhs=hT[:, k, :],
                                                              start=(k == 0), stop=(k == 7)),
                     reads=[b_wqkv, b_hT], writes=[b_pp[pi]])
            if kb >= 1:
                S.op("act", lambda e, pi=pi, c=c, kb=kb: e.activation(out=QQ[(kb - 1) % 2][:, c, 256:512], in_=pp[pi][:, 0:256], func=AF.Identity,
                                                                     bias=bq8[:, c:c + 1], scale=0.125),
                     reads=[b_pp[pi], b_c2], writes=[b_QQ[(kb - 1) % 2]])
            if kb <= 7:
                S.op("act", lambda e, pi=pi, c=c, kb=kb: e.activation(out=QQ[kb % 2][:, c, 0:256], in_=pp[pi][:, 256:512], func=AF.Identity,
                                                                     bias=bq8[:, c:c + 1], scale=0.125),
                     reads=[b_pp[pi], b_c2], writes=[b_QQ[kb % 2]])
        for t in range(4):
            for hf in range(2):
                pi = nxt("pp", 2)
                for k in range(8):
                    S.op("pe", lambda e, pi=pi, k=k, t=t, hf=hf: e.matmul(pp[pi][:], lhsT=hT[:, k, t * 128:(t + 1) * 128],
                                                                         rhs=wqkv[:, k, 2 * D + hf * 512:2 * D + (hf + 1) * 512],
                                                                         start=(k == 0), stop=False),
                         reads=[b_wqkv, b_hT], writes=[b_pp[pi]])
                S.op("pe", lambda e, pi=pi, hf=hf: e.matmul(pp[pi][:], lhsT=onesr[:], rhs=bvb[:, hf * 512:(hf + 1) * 512], start=False, stop=True),
                     reads=[b_c2], writes=[b_pp[pi]])
                S.op("act", lambda e, pi=pi, t=t, hf=hf, rb=rb: e.copy(out=VV[rb][:, t, hf * 512:(hf + 1) * 512], in_=pp[pi][:]),
                     reads=[b_pp[pi]], writes=[b_VV[rb]])

    def attend(bq):
        qb = bq % 2
        units = []
        for pl in range(4):
            p = 4 * bq + pl
            ml = na_mlist(p)
            for c in range(8):
                o_i = nxt("oz", 2)
                for hp in range(2):
                    for gi, grp in enumerate([ml[0:4], ml[4:]]):
                        units.append({"pl": pl, "p": p, "c": c, "hp": hp, "gi": gi, "grp": grp, "o_i": o_i, "nmm": len(ml),
                                      "last": hp == 1 and gi == 1})

        def u1(u):
            pl, c, hp, grp = u["pl"], u["c"], u["hp"], u["grp"]
            h = 2 * c + hp
            hs = slice(hp * 64, (hp + 1) * 64)
            si = nxt("sp", 3)
            u["si"] = si
            for mi, mm in enumerate(grp):
                blk, tl = bq + (pl + mm) // 4, (pl + mm) % 4
                S.op("pe", lambda e, si=si, mi=mi, h=h, mm=mm: e.matmul(sp_[si][:, mi * 128:(mi + 1) * 128], lhsT=idb[:], rhs=BTb[:, h, mm + 1, :],
                                                                      start=True, stop=False),
                     reads=[b_idb, b_BT], writes=[b_sp[si]])
                S.op("pe", lambda e, si=si, mi=mi, blk=blk, tl=tl, hs=hs, c=c, pl=pl: e.matmul(
                    sp_[si][:, mi * 128:(mi + 1) * 128], lhsT=KT[blk % 2][hs, c, tl * 128:(tl + 1) * 128],
                    rhs=QQ[qb][hs, c, pl * 128:(pl + 1) * 128], start=False, stop=True),
                     reads=[b_KT[blk % 2], b_QQ[qb]], writes=[b_sp[si]])

        def u2(u):
            p, gi, grp, si = u["p"], u["gi"], u["grp"], u["si"]
            ncol = len(grp) * 128
            ei = nxt("e", NEB)
            u["ei"] = ei
            S.op("act", lambda e, ei=ei, si=si, ncol=ncol: e.activation(out=Eb[ei][:, 0:ncol], in_=sp_[si][:, 0:ncol], func=AF.Exp),
                 reads=[b_sp[si]], writes=[b_Eb[ei]])
            nmask = 1 if (gi == 0 and 2 <= p <= 29) else len(grp)
            mi0 = 4 * gi
            S.op("dve", lambda e, ei=ei, p=p, mi0=mi0, n=nmask: e.tensor_tensor(
                out=Eb[ei][:, 0:n * 128].rearrange("p (a q w) -> p a q w", q=2, w=64),
                in0=Eb[ei][:, 0:n * 128].rearrange("p (a q w) -> p a q w", q=2, w=64),
                in1=rmask[:, p, mi0:mi0 + n, :].unsqueeze(3).broadcast_to([128, n, 2, 64]), op=ALU.mult),
                 reads=[b_Eb[ei], b_c], writes=[b_Eb[ei]])

        def u3(u):
            pl, c, hp, gi, grp, ei, o_i, nmm = u["pl"], u["c"], u["hp"], u["gi"], u["grp"], u["ei"], u["o_i"], u["nmm"]
            h = 2 * c + hp
            hs = slice(hp * 64, (hp + 1) * 64)
            for mi, mm in enumerate(grp):
                blk, tl = bq + (pl + mm) // 4, (pl + mm) % 4
                done = 4 * gi + mi + 1
                S.op("pe", lambda e, o_i=o_i, ei=ei, mi=mi, blk=blk, tl=tl, hs=hs, h=h, st=(done == 1), sp=(done == nmm): e.matmul(
                    obank[hs, o_i * 128:(o_i + 1) * 128], lhsT=VV[blk % 2][:, tl, h * 64:(h + 1) * 64], rhs=Eb[ei][:, mi * 128:(mi + 1) * 128],
                    start=st, stop=sp),
                     reads=[b_VV[blk % 2], b_Eb[ei]], writes=[b_oz[o_i]])
                S.op("pe", lambda e, o_i=o_i, ei=ei, mi=mi, hs=hs, st=(done == 1), sp=(done == nmm): e.matmul(
                    zbank[hs, o_i * 128:(o_i + 1) * 128], lhsT=onesc[:], rhs=Eb[ei][:, mi * 128:(mi + 1) * 128], start=st, stop=sp),
                     reads=[b_c2, b_Eb[ei]], writes=[b_oz[o_i]])
            if u["last"]:
                S.op("dve", lambda e, o_i=o_i: e.reciprocal(out=rz[o_i][:], in_=zbank[:, o_i * 128:(o_i + 1) * 128]), reads=[b_oz[o_i]], writes=[b_rz[o_i]])
                S.op("dve", lambda e, o_i=o_i, c=c, pl=pl: e.tensor_tensor(out=aT[:, c, pl * 128:(pl + 1) * 128], in0=obank[:, o_i * 128:(o_i + 1) * 128],
                                                                          in1=rz[o_i][:], op=ALU.mult),
                     reads=[b_oz[o_i], b_rz[o_i]], writes=[b_aT])

        run_pipeline(units, [u1, u2, u3])
        for t in range(4):
            j = nxt("x", 2)
            tok_e = (4 + 8 * bq) * 64 + t * 128
            tok_o = bq * 512 + t * 128
            S.dma("sp", xt[j][:], xe[tok_e:tok_e + 128, :], writes=[b_xt[j]])
            o = 0
            for hf in range(2):
                pi = nxt("pp", 2)
                for k in range(8):
                    S.op("pe", lambda e, pi=pi, k=k, t=t, hf=hf: e.matmul(pp[pi][:], lhsT=aT[:, k, t * 128:(t + 1) * 128],
                                                                         rhs=wo[:, k, hf * 512:(hf + 1) * 512], start=(k == 0), stop=False),
                         reads=[b_aT, b_wo], writes=[b_pp[pi]])
                S.op("pe", lambda e, pi=pi, hf=hf: e.matmul(pp[pi][:], lhsT=onesr[:], rhs=bob[:, hf * 512:(hf + 1) * 512], start=False, stop=True),
                     reads=[b_c2], writes=[b_pp[pi]])
                S.op("dve", lambda e, pi=pi, hf=hf, j=j, o=o: e.tensor_tensor(out=ot[o][:, hf * 512:(hf + 1) * 512], in0=pp[pi][:],
                                                                             in1=xt[j][:, hf * 512:(hf + 1) * 512], op=ALU.add),
                     reads=[b_pp[pi], b_xt[j]], writes=[b_ot[o]])
            S.dma("pool", x_out[tok_o:tok_o + 128, :], ot[o][:], reads=[b_ot[o]], is_output=True)

    for kb in range(9):
        project(kb)
        if kb >= 1:
            attend(kb - 1)


def d_io(nc, ident=None):
    DT = nc.dram_tensor
    return {"gnorm": DT("d_gnorm", [128, D], F32, kind="ExternalInput").ap(),
            "ident": ident if ident is not None else DT("ident", [128, 128], F32, kind="ExternalInput").ap(),
            "w_qkv": DT("w_qkv", [D, 3 * D], F32, kind="ExternalInput").ap(),
            "b_qk": DT("b_qk", [128, 16], F32, kind="ExternalInput").ap(),
            "b_v": DT("b_v", [1, D], F32, kind="ExternalInput").ap(),
            "w_o": DT("w_o", [D, D], F32, kind="ExternalInput").ap(),
            "b_o": DT("b_o", [1, D], F32, kind="ExternalInput").ap(),
            "bt": DT("bt", [128, 16, 7, 128], F32, kind="ExternalInput").ap(),
            "rowmask": DT("rowmask", [128, 32, 6, 2], F32, kind="ExternalInput").ap()}


def build_D():
    nc = bass.Bass("TRN2", target_bir_lowering=False)
    S = Sched(nc)
    io = d_io(nc)
    io["xe"] = nc.dram_tensor("xe", [NROWS_LOC * 64, D], F32, kind="ExternalInput").ap()
    io["x_out"] = nc.dram_tensor("x_out", [TOK, D], F32, kind="ExternalOutput").ap()
    phase_D(nc, S, io)
    S.finish()
    return nc


def run_D(inp, xb_full):
    nc = build_D()
    ident = np.eye(128, dtype=np.float32)
    bq = inp["na_b_qkv"][0]
    bqk = np.concatenate([bq[0:D].reshape(8, 128).T, bq[D:2 * D].reshape(8, 128).T], 1).astype(np.float32)
    base = {"d_gnorm": rep_rows(inp["norm_mix"][1]), "ident": ident, "w_qkv": np.ascontiguousarray(inp["na_w_qkv"][0]),
            "b_qk": np.ascontiguousarray(bqk), "b_v": np.ascontiguousarray(bq[2 * D:][None, :]),
            "w_o": np.ascontiguousarray(inp["na_w_o"][0]), "b_o": np.ascontiguousarray(inp["na_b_o"][0][None, :]),
            "bt": na_tables(np.asarray(inp["na_rpb"][0], np.float32))}
    in_maps = []
    for c in range(NCORES):
        b, q = divmod(c, 4)
        xe = np.zeros((NROWS_LOC * 64, D), np.float32)
        g0 = (64 * q - 4) * 64
        lo, hi = max(g0, 0), min(g0 + NROWS_LOC * 64, SEQ)
        xe[lo - g0:hi - g0] = xb_full[b, lo:hi]
        in_maps.append(dict(base, xe=xe, rowmask=na_rowmask(q)))
    res = _run(nc, in_maps, "D")
    return [r["x_out"] for r in res.results]


def _tok_shards(a):
    return [np.ascontiguousarray(a[c // 4, (c % 4) * TOK:(c % 4 + 1) * TOK]) for c in range(NCORES)]


def _assemble(shards):
    out = np.empty((BATCH, SEQ, D), np.float32)
    for c in range(NCORES):
        out[c // 4, (c % 4) * TOK:(c % 4 + 1) * TOK] = shards[c]
    return out


NEXT = NROWS_LOC * 64


def build_L2():
    nc0 = bass.Bass("TRN2", target_bir_lowering=False)
    S = Sched(nc0)
    nc = NCP(nc0)
    DT = nc0.dram_tensor
    ext = lambda name, shape: DT(name, shape, F32, kind="ExternalInput").ap()
    ident = ext("ident", [128, 128])
    xa_s = DT("xa_scr", [NEXT, D], F32).ap()
    xb_s = DT("xb_scr", [NEXT, D], F32).ap()
    xc_s = DT("xc_scr", [TOK, D], F32).ap()
    out = DT("out", [TOK, D], F32, kind="ExternalOutput").ap()
    ioC = {"x_in": ext("x_ext", [NEXT, D]), "yT": ext("yT", [D, NEXT]), "sT": ext("sT", [D, NEXT]), "x0T": ext("x0T", [D, NEXT]),
           "skip": ext("skip", [128, 8]), "w_out": ext("w_out", [D, D]), "b_out": ext("b_out", [1, D]), "x_out": xa_s}
    ioF0 = {"x_in": xa_s, "gnorm": ext("gn_f0", [128, D]), "ident": ident, "wg": ext("wg0", [D, FF]), "wu": ext("wu0", [D, FF]),
            "wd": ext("wd0", [FF, D]), "x_out": xb_s}
    ioD = d_io(nc0, ident)
    ioD["xe"] = xb_s
    ioD["x_out"] = xc_s
    ioF1 = {"x_in": xc_s, "gnorm": ext("gn_f1", [128, D]), "ident": ident, "wg": ext("wg1", [D, FF]), "wu": ext("wu1", [D, FF]),
            "wd": ext("wd1", [FF, D]), "gfin": ext("gfin", [128, D]), "x_out": out}
    S.pfx = "c_"
    phase_C(nc, S, ioC, NEXT)
    S.barrier()
    nc.reset()
    S.pfx = "f0_"
    phase_F(nc, S, ioF0, NEXT, False)
    S.barrier()
    nc.reset()
    S.pfx = "d_"
    phase_D(nc, S, ioD)
    S.barrier()
    nc.reset()
    S.pfx = "f1_"
    phase_F(nc, S, ioF1, TOK, True)
    S.finish()
    return nc0


def _ext_tok(a, c):
    b, q = divmod(c, 4)
    g0 = q * TOK - 256
    out = np.zeros((NEXT,) + a.shape[2:], np.float32)
    lo, hi = max(g0, 0), min(g0 + NEXT, SEQ)
    out[lo - g0:hi - g0] = a[b, lo:hi]
    return out


def run_L2(inp, x, y_full, s_full, x0_full):
    nc = build_L2()
    bq = inp["na_b_qkv"][0]
    bqk = np.concatenate([bq[0:D].reshape(8, 128).T, bq[D:2 * D].reshape(8, 128).T], 1).astype(np.float32)
    base = {"ident": np.eye(128, dtype=np.float32),
            "skip": np.ascontiguousarray(inp["hy_skip"][0].reshape(8, 128).T), "w_out": np.ascontiguousarray(inp["hy_w_out"][0]),
            "b_out": np.ascontiguousarray(inp["hy_b_out"][0][None, :]),
            "gn_f0": rep_rows(inp["norm_ffn"][0]), "wg0": np.ascontiguousarray(inp["ffn_w_gate"][0]),
            "wu0": np.ascontiguousarray(inp["ffn_w_up"][0]), "wd0": np.ascontiguousarray(inp["ffn_w_down"][0]),
            "gn_f1": rep_rows(inp["norm_ffn"][1]), "wg1": np.ascontiguousarray(inp["ffn_w_gate"][1]),
            "wu1": np.ascontiguousarray(inp["ffn_w_up"][1]), "wd1": np.ascontiguousarray(inp["ffn_w_down"][1]),
            "gfin": rep_rows(inp["norm_final"]),
            "d_gnorm": rep_rows(inp["norm_mix"][1]), "w_qkv": np.ascontiguousarray(inp["na_w_qkv"][0]),
            "b_qk": np.ascontiguousarray(bqk), "b_v": np.ascontiguousarray(bq[2 * D:][None, :]),
            "w_o": np.ascontiguousarray(inp["na_w_o"][0]), "b_o": np.ascontiguousarray(inp["na_b_o"][0][None, :]),
            "bt": na_tables(np.asarray(inp["na_rpb"][0], np.float32))}
    in_maps = []
    for c in range(NCORES):
        m = dict(base)
        m["x_ext"] = _ext_tok(x, c)
        m["yT"] = np.ascontiguousarray(_ext_tok(y_full, c).T)
        m["sT"] = np.ascontiguousarray(_ext_tok(s_full, c).T)
        m["x0T"] = np.ascontiguousarray(_ext_tok(x0_full, c).T)
        m["rowmask"] = na_rowmask(c % 4)
        in_maps.append(m)
    res = _run(nc, in_maps, "L2")
    return [r["out"] for r in res.results]


def kernel(**inp):
    inp = {k: np.asarray(v, dtype=np.float32) for k, v in inp.items()}
    x = inp["x"]
    resA = run_A(inp)
    sT = [r["sT"] for r in resA]
    x0T = [r["x0T"] for r in resA]
    s_cs = [np.empty((2, 128, SEQ), np.float32) for _ in range(NCORES)]
    for c in range(NCORES):
        b, q = divmod(c, 4)
        for cg in range(NCORES):
            s_cs[cg][b, :, q * TOK:(q + 1) * TOK] = sT[c][cg * 128:(cg + 1) * 128]
    resB = run_B(inp, s_cs)
    y_full = np.empty((BATCH, SEQ, D), np.float32)
    for cg in range(NCORES):
        y_full[:, :, cg * 128:(cg + 1) * 128] = resB[cg]["y_out"].transpose(0, 2, 1)
    s_full = _assemble([t.T for t in sT])
    x0_full = _assemble([t.T for t in x0T])
    out = run_L2(inp, x, y_full, s_full, x0_full)
    return _assemble(out)
```
